# Optimizing a Trainium2 kernel written in Bass

```python
import jax, jax.numpy as jnp
from jax import lax
import numpy as np

D_MODEL = 1024
BATCH = 16
SEQ = 2048
DEPTH = 1

GRID_W = 64
CTX_LEN = 256
CHUNK = 64
EPS = 1e-6
N_MOD = 6

SSD_HEAD_DIM = 64
SSD_D_INNER = D_MODEL
SSD_HEADS = SSD_D_INNER // SSD_HEAD_DIM
SSD_GROUPS = 4
SSD_STATE = 128
SSD_CONV_W = 3
SSD_XBC = SSD_D_INNER + 2 * SSD_GROUPS * SSD_STATE

GLA_HEADS = 4
GLA_KEY_DIM = D_MODEL // 2
GLA_VAL_DIM = D_MODEL
GLA_DK = GLA_KEY_DIM // GLA_HEADS
GLA_DV = GLA_VAL_DIM // GLA_HEADS
GLA_GATE_RANK = 16
GLA_GATE_NORM = 16.0

D_FF = 256 * ((8 * D_MODEL // 3 + 255) // 256)
FFN_CONV_W = 3

IN_SIZES = (SSD_D_INNER, SSD_XBC, 2 * SSD_HEADS, GLA_KEY_DIM, GLA_KEY_DIM, GLA_VAL_DIM, GLA_GATE_RANK, GLA_VAL_DIM)
IN_DIM = sum(IN_SIZES)

kernel_name = "hybrid_ssd_gla_convffn_dit"


def rmsnorm(x, g):
    xf = x.astype(jnp.float32)
    y = xf * lax.rsqrt(jnp.mean(xf * xf, axis=-1, keepdims=True) + EPS)
    return y.astype(x.dtype) * g


def modulate(x, shift, scale):
    return x * (1.0 + scale) + shift


def adaln(cond, w, b):
    mod = jax.nn.silu(cond) @ w + b
    return mod.reshape(cond.shape[0], N_MOD, 1, D_MODEL)


def _rev(t):
    return jnp.flip(t, axis=1)


def dwconv1d(x, w, b):
    width = w.shape[0]
    pad = width // 2
    L = x.shape[1]
    xp = jnp.pad(x, ((0, 0), (pad, pad), (0, 0)))
    out = xp[:, 0:L] * w[0]
    for i in range(1, width):
        out = out + xp[:, i:i + L] * w[i]
    return out + b


def chunk_recurrence(decay, states, init):
    def step(h, inp):
        d, s = inp
        return d * h + s, h
    h_final, h_prev = lax.scan(step, init, (jnp.moveaxis(decay, 1, 0), jnp.moveaxis(states, 1, 0)))
    return jnp.moveaxis(h_prev, 0, 1), h_final


def ssd_scan(x, dt, a, bmat, cmat, init, with_output):
    bsz, L, H, P = x.shape
    G, N = bmat.shape[-2:]
    R = H // G
    nc = L // CHUNK
    f32 = jnp.float32
    dt = dt.astype(f32)
    xd = (x.astype(f32) * dt[..., None]).reshape(bsz, nc, CHUNK, G, R, P)
    da = (dt * a.astype(f32)).reshape(bsz, nc, CHUNK, G, R)
    bm = bmat.astype(f32).reshape(bsz, nc, CHUNK, G, N)
    cm = cmat.astype(f32).reshape(bsz, nc, CHUNK, G, N)
    cum = jnp.cumsum(da, axis=2)
    last = cum[:, :, -1]
    states = jnp.einsum("bcsgn,bcsgr,bcsgrp->bcgrpn", bm, jnp.exp(last[:, :, None] - cum), xd)
    if init is None:
        init = jnp.zeros((bsz, G, R, P, N), f32)
    h_prev, h_final = chunk_recurrence(jnp.exp(last)[..., None, None], states, init)
    if not with_output:
        return None, h_final
    tri = jnp.tril(jnp.ones((CHUNK, CHUNK), bool))[:, :, None, None]
    seg = cum[:, :, :, None] - cum[:, :, None, :]
    decay = jnp.exp(jnp.where(tri, seg, -jnp.inf))
    cb = jnp.einsum("bcqgn,bcsgn->bcqsg", cm, bm)
    y = (jnp.einsum("bcqsg,bcqsgr,bcsgrp->bcqgrp", cb, decay, xd)
         + jnp.einsum("bcqgn,bcqgr,bcgrpn->bcqgrp", cm, jnp.exp(cum), h_prev))
    return y.reshape(bsz, L, H, P).astype(x.dtype), h_final


def gla_scan(q, k, v, log_a, init, with_output):
    bsz, L, H, K = q.shape
    V = v.shape[-1]
    nc = L // CHUNK
    f32 = jnp.float32
    qc = q.astype(f32).reshape(bsz, nc, CHUNK, H, K)
    kc = k.astype(f32).reshape(bsz, nc, CHUNK, H, K)
    vc = v.astype(f32).reshape(bsz, nc, CHUNK, H, V)
    cum = jnp.cumsum(log_a.astype(f32).reshape(bsz, nc, CHUNK, H, K), axis=2)
    last = cum[:, :, -1]
    states = jnp.einsum("bcshk,bcshv->bchkv", kc * jnp.exp(last[:, :, None] - cum), vc)
    if init is None:
        init = jnp.zeros((bsz, H, K, V), f32)
    h_prev, h_final = chunk_recurrence(jnp.exp(last)[..., None], states, init)
    if not with_output:
        return None, h_final
    ref = cum[:, :, CHUNK // 2 - 1:CHUNK // 2]
    scores = jnp.einsum("bcqhk,bcshk->bchqs", qc * jnp.exp(cum - ref), kc * jnp.exp(ref - cum))
    tri = jnp.tril(jnp.ones((CHUNK, CHUNK), bool))
    scores = jnp.where(tri, scores, 0.0)
    o = (jnp.einsum("bchqs,bcshv->bcqhv", scores, vc)
         + jnp.einsum("bcqhk,bchkv->bcqhv", qc * jnp.exp(cum), h_prev))
    return o.reshape(bsz, L, H, V).astype(v.dtype), h_final


def token_mixer(h, init, with_output, w_in, conv_w, conv_b, dt_bias, a_log, d_skip, ssd_norm_g,
                gla_gate_w, gla_gate_b, gla_norm_g, w_br_ssd, w_br_gla, w_merge, b_merge, w_o):
    bsz, L, _ = h.shape
    proj = h @ w_in
    z, xbc, dt_raw, q, k, v, g_lr, r = jnp.split(proj, np.cumsum(IN_SIZES)[:-1].tolist(), axis=-1)
    s0 = (None, None, None, None) if init is None else init

    xbc = jax.nn.silu(dwconv1d(xbc, conv_w, conv_b))
    xs, bm, cm = jnp.split(xbc, [SSD_D_INNER, SSD_D_INNER + SSD_GROUPS * SSD_STATE], axis=-1)
    xs = xs.reshape(bsz, L, SSD_HEADS, SSD_HEAD_DIM)
    bm = bm.reshape(bsz, L, SSD_GROUPS, SSD_STATE)
    cm = cm.reshape(bsz, L, SSD_GROUPS, SSD_STATE)
    dt = jax.nn.softplus(dt_raw.reshape(bsz, L, 2, SSD_HEADS) + dt_bias)
    a = -jnp.exp(a_log.astype(jnp.float32))
    y_f, sf = ssd_scan(xs, dt[:, :, 0], a[0], bm, cm, s0[0], with_output)
    y_b, sb = ssd_scan(_rev(xs), _rev(dt[:, :, 1]), a[1], _rev(bm), _rev(cm), s0[1], with_output)

    q = q.reshape(bsz, L, GLA_HEADS, GLA_DK) * (GLA_DK ** -0.5)
    k = k.reshape(bsz, L, GLA_HEADS, GLA_DK)
    v = v.reshape(bsz, L, GLA_HEADS, GLA_DV)
    log_a = jax.nn.log_sigmoid(jnp.einsum("blr,drk->bldk", g_lr, gla_gate_w) + gla_gate_b) / GLA_GATE_NORM
    log_a = log_a.reshape(bsz, L, 2, GLA_HEADS, GLA_DK)
    o_f, gf = gla_scan(q, k, v, log_a[:, :, 0], s0[2], with_output)
    o_b, gb = gla_scan(_rev(q), _rev(k), _rev(v), _rev(log_a[:, :, 1]), s0[3], with_output)
    states = (sf, sb, gf, gb)
    if not with_output:
        return None, states

    y = y_f + _rev(y_b) + d_skip[:, None] * xs
    y = rmsnorm(y.reshape(bsz, L, SSD_D_INNER) * jax.nn.silu(z), ssd_norm_g)
    o = rmsnorm(o_f + _rev(o_b), gla_norm_g).reshape(bsz, L, GLA_VAL_DIM) * jax.nn.silu(r)

    gates = jax.nn.sigmoid(h @ w_merge + b_merge)
    g_ssd, g_gla = jnp.split(gates, 2, axis=-1)
    out = (g_ssd * (y @ w_br_ssd) + g_gla * (o @ w_br_gla)) @ w_o
    return out, states


def conv_ffn(h, w_up, conv_w, conv_b, w_down, rows):
    u = h @ w_up
    if rows is None:
        u = dwconv1d(u, conv_w[FFN_CONV_W // 2], conv_b)
    else:
        bsz, L, ch = u.shape
        grid = u.reshape(bsz, rows, GRID_W, ch)
        grid = lax.conv_general_dilated(grid, conv_w[:, :, None, :], (1, 1), "SAME",
                                        dimension_numbers=("NHWC", "HWIO", "NHWC"),
                                        feature_group_count=ch)
        u = grid.reshape(bsz, L, ch) + conv_b
    gate, val = jnp.split(u, 2, axis=-1)
    return (jax.nn.silu(gate) * val) @ w_down


def setup_inputs(seed: int = 0) -> dict:
    key = jax.random.key(seed)
    ks = jax.random.split(key, 32)
    f32 = jnp.float32
    nrm = lambda k, shape, s: jax.random.normal(k, shape, f32) * s
    gain = lambda k, shape: 1.0 + 0.1 * jax.random.normal(k, shape, f32)
    dt0 = jnp.exp(jax.random.uniform(ks[8], (DEPTH, 2, SSD_HEADS), f32) * (np.log(0.1) - np.log(0.001)) + np.log(0.001))
    return {
        "x": nrm(ks[0], (BATCH, SEQ, D_MODEL), 1.0),
        "c": nrm(ks[1], (BATCH, D_MODEL), 1.0),
        "ctx": nrm(ks[2], (BATCH, CTX_LEN, D_MODEL), 1.0),
        "c_ctx": nrm(ks[3], (D_MODEL,), 1.0),
        "w_ada": nrm(ks[4], (DEPTH, D_MODEL, N_MOD * D_MODEL), 0.5 * D_MODEL ** -0.5),
        "b_ada": nrm(ks[5], (DEPTH, N_MOD * D_MODEL), 0.02),
        "norm1_g": gain(ks[6], (DEPTH, D_MODEL)),
        "w_in": nrm(ks[7], (DEPTH, D_MODEL, IN_DIM), D_MODEL ** -0.5),
        "ssd_conv_w": nrm(ks[9], (DEPTH, SSD_CONV_W, SSD_XBC), SSD_CONV_W ** -0.5),
        "ssd_conv_b": nrm(ks[10], (DEPTH, SSD_XBC), 0.02),
        "ssd_dt_bias": dt0 + jnp.log(-jnp.expm1(-dt0)),
        "ssd_a_log": jnp.log(jax.random.uniform(ks[11], (DEPTH, 2, SSD_HEADS), f32, 1.0, 16.0)),
        "ssd_d": gain(ks[12], (DEPTH, SSD_HEADS)),
        "ssd_norm_g": gain(ks[13], (DEPTH, SSD_D_INNER)),
        "gla_gate_w": nrm(ks[14], (DEPTH, 2, GLA_GATE_RANK, GLA_KEY_DIM), GLA_GATE_RANK ** -0.5),
        "gla_gate_b": nrm(ks[15], (DEPTH, 2, GLA_KEY_DIM), 0.1),
        "gla_norm_g": gain(ks[16], (DEPTH, GLA_DV)),
        "w_br_ssd": nrm(ks[17], (DEPTH, SSD_D_INNER, D_MODEL), SSD_D_INNER ** -0.5),
        "w_br_gla": nrm(ks[18], (DEPTH, GLA_VAL_DIM, D_MODEL), GLA_VAL_DIM ** -0.5),
        "w_merge": nrm(ks[19], (DEPTH, D_MODEL, 2 * D_MODEL), D_MODEL ** -0.5),
        "b_merge": nrm(ks[20], (DEPTH, 2 * D_MODEL), 0.02),
        "w_o": nrm(ks[21], (DEPTH, D_MODEL, D_MODEL), D_MODEL ** -0.5),
        "norm2_g": gain(ks[22], (DEPTH, D_MODEL)),
        "w_up": nrm(ks[23], (DEPTH, D_MODEL, 2 * D_FF), D_MODEL ** -0.5),
        "ffn_conv_w": nrm(ks[24], (DEPTH, FFN_CONV_W, FFN_CONV_W, 2 * D_FF), 1.0 / FFN_CONV_W),
        "ffn_conv_b": nrm(ks[25], (DEPTH, 2 * D_FF), 0.02),
        "w_down": nrm(ks[26], (DEPTH, D_FF, D_MODEL), D_FF ** -0.5),
        "final_norm_g": gain(ks[27], (D_MODEL,)),
    }


def reference(x, c, ctx, c_ctx, w_ada, b_ada, norm1_g, w_in, ssd_conv_w, ssd_conv_b, ssd_dt_bias,
              ssd_a_log, ssd_d, ssd_norm_g, gla_gate_w, gla_gate_b, gla_norm_g, w_br_ssd, w_br_gla,
              w_merge, b_merge, w_o, norm2_g, w_up, ffn_conv_w, ffn_conv_b, w_down, final_norm_g):
    rows = x.shape[1] // GRID_W
    for l in range(DEPTH):
        last = l == DEPTH - 1
        lp = dict(w_in=w_in[l], conv_w=ssd_conv_w[l], conv_b=ssd_conv_b[l], dt_bias=ssd_dt_bias[l],
                  a_log=ssd_a_log[l], d_skip=ssd_d[l], ssd_norm_g=ssd_norm_g[l],
                  gla_gate_w=gla_gate_w[l], gla_gate_b=gla_gate_b[l], gla_norm_g=gla_norm_g[l],
                  w_br_ssd=w_br_ssd[l], w_br_gla=w_br_gla[l], w_merge=w_merge[l], b_merge=b_merge[l],
                  w_o=w_o[l])
        mx = adaln(c, w_ada[l], b_ada[l])
        mc = adaln(c_ctx[None], w_ada[l], b_ada[l])

        h_ctx = modulate(rmsnorm(ctx, norm1_g[l]), mc[:, 0], mc[:, 1])
        out_ctx, ctx_states = token_mixer(h_ctx, None, not last, **lp)

        h_x = modulate(rmsnorm(x, norm1_g[l]), mx[:, 0], mx[:, 1])
        out_x, _ = token_mixer(h_x, ctx_states, True, **lp)
        x = x + mx[:, 2] * out_x
        h2 = modulate(rmsnorm(x, norm2_g[l]), mx[:, 3], mx[:, 4])
        x = x + mx[:, 5] * conv_ffn(h2, w_up[l], ffn_conv_w[l], ffn_conv_b[l], w_down[l], rows)

        if not last:
            ctx = ctx + mc[:, 2] * out_ctx
            h2c = modulate(rmsnorm(ctx, norm2_g[l]), mc[:, 3], mc[:, 4])
            ctx = ctx + mc[:, 5] * conv_ffn(h2c, w_up[l], ffn_conv_w[l], ffn_conv_b[l], w_down[l], None)
    return rmsnorm(x, final_norm_g)
```

```python
import numpy as np
import concourse.bass as bass
import concourse.mybir as mybir
from concourse.bass_utils import run_bass_kernel_spmd
from contextlib import ExitStack

F32 = mybir.dt.float32
BF16 = mybir.dt.bfloat16
AF = mybir.ActivationFunctionType
ALU = mybir.AluOpType
AX = mybir.AxisListType


class Res:
    def __init__(self, name, t=None):
        self.name = name
        self.t = t
        self.lw = {}
        self.rd = {}


class Prog:
    NDMA = 8

    def __init__(self, nc, es):
        self.nc = nc
        self.es = es
        self.names = ['pe', 'act', 'dve', 'pool', 'sp']
        self.E = {'pe': nc.tensor, 'act': nc.scalar, 'dve': nc.vector, 'pool': nc.gpsimd, 'sp': nc.sync}
        self.cnt = {k: 0 for k in self.names}
        self.h = {}
        for k in self.names:
            self.h[('e', k)] = es.enter_context(nc.semaphore("s_" + k))
        self.dq = ('sp', 'pool', 'act')
        self.dcnt = {q: 0 for q in self.dq}
        self.dval = {}
        for q in self.dq:
            for i in range(self.NDMA):
                self.h[('d', q, i)] = es.enter_context(nc.semaphore("d_%s%d" % (q, i)))
                self.dval[('d', q, i)] = 0
        self.seen = {k: {} for k in self.names}
        self.dram = {}
        self.final = []
        self.nwait = 0

    def sbuf(self, name, shape, dtype):
        self.uid = getattr(self, "uid", 0) + 1
        t = self.es.enter_context(self.nc.sbuf_tensor("sb%d_%s" % (self.uid, name), list(shape), dtype))
        return Res(name, t)

    def psum(self, name, shape, dtype):
        t = self.es.enter_context(self.nc.psum_tensor("ps_" + name, list(shape), dtype))
        return Res(name, t)

    def _dres(self, name):
        if name not in self.dram:
            self.dram[name] = Res(name)
        return self.dram[name]

    def _collect(self, eng, reads, writes):
        need = {}

        def add(tok):
            if tok is None:
                return
            k, v = tok
            if need.get(k, 0) < v:
                need[k] = v

        for r in reads:
            for k, v in r.lw.items():
                add((k, v))
        for w in writes:
            for k, v in w.lw.items():
                add((k, v))
            for k, v in w.rd.items():
                add((k, v))
        out = []
        for k, v in need.items():
            if eng == 'pe' and k == ('e', 'pe'):
                continue
            if self.seen[eng].get(k, 0) >= v:
                continue
            self.seen[eng][k] = v
            out.append((k, v))
        return out

    def _mark(self, tok, reads, writes):
        k, v = tok
        for r in reads:
            if r.rd.get(k, 0) < v:
                r.rd[k] = v
        for w in writes:
            if w.lw.get(k, 0) < v:
                w.lw[k] = v
            w.rd = {}

    def op(self, eng, fn, reads=(), writes=()):
        waits = self._collect(eng, reads, writes)
        self.cnt[eng] += 1
        sem = self.h[('e', eng)]
        hs = [(self.h[k], v) for k, v in waits]
        self.nwait += len(hs)

        e = self.E[eng]
        for hh, v in hs:
            e.wait_ge(hh, v)
        fn(e).then_inc(sem, 1)
        self._mark((('e', eng), self.cnt[eng]), reads, writes)

    def dma(self, q, out_ap, in_ap, reads=(), writes=(), dram_r=(), dram_w=(), final=False, accum=None):
        reads = list(reads) + [self._dres(n) for n in dram_r]
        writes = list(writes) + [self._dres(n) for n in dram_w]
        i = self.dcnt[q] % self.NDMA
        self.dcnt[q] += 1
        key = ('d', q, i)
        prev = self.dval[key]
        waits = self._collect(q, reads, writes)
        if prev > 0 and self.seen[q].get(key, 0) < prev:
            self.seen[q][key] = prev
            waits.append((key, prev))
        self.dval[key] = prev + 16
        sem = self.h[key]
        hs = [(self.h[k], v) for k, v in waits]
        self.nwait += len(hs)

        e = self.E[q]
        for hh, v in hs:
            e.wait_ge(hh, v)
        if accum is not None:
            e.dma_start(out=out_ap, in_=in_ap, accum_op=accum).then_inc(sem, 16)
        else:
            e.dma_start(out=out_ap, in_=in_ap).then_inc(sem, 16)
        tok = (key, prev + 16)
        self._mark(tok, reads, writes)
        if final:
            self.final.append(tok)

    def barrier(self):
        toks = [(('e', k), self.cnt[k]) for k in self.names if self.cnt[k] > 0]
        toks += [(k, v) for k, v in self.dval.items() if v > 0]
        for eng in self.names:
            e = self.E[eng]
            for k, v in toks:
                if eng == 'pe' and k == ('e', 'pe'):
                    continue
                if self.seen[eng].get(k, 0) >= v:
                    continue
                self.seen[eng][k] = v
                e.wait_ge(self.h[k], v)

    def finish(self):
        fin = {}
        for k, v in self.final:
            fin[k] = max(fin.get(k, 0), v)
        hs = [(self.h[k], v) for k, v in fin.items()]

        e = self.E['sp']
        for hh, v in hs:
            e.wait_ge(hh, v)


D = 1024
SEQ = 2048
CTXL = 256
NTOK = CTXL + SEQ
NBL = 2
DFF = 2816
EPS = 1e-6
WM0, WMN = 1024, 4144
O_XBC, O_DT, O_Q, O_K, O_V, O_G = 0, 2048, 2080, 2592, 3104, 4128
V_N1G, V_N2G, V_BADA, V_CW, V_CB, V_BM, V_FCW, V_FCB, V_SNG, V_GNG, NV = 0, 8, 16, 64, 112, 128, 144, 540, 584, 592, 600
R_DTB, R_ALOG, R_DSK, NR = 0, 32, 64, 80
RB_FNG, RB_BA2, RB_BA5, NRB = 0, 1024, 2048, 3072
K_ID, K_ONE, K_L, K_U, K_NM, K_V, NCN = 0, 128, 256, 384, 512, 640, 768


def host_consts():
    c = np.zeros((128, NCN), np.float32)
    c[:, K_ID:K_ID + 128] = np.eye(128, dtype=np.float32)
    c[:, K_ONE:K_ONE + 128] = 1.0
    t = np.arange(64)[:, None]
    i = np.arange(64)[None, :]
    c[0:64, K_L:K_L + 64] = (t <= i)
    c[0:64, K_L + 64:K_L + 128] = (t >= i)
    c[0:64, K_U:K_U + 64] = (t > i)
    c[0:64, K_U + 64:K_U + 128] = (t < i)
    c[0:64, K_NM:K_NM + 64] = np.where(t <= i, 0.0, -30000.0)
    c[0:64, K_NM + 64:K_NM + 128] = np.where(t >= i, 0.0, -30000.0)
    c[0:64, K_V:K_V + 64] = (t <= i)
    c[0:64, K_V + 64:K_V + 128] = (t >= i)
    return c


def fm(v):
    v = np.asarray(v, np.float32)
    return np.ascontiguousarray(v.reshape(-1, 128).T)


def prep_shared(inp):
    vecs = np.zeros((128, NV), np.float32)
    vecs[:, V_N1G:V_N1G + 8] = fm(inp["norm1_g"][0])
    vecs[:, V_N2G:V_N2G + 8] = fm(inp["norm2_g"][0])
    vecs[:, V_BADA:V_BADA + 48] = fm(inp["b_ada"][0])
    cw = inp["ssd_conv_w"][0]
    for k in range(3):
        vecs[:, V_CW + k:V_CW + 48:3] = fm(cw[k])
    vecs[:, V_CB:V_CB + 16] = fm(inp["ssd_conv_b"][0])
    vecs[:, V_BM:V_BM + 16] = fm(inp["b_merge"][0])
    fw = inp["ffn_conv_w"][0].reshape(9, -1)
    for k in range(9):
        vecs[:, V_FCW + k:V_FCW + 396:9] = fm(fw[k])
    vecs[:, V_FCB:V_FCB + 44] = fm(inp["ffn_conv_b"][0])
    vecs[:, V_SNG:V_SNG + 8] = fm(inp["ssd_norm_g"][0])
    vecs[:, V_GNG:V_GNG + 8] = np.tile(fm(inp["gla_norm_g"][0]), (1, 4))
    rows = np.zeros((1, NR), np.float32)
    rowsbig = np.zeros((1, NRB), np.float32)
    rowsbig[0, RB_FNG:RB_FNG + 1024] = inp["final_norm_g"]
    rowsbig[0, RB_BA2:RB_BA2 + 1024] = inp["b_ada"][0][2048:3072]
    rowsbig[0, RB_BA5:RB_BA5 + 1024] = inp["b_ada"][0][5120:6144]
    rows[0, R_DTB:R_DTB + 32] = inp["ssd_dt_bias"][0].reshape(-1)
    rows[0, R_ALOG:R_ALOG + 32] = inp["ssd_a_log"][0].reshape(-1)
    rows[0, R_DSK:R_DSK + 16] = inp["ssd_d"][0]
    gw = np.zeros((17, 1024), np.float32)
    gw[0:16] = np.transpose(inp["gla_gate_w"][0], (1, 0, 2)).reshape(16, 1024)
    gw[16] = inp["gla_gate_b"][0].reshape(-1)
    return dict(vecs=vecs, rows=rows, rowsbig=rowsbig, gw=gw, consts=host_consts())


class Rot:
    def __init__(self, p, name, shape, dtype, n=2):
        self.b = [p.sbuf("%s%d" % (name, i), shape, dtype) for i in range(n)]
        self.i = 0

    def next(self):
        r = self.b[self.i % len(self.b)]
        self.i += 1
        return r


def build_program(dbg=None, nseq=NBL, phases="AMPF"):
    dbg = dbg or {}
    nc = bass.Bass("TRN2", target_bir_lowering=False)
    IN = "ExternalInput"

    def din(name, shape):
        return nc.dram_tensor(name, list(shape), F32, kind=IN).ap()

    x_d = din("x", [NBL, SEQ, D])
    ctx_d = din("ctx", [NBL, CTXL, D])
    cT_d = din("cT", [128, 8, 3])
    w_ada_d = din("w_ada", [D, 6 * D])
    w_in_d = din("w_in", [D, 6192])
    w_merge_d = din("w_merge", [D, 2 * D])
    w_brs_d = din("w_br_ssd", [D, D])
    w_brg_d = din("w_br_gla", [D, D])
    w_o_d = din("w_o", [D, D])
    w_up_d = din("w_up", [D, 2 * DFF])
    w_down_d = din("w_down", [DFF, D])
    gw_d = din("gw", [17, 1024])
    vecs_d = din("vecs", [128, NV])
    rows_d = din("rows", [1, NR])
    rowsbig_d = din("rowsbig", [1, NRB])
    consts_d = din("consts", [128, NCN])
    out_d = nc.dram_tensor("out", [NBL, SEQ, D], F32, kind="ExternalOutput").ap()
    yo_kind = "ExternalOutput" if dbg.get("dump_yo") else "Internal"
    yo_d = nc.dram_tensor("yo", [NBL, SEQ, 2 * D], F32, kind=yo_kind).ap()
    sx_d = nc.dram_tensor("sx", [NBL, 9, 128, 16 * 256], BF16, kind="Internal").ap()
    sqk_d = nc.dram_tensor("sqk", [NBL, 9, 128, 8 * 256], BF16, kind="Internal").ap()
    sg_d = nc.dram_tensor("sg", [NBL, 9, 32, 256], BF16, kind="Internal").ap()
    hts_d = nc.dram_tensor("hts", [NBL, 128, 8, SEQ], BF16, kind="Internal").ap()
    x1_kind = "ExternalOutput" if dbg.get("dump_x1") else "Internal"
    x1_d = nc.dram_tensor("x1", [NBL, SEQ, D], F32, kind=x1_kind).ap()
    dbg_outs = {}

    es = ExitStack()
    with es:
        p = Prog(nc, es)

        def MM(out, lhsT, rhs, start, stop, reads, writes):
            p.op('pe', lambda e: e.matmul(out, lhsT, rhs, start=start, stop=stop), reads, writes)

        def TR(out, in_, ident, reads, writes):
            p.op('pe', lambda e: e.transpose(out, in_, ident), reads, writes)

        def ACT(out, in_, func, reads, writes, bias=None, scale=None):
            kw = {}
            if bias is not None:
                kw['bias'] = bias
            if scale is not None:
                kw['scale'] = scale
            p.op('act', lambda e: e.activation(out, in_, func, **kw), reads, writes)

        def TT(out, in0, in1, op, reads, writes, eng='dve'):
            p.op(eng, lambda e: e.tensor_tensor(out, in0, in1, op), reads, writes)

        def TS(out, in0, s1, s2, op0, op1, reads, writes, eng='dve'):
            if s2 is None:
                p.op(eng, lambda e: e.tensor_scalar(out, in0, s1, None, op0), reads, writes)
            else:
                p.op(eng, lambda e: e.tensor_scalar(out, in0, s1, s2, op0, op1), reads, writes)

        def STT(out, in0, sc, in1, op0, op1, reads, writes):
            p.op('dve', lambda e: e.scalar_tensor_tensor(out, in0, sc, in1, op0, op1), reads, writes)

        def CP(out, in_, reads, writes, eng='dve'):
            if eng == 'act':
                p.op('act', lambda e: e.activation(out, in_, AF.Identity), reads, writes)
            else:
                p.op(eng, lambda e: e.tensor_copy(out, in_), reads, writes)

        def MEMSET(ap, val, writes, eng='dve'):
            p.op(eng, lambda e: e.memset(ap, val), (), writes)

        def dump(name, res, ap, shape):
            t = nc.dram_tensor("dbg_" + name, list(shape), ap.dtype, kind="ExternalOutput").ap()
            p.dma('sp', t, ap, reads=[res], final=True)
            dbg_outs[name] = t

        pp = [p.psum("pp%d" % i, [128, 1024], F32) for i in range(4)]
        pidx = [0]

        def PS():
            r = pp[pidx[0] % 4]
            pidx[0] += 1
            return r

        consts = p.sbuf("consts", [128, NCN], F32)
        vecs = p.sbuf("vecs", [128, NV], F32)
        rowsb = p.sbuf("rowsb", [128, NR], F32)
        gw = p.sbuf("gw", [17, 1024], BF16)
        constb = p.sbuf("constb", [128, 384], BF16)
        p.dma('sp', consts.t[:], consts_d, writes=[consts])
        p.dma('sp', vecs.t[:], vecs_d, writes=[vecs])
        p.dma('sp', rowsb.t[:], rows_d.partition_broadcast(128), writes=[rowsb])
        p.dma('pool', gw.t[:], gw_d, writes=[gw])
        identf = consts.t[:, K_ID:K_ID + 128]
        onesf = consts.t[:, K_ONE:K_ONE + 128]

        def Lm(d):
            return consts.t[0:64, K_L + 64 * d:K_L + 64 * d + 64]

        def Um(d):
            return consts.t[0:64, K_U + 64 * d:K_U + 64 * d + 64]

        def NMm(d):
            return consts.t[0:64, K_NM + 64 * d:K_NM + 64 * d + 64]

        def Vm(d):
            return consts.t[0:64, K_V + 64 * d:K_V + 64 * d + 64]

        identb = p.sbuf("identb", [128, 128], BF16)
        CP(identb.t[:], identf, [consts], [identb])
        CP(constb.t[:], consts.t[:, K_ONE:K_ONE + 384], [consts], [constb])

        def Lb(d):
            return constb.t[0:64, 128 + 64 * d:128 + 64 * d + 64]

        def Ub(d):
            return constb.t[0:64, 256 + 64 * d:256 + 64 * d + 64]
        aneg = p.sbuf("aneg", [64, 32], F32)
        ACT(aneg.t[:], rowsb.t[0:64, R_ALOG:R_ALOG + 32], AF.Exp, [rowsb], [aneg])
        TS(aneg.t[:], aneg.t[:], -1.0, None, ALU.mult, None, [aneg], [aneg])
        dkd = p.sbuf("dkd", [64, 1024], BF16)
        TT(dkd.t[:].rearrange("p (h q) -> p h q", h=16),
           consts.t[0:64, K_ID:K_ID + 64].unsqueeze(1).to_broadcast([64, 16, 64]),
           rowsb.t[0:64, R_DSK:R_DSK + 16].unsqueeze(2).to_broadcast([64, 16, 64]),
           ALU.mult, [consts, rowsb], [dkd])

        modT = p.sbuf("modT", [128, 48, 3], F32)
        scT = p.sbuf("scT", [128, 8, 3], BF16)
        A1 = p.sbuf("A1", [128, 8, 3], F32)
        A2 = p.sbuf("A2", [128, 8, 3], F32)

        with ExitStack() as esA:
            p.es = esA
            cT = p.sbuf("cT", [128, 8, 3], F32)
            p.dma('sp', cT.t[:], cT_d, writes=[cT])
            ACT(scT.t[:], cT.t[:], AF.Silu, [cT], [scT])
            wrot = Rot(p, "wada", [128, 8, 512], BF16, 2)
            mps = PS()
            for nb in range(12):
                wb = wrot.next()
                for dk in range(8):
                    p.dma('pool', wb.t[:, dk, :], w_ada_d[dk * 128:(dk + 1) * 128, nb * 512:(nb + 1) * 512], writes=[wb])
                for cc in range(4):
                    j = nb * 4 + cc
                    for dk in range(8):
                        MM(mps.t[:, j * 4:j * 4 + 3], wb.t[:, dk, cc * 128:(cc + 1) * 128], scT.t[:, dk, :],
                           dk == 0, dk == 7, [wb, scT], [mps])
            TT(modT.t[:], mps.t[:, 0:192].rearrange("p (j r) -> p j r", r=4)[:, :, 0:3],
               vecs.t[:, V_BADA:V_BADA + 48].unsqueeze(2).to_broadcast([128, 48, 3]), ALU.add, [mps, vecs], [modT])
            for (A, vg, so) in ((A1, V_N1G, 8), (A2, V_N2G, 32)):
                TS(A.t[:], modT.t[:, so:so + 8, :], 1.0, None, ALU.add, None, [modT], [A])
                TT(A.t[:], A.t[:], vecs.t[:, vg:vg + 8].unsqueeze(2).to_broadcast([128, 8, 3]), ALU.mult, [A, vecs], [A])
            p.es = es
        p.barrier()
        if dbg.get("dump_mod"):
            dump("modT", modT, modT.t[:], [128, 48, 3])
            dump("A1", A1, A1.t[:], [128, 8, 3])

        def norm_to_T(xt, xtr, A, Bap_fn, r, dst, dst_tok0, tmp):
            sq, sqr, ssq, xn, xnr = tmp
            ACT(sq, xt, AF.Square, [xtr], [sqr])
            p.op('dve', lambda e: e.reduce_sum(ssq.t[:, 0:1], sq, AX.X), [sqr], [ssq])
            ACT(ssq.t[:, 1:2], ssq.t[:, 0:1], AF.Sqrt, [ssq], [ssq], bias=EPS_AP[0], scale=1.0 / D)
            p.op('dve', lambda e: e.reciprocal(ssq.t[:, 2:3], ssq.t[:, 1:2]), [ssq], [ssq])
            TS(xn, xt, ssq.t[:, 2:3], None, ALU.mult, None, [xtr, ssq], [xnr])
            ps = PS()
            for j in range(8):
                TR(ps.t[:, j * 128:(j + 1) * 128], xn[:, j * 128:(j + 1) * 128], identf, [xnr, consts], [ps])
            for j in range(8):
                if j % 2 == 0:
                    TS(dst.t[:, j, dst_tok0:dst_tok0 + 128], ps.t[:, j * 128:(j + 1) * 128],
                       A.t[:, j, r:r + 1], Bap_fn(j, r), ALU.mult, ALU.add, [ps, A, modT], [dst])
                else:
                    ACT(dst.t[:, j, dst_tok0:dst_tok0 + 128], ps.t[:, j * 128:(j + 1) * 128], AF.Identity,
                        [ps, A, modT], [dst], bias=Bap_fn(j, r), scale=A.t[:, j, r:r + 1])

        epsb = p.sbuf("epsb", [128, 1], F32)
        MEMSET(epsb.t[:], EPS, [epsb])
        EPS_AP = [epsb.t[:, 0:1]]

        if "M" in phases:
          with ExitStack() as esM:
            p.es = esM
            wmix = p.sbuf("wmix", [128, 8, WMN], BF16)
            nmrep = p.sbuf("nmrep", [64, 2, 1024], BF16)
            for d_ in range(2):
                CP(nmrep.t[:, d_, :].rearrange("p (h q) -> p h q", h=16),
                   consts.t[0:64, K_NM + 64 * d_:K_NM + 64 * d_ + 64].unsqueeze(1).to_broadcast([64, 16, 64]), [consts], [nmrep])

            for dk in range(8):
                p.dma('pool', wmix.t[:, dk, :], w_in_d[dk * 128:(dk + 1) * 128, WM0:WM0 + WMN], writes=[wmix])
            hT = p.sbuf("hT", [128, 8, NTOK], BF16)
            scr8 = p.sbuf("scr8", [128, 2048], F32)
            sq = p.sbuf("sq", [128, 1024], BF16)
            ssq = p.sbuf("ssq", [128, 4], F32)
            Hs = p.sbuf("Hs", [128, 1024], F32)
            Hb = p.sbuf("Hb", [128, 1024], BF16)
            Ss = p.sbuf("Ss", [128, 1024], F32)
            Sb = p.sbuf("Sb", [128, 1024], BF16)
            xbcT = p.sbuf("xbcT", [128, 16, 256], BF16)
            qkT = p.sbuf("qkT", [128, 8, 256], BF16)
            glrT = p.sbuf("glrT", [32, 256], BF16)
            MEMSET(glrT.t[:], 1.0, [glrT])
            accr = Rot(p, "acc", [128, 256], F32, 2)
            dts_r = Rot(p, "dts", [64, 32], F32, 2)
            da_r = Rot(p, "da", [64, 16], F32, 2)
            cum_r = Rot(p, "cum_sb", [64, 16], F32, 2)
            dtw_r = Rot(p, "dtw", [64, 16], F32, 2)
            ecum_r = Rot(p, "ecum", [64, 16], F32, 2)
            eL_r = Rot(p, "eL", [128, 16], F32, 2)
            dahl_r = Rot(p, "dahl", [64, 32], BF16, 2)
            nlm = p.sbuf("nlm", [64, 2, 64], BF16)
            for d_ in range(2):
                mid_ = 31 if d_ == 0 else 32
                TS(nlm.t[:, d_, :], consts.t[0:64, K_L + 64 * d_ + mid_:K_L + 64 * d_ + mid_ + 1].to_broadcast([64, 64]),
                   -1.0, None, ALU.mult, None, [consts], [nlm])
            ndahl_r = Rot(p, "ndahl", [64, 32], BF16, 2)
            seg = p.sbuf("seg", [64, 1024], F32)
            MTt_r = Rot(p, "MTt", [64, 1024], BF16, 2)
            cmc_r = Rot(p, "cmc", [128, 256], BF16, 2)
            xsb_r = Rot(p, "xsb", [64, 1536], BF16, 2)
            xd_r = Rot(p, "xd", [64, 1024], BF16, 2)
            xdw_r = Rot(p, "xdw", [64, 1024], BF16, 2)
            ybufs = [Res("yb0", scr8.t[0:64, 0:1024]), Res("yb1", scr8.t[0:64, 1024:2048])]
            ycnt = [0]
            obuf_r = Rot(p, "obuf", [64, 1024], F32, 2)
            la_r = Rot(p, "la", [64, 512], F32, 1)
            lah_r = Rot(p, "lah", [64, 512], BF16, 1)
            lal_r = Rot(p, "lal", [64, 512], BF16, 1)
            ref = p.sbuf("ref", [128, 4], F32)
            dl = p.sbuf("dl", [128, 256], F32)
            eq = p.sbuf("eq", [128, 256], F32)
            ek = p.sbuf("ek", [128, 256], F32)
            ec = p.sbuf("ec", [128, 256], F32)
            eT_r = Rot(p, "eT", [128, 4], F32, 2)
            erc = p.sbuf("erc", [64, 512], F32)
            vsb_r = Rot(p, "vsb", [64, 1024], BF16, 2)
            kdec_r = Rot(p, "kdec", [64, 512], BF16, 2)
            qdT_r = Rot(p, "qdT", [128, 256], BF16, 2)
            kdT_r = Rot(p, "kdT", [128, 256], BF16, 2)
            qeT_r = Rot(p, "qeT", [128, 256], BF16, 2)
            scTt = p.sbuf("scTt", [64, 256], BF16)

            def B1fn(j, r):
                return modT.t[:, j, r:r + 1]

            small_pssB = Res("pssB", pp[1].t[:, 0:512])
            small_pscB = Res("pscB", pp[1].t[:, 512:1024])
            chunk_par = [0]
            pj = [0]

            def PSJ():
                r = pp[(0, 2, 3)[pj[0] % 3]]
                pj[0] += 1
                return r

            def proj_super(tok0, lo, hi):
                T = 256
                a = max(tok0 - 1, lo)
                e_ = min(tok0 + T + 1, hi)
                n = e_ - a
                off = a - (tok0 - 1)
                for cc in range(16):
                    ps = PSJ()
                    for dk in range(8):
                        MM(ps.t[:, off:off + n], wmix.t[:, dk, O_XBC + cc * 128:O_XBC + (cc + 1) * 128],
                           hT.t[:, dk, a:e_], dk == 0, dk == 7, [wmix, hT], [ps])
                    acc = accr.next()
                    cwb = V_CW + cc * 3
                    ACT(acc.t[:], ps.t[:, 1:257], AF.Identity, [ps, vecs], [acc],
                        bias=vecs.t[:, V_CB + cc:V_CB + cc + 1], scale=vecs.t[:, cwb + 1:cwb + 2])
                    i0 = 1 if off == 1 else 0
                    STT(acc.t[:, i0:256], ps.t[:, i0:256], vecs.t[:, cwb:cwb + 1], acc.t[:, i0:256],
                        ALU.mult, ALU.add, [ps, vecs, acc], [acc])
                    i1 = 255 if e_ < tok0 + T + 1 else 256
                    STT(acc.t[:, 0:i1], ps.t[:, 2:2 + i1], vecs.t[:, cwb + 2:cwb + 3], acc.t[:, 0:i1],
                        ALU.mult, ALU.add, [ps, vecs, acc], [acc])
                    ACT(xbcT.t[:, cc, :], acc.t[:], AF.Silu, [acc], [xbcT])
                for j in range(8):
                    ps = PSJ()
                    for dk in range(8):
                        MM(ps.t[:, 0:256], wmix.t[:, dk, O_Q + j * 128:O_Q + (j + 1) * 128],
                           hT.t[:, dk, tok0:tok0 + 256], dk == 0, dk == 7, [wmix, hT], [ps])
                    ACT(qkT.t[:, j, :], ps.t[:, 0:256], AF.Identity, [ps], [qkT],
                        scale=(128.0 ** -0.5) if j < 4 else 1.0)
                ps = PSJ()
                for dk in range(8):
                    MM(ps.t[0:16, 0:256], wmix.t[:, dk, O_G:O_G + 16], hT.t[:, dk, tok0:tok0 + 256],
                       dk == 0, dk == 7, [wmix, hT], [ps])
                CP(glrT.t[0:16, :], ps.t[0:16, 0:256], [ps], [glrT])

            def chunk_head(b, d, t0, cl, is_ctx):
                dsl = slice(d * 16, d * 16 + 16)
                dts, da, cum_sb, dtw, ecum, eL = dts_r.next(), da_r.next(), cum_r.next(), dtw_r.next(), ecum_r.next(), eL_r.next()
                xsb, xd, xdw = xsb_r.next(), xd_r.next(), xdw_r.next()
                la, vsb, kdec = la_r.next(), vsb_r.next(), kdec_r.next()
                lah, lal, dahl = lah_r.next(), lal_r.next(), dahl_r.next()
                ybuf = obuf = None
                if not is_ctx:
                    ybuf = ybufs[ycnt[0] % 2]
                    ycnt[0] += 1
                    obuf = obuf_r.next()
                par = chunk_par[0] % 2
                chunk_par[0] += 1
                pss = small_pssB
                psc = small_pscB
                sps = small_pssB
                h16 = lambda ap: ap.rearrange("p (h q) -> p h q", h=16)
                h4 = lambda ap: ap.rearrange("p (h q) -> p h q", h=4)
                lps = pp[0]
                MM(lps.t[0:64, 0:512], glrT.t[0:17, cl:cl + 64], gw.t[0:17, d * 512:(d + 1) * 512], True, True, [glrT, gw], [lps])
                ACT(la.t[:], lps.t[0:64, 0:512], AF.Exp, [lps], [la], scale=-1.0)
                ACT(la.t[:], la.t[:], AF.Ln, [la], [la], bias=1.0)
                CP(lah.t[:], la.t[:], [la], [lah], eng='act')
                TT(lal.t[:], la.t[:], lah.t[:], ALU.subtract, [la, lah], [lal])
                for dk in range(8):
                    MM(pss.t[0:64, 0:32], hT.t[:, dk, t0:t0 + 64], wmix.t[:, dk, O_DT:O_DT + 32],
                       dk == 0, dk == 7, [wmix, hT], [pss])
                TT(dts.t[:], pss.t[0:64, 0:32], rowsb.t[0:64, R_DTB:R_DTB + 32], ALU.add, [pss, rowsb], [dts])
                ACT(dts.t[:], dts.t[:], AF.Exp, [dts], [dts])
                ACT(dts.t[:], dts.t[:], AF.Ln, [dts], [dts], bias=1.0)
                TT(da.t[:], dts.t[:, dsl], aneg.t[:, dsl], ALU.mult, [dts, aneg], [da])
                CP(dahl.t[:, 0:16], da.t[:], [da], [dahl])
                TT(dahl.t[:, 16:32], da.t[:], dahl.t[:, 0:16], ALU.subtract, [da, dahl], [dahl])
                ndahl = ndahl_r.next()
                TS(ndahl.t[:], dahl.t[:], -1.0, None, ALU.mult, None, [dahl], [ndahl])
                vps = pp[2]
                for nb in range(2):
                    for dk in range(8):
                        MM(vps.t[0:64, nb * 512:(nb + 1) * 512], hT.t[:, dk, t0:t0 + 64],
                           wmix.t[:, dk, O_V + nb * 512:O_V + (nb + 1) * 512], dk == 0, dk == 7, [wmix, hT], [vps])
                CP(vsb.t[:], vps.t[0:64, :], [vps], [vsb], eng='act')
                return dict(dts=dts, da=da, cum_sb=cum_sb, dtw=dtw, ecum=ecum, eL=eL, xsb=xsb, xd=xd, xdw=xdw, la=la, vsb=vsb,
                            kdec=kdec, lah=lah, lal=lal, dahl=dahl, ndahl=ndahl, ybuf=ybuf, obuf=obuf, pss=pss, psc=psc, sps=sps, vps=vps, lps=lps)

            def chunk_mid(b, d, t0, cl, is_ctx, hd):
                dsl = slice(d * 16, d * 16 + 16)
                h16 = lambda ap: ap.rearrange("p (h q) -> p h q", h=16)
                h4 = lambda ap: ap.rearrange("p (h q) -> p h q", h=4)
                dts, da, cum_sb, dtw, ecum, eL = hd["dts"], hd["da"], hd["cum_sb"], hd["dtw"], hd["ecum"], hd["eL"]
                xsb, xd, xdw, la, vsb, kdec = hd["xsb"], hd["xd"], hd["xdw"], hd["la"], hd["vsb"], hd["kdec"]
                lah, lal, dahl, ybuf, obuf = hd["lah"], hd["lal"], hd["dahl"], hd["ybuf"], hd["obuf"]
                ndahl = hd["ndahl"]
                pss, psc, sps = hd["pss"], hd["psc"], hd["sps"]
                MTt, cmc, eT, qdT, kdT, qeT = MTt_r.next(), cmc_r.next(), eT_r.next(), qdT_r.next(), kdT_r.next(), qeT_r.next()
                hd.update(MTt=MTt, cmc=cmc, eT=eT, qdT=qdT, kdT=kdT, qeT=qeT)
                if not is_ctx:
                    CP(cmc.t[:].rearrange("p (g q) -> p g q", g=4), xbcT.t[:, 12:16, cl:cl + 64], [xbcT], [cmc], eng='pool')
                for (oc, lt) in ((slice(32, 48), Lb(d)), (slice(48, 64), Ub(d))):
                    MM(pss.t[0:64, oc], lt, dahl.t[:, 0:16], True, False, [constb, dahl], [pss])
                    MM(pss.t[0:64, oc], lt, dahl.t[:, 16:32], False, True, [constb, dahl], [pss])
                MM(pss.t[:, 64:80], constb.t[0:64, 0:128], dahl.t[:, 0:16], True, False, [constb, dahl], [pss])
                MM(pss.t[:, 64:80], constb.t[0:64, 0:128], dahl.t[:, 16:32], False, True, [constb, dahl], [pss])
                psx = pp[3]
                psxb = psx.t[:].bitcast(BF16)
                for j in range(12):
                    TR(psxb[0:64, j * 128:(j + 1) * 128], xbcT.t[:, j, cl:cl + 64], identb.t[:], [xbcT, identb], [psx])
                CP(xsb.t[:], psxb[0:64, 0:1536], [psx], [xsb], eng='act')
                if not is_ctx:
                    cq = pp[0]
                    for hb in range(2):
                        MM(cq.t[0:64, hb * 512:(hb + 1) * 512], identb.t[0:64, 0:64], nmrep.t[:, d, hb * 512:(hb + 1) * 512],
                           True, False, [identb, nmrep], [cq])
                    for h in range(16):
                        for part in range(2):
                            MM(cq.t[0:64, h * 64:(h + 1) * 64], dahl.t[:, part * 16 + h:part * 16 + h + 1].to_broadcast([64, 64]),
                               Lb(d), False, False, [dahl, constb], [cq])
                        for part in range(2):
                            MM(cq.t[0:64, h * 64:(h + 1) * 64], Lb(d),
                               ndahl.t[:, part * 16 + h:part * 16 + h + 1].to_broadcast([64, 64]),
                               False, (h % 8 == 7) and part == 1, [ndahl, constb], [cq])
                cps = pp[2]
                for h in range(4):
                    MM(cps.t[:, h * 64:(h + 1) * 64], lah.t[:, h * 128:(h + 1) * 128], Lb(d), True, False, [lah, constb], [cps])
                    MM(cps.t[:, h * 64:(h + 1) * 64], lal.t[:, h * 128:(h + 1) * 128], Lb(d), False, True, [lal, constb], [cps])
                for h in range(4):
                    co = slice(256 + h * 64, 256 + (h + 1) * 64)
                    MM(cps.t[:, co], lah.t[:, h * 128:(h + 1) * 128], Lb(d), True, False, [lah, constb], [cps])
                    MM(cps.t[:, co], lal.t[:, h * 128:(h + 1) * 128], Lb(d), False, False, [lal, constb], [cps])
                    MM(cps.t[:, co], lah.t[:, h * 128:(h + 1) * 128], nlm.t[:, d, :], False, False, [lah, nlm], [cps])
                    MM(cps.t[:, co], lal.t[:, h * 128:(h + 1) * 128], nlm.t[:, d, :], False, True, [lal, nlm], [cps])
                MM(cps.t[0:64, 512:1024], Ub(d), lah.t[:], True, False, [lah, constb], [cps])
                MM(cps.t[0:64, 512:1024], Ub(d), lal.t[:], False, True, [lal, constb], [cps])
                kps = pp[3]
                kpsb = kps.t[:].bitcast(BF16)
                for h in range(4):
                    TR(kpsb[0:64, h * 128:(h + 1) * 128], qkT.t[:, 4 + h, cl:cl + 64], identb.t[:], [qkT, identb], [kps])
                ACT(dtw.t[:], pss.t[0:64, 48:64], AF.Exp, [pss], [dtw])
                TT(dtw.t[:], dtw.t[:], dts.t[:, dsl], ALU.mult, [dtw, dts], [dtw])
                ACT(eL.t[:], pss.t[:, 64:80], AF.Exp, [pss], [eL])
                TT(h16(xd.t[:]), h16(xsb.t[:, 0:1024]), dts.t[:, dsl].unsqueeze(2).to_broadcast([64, 16, 64]), ALU.mult, [xsb, dts], [xd])
                TT(h16(xdw.t[:]), h16(xsb.t[:, 0:1024]), dtw.t[:].unsqueeze(2).to_broadcast([64, 16, 64]), ALU.mult, [xsb, dtw], [xdw], eng='pool')
                if not is_ctx:
                    ACT(seg.t[:], cq.t[0:64, :], AF.Exp, [cq], [seg])
                    ACT(ecum.t[:], pss.t[0:64, 32:48], AF.Exp, [pss], [ecum])
                    for g in range(4):
                        MM(psc.t[0:64, g * 64:(g + 1) * 64], xbcT.t[:, 8 + g, cl:cl + 64], xbcT.t[:, 12 + g, cl:cl + 64],
                           True, True, [xbcT], [psc])
                if not is_ctx:
                    TT(MTt.t[:].rearrange("p (g r q) -> p g r q", g=4, r=4),
                       seg.t[:].rearrange("p (g r q) -> p g r q", g=4, r=4),
                       h4(psc.t[0:64, 0:256]).unsqueeze(2).to_broadcast([64, 4, 4, 64]),
                       ALU.mult, [seg, psc], [MTt])
                cv = h4(cps.t[:, 0:256])
                mid = 31 if d == 0 else 32
                last = 63 if d == 0 else 0
                ACT(erc.t[:], cps.t[0:64, 512:1024], AF.Exp, [cps], [erc], scale=-1.0 / 16)
                ACT(ek.t[:], cps.t[:, 256:512], AF.Exp, [cps], [ek], scale=1.0 / 16)
                ACT(eT.t[:], cv[:, :, last], AF.Exp, [cps], [eT], scale=-1.0 / 16)
                TT(kdec.t[:], kpsb[0:64, 0:512], erc.t[:], ALU.mult, [kps, erc], [kdec])
                TT(h4(kdT.t[:]), qkT.t[:, 4:8, cl:cl + 64], h4(ek.t[:]), ALU.mult, [qkT, ek], [kdT])
                if not is_ctx:
                    ACT(eq.t[:], cps.t[:, 256:512], AF.Exp, [cps], [eq], scale=-1.0 / 16)
                    ACT(ec.t[:], cps.t[:, 0:256], AF.Exp, [cps], [ec], scale=-1.0 / 16)
                    TT(h4(qdT.t[:]), qkT.t[:, 0:4, cl:cl + 64], h4(eq.t[:]), ALU.mult, [qkT, eq], [qdT])
                    TT(h4(qeT.t[:]), qkT.t[:, 0:4, cl:cl + 64], h4(ec.t[:]), ALU.mult, [qkT, ec], [qeT], eng='pool')
                return hd

            def chunk_fin(b, d, t0, cl, is_ctx, hd):
                dsl = slice(d * 16, d * 16 + 16)
                h16 = lambda ap: ap.rearrange("p (h q) -> p h q", h=16)
                h4 = lambda ap: ap.rearrange("p (h q) -> p h q", h=4)
                ecum, eL, xsb, xd, xdw, vsb, kdec = hd["ecum"], hd["eL"], hd["xsb"], hd["xd"], hd["xdw"], hd["vsb"], hd["kdec"]
                ybuf, obuf, sps = hd["ybuf"], hd["obuf"], hd["sps"]
                MTt, cmc, eT, qdT, kdT, qeT = hd["MTt"], hd["cmc"], hd["eT"], hd["qdT"], hd["kdT"], hd["qeT"]
                TT(h16(Hs.t[:]), h16(Hs.t[:]), eL.t[:].unsqueeze(2).to_broadcast([128, 16, 64]), ALU.mult, [Hs, eL], [Hs], eng='pool')
                if not is_ctx:
                    for h in range(4):
                        hs = slice(h * 64, h * 64 + 64)
                        MM(sps.t[0:64, 256 + h * 64:256 + (h + 1) * 64], kdT.t[:, hs], qdT.t[:, hs], True, True, [kdT, qdT], [sps])
                    TT(h4(scTt.t[:]), h4(sps.t[0:64, 256:512]), Vm(d).unsqueeze(1).to_broadcast([64, 4, 64]), ALU.mult, [sps, consts], [scTt])
                    yi = pp[0]
                    for h in range(16):
                        hs = slice(h * 64, h * 64 + 64)
                        MM(yi.t[0:64, hs], MTt.t[:, hs], xd.t[:, hs], True, d == 1, [MTt, xd], [yi])
                        if d == 0:
                            MM(yi.t[0:64, hs], dkd.t[:, hs], xsb.t[:, hs], False, True, [dkd, xsb], [yi])
                    yh = pp[2]
                    for g in range(4):
                        gs = slice(g * 256, g * 256 + 256)
                        MM(yh.t[0:64, gs], cmc.t[:, g * 64:(g + 1) * 64], Hb.t[:, gs], True, True, [cmc, Hb], [yh])
                scp = pp[3]
                for h in range(16):
                    g = h // 4
                    hs = slice(h * 64, h * 64 + 64)
                    MM(scp.t[:, hs], xsb.t[:, 1024 + g * 128:1024 + (g + 1) * 128], xdw.t[:, hs], True, True, [xsb, xdw], [scp])
                if not is_ctx:
                    TT(h16(ybuf.t[:]), h16(yh.t[0:64, :]), ecum.t[:].unsqueeze(2).to_broadcast([64, 16, 64]), ALU.mult, [yh, ecum], [ybuf])
                    ops_ = pp[2]
                    for h in range(4):
                        hs = slice(h * 64, h * 64 + 64)
                        vs = slice(h * 256, h * 256 + 256)
                        MM(ops_.t[0:64, vs], scTt.t[:, hs], vsb.t[:, vs], True, False, [scTt, vsb], [ops_])
                        MM(ops_.t[0:64, vs], qeT.t[:, hs], Sb.t[:, vs], False, True, [qeT, Sb], [ops_])
                TT(Hs.t[:], Hs.t[:], scp.t[:], ALU.add, [Hs, scp], [Hs])
                CP(Hb.t[:], Hs.t[:], [Hs], [Hb], eng='act')
                sgp = pp[3]
                for h in range(4):
                    vs = slice(h * 256, h * 256 + 256)
                    MM(sgp.t[:, vs], kdec.t[:, h * 128:(h + 1) * 128], vsb.t[:, vs], True, True, [kdec, vsb], [sgp])
                if not is_ctx:
                    TT(ybuf.t[:], ybuf.t[:], yi.t[0:64, :], ALU.add, [ybuf, yi], [ybuf])
                    CP(obuf.t[:], ops_.t[0:64, :], [ops_], [obuf], eng='act')
                for h in range(4):
                    vs = slice(h * 256, h * 256 + 256)
                    STT(Ss.t[:, vs], Ss.t[:, vs], eT.t[:, h:h + 1], sgp.t[:, vs], ALU.mult, ALU.add, [Ss, eT, sgp], [Ss])
                CP(Sb.t[:], Ss.t[:], [Ss], [Sb], eng='act')
                if not is_ctx:
                    tx = t0 - CTXL
                    nm = "yo%d_%d" % (b, tx)
                    if d == 0:
                        p.dma('sp', yo_d[b, tx:tx + 64, 0:1024], ybuf.t[:], reads=[ybuf], dram_w=[nm + "y"])
                        p.dma('sp', yo_d[b, tx:tx + 64, 1024:2048], obuf.t[:], reads=[obuf], dram_w=[nm + "o"])
                    else:
                        p.dma('pool', yo_d[b, tx:tx + 64, 0:1024], ybuf.t[:], reads=[ybuf], dram_r=[nm + "y"], dram_w=[nm + "y"], accum=ALU.add)
                        p.dma('pool', yo_d[b, tx:tx + 64, 1024:2048], obuf.t[:], reads=[obuf], dram_r=[nm + "o"], dram_w=[nm + "o"], accum=ALU.add)

            nsc = dbg.get("nsc", 8)
            for b in range(nseq):
                for i in range(2 + 2 * nsc):
                    xt = scr8.t[:, 0:1024]
                    if i < 2:
                        p.dma('sp', xt, ctx_d[b, i * 128:(i + 1) * 128, :], writes=[scr8])
                        r = 2
                    else:
                        p.dma('sp', xt, x_d[b, (i - 2) * 128:(i - 1) * 128, :], writes=[scr8])
                        r = b
                    norm_to_T(xt, scr8, A1, B1fn, r, hT, i * 128, (sq.t[:], sq, ssq, scr8.t[:, 1024:2048], scr8))
                    if i >= 2:
                        p.dma('sp', hts_d[b, :, :, (i - 2) * 128:(i - 1) * 128], hT.t[:, :, i * 128:(i + 1) * 128],
                              reads=[hT], dram_w=["hts%d_%d" % (b, (i - 2) * 128)])
                if dbg.get("dump_hT") and b == 0:
                    dump("hT", hT, hT.t[:], [128, 8, NTOK])
                xhi = CTXL + 256 * nsc
                p.barrier()
                for d in range(2):
                    for st in (Hs, Ss):
                        MEMSET(st.t[:], 0.0, [st])
                    for st in (Hb, Sb):
                        MEMSET(st.t[:], 0.0, [st])
                    supers = [(0, 0, CTXL, True)] + [(CTXL + 256 * i, CTXL, xhi, False) for i in range(nsc)]
                    if d == 1:
                        supers = [supers[0]] + supers[1:][::-1]
                    chunks = []
                    for si, (tok0, lo, hi, is_ctx) in enumerate(supers):
                        cs = list(range(4) if d == 0 else range(3, -1, -1))
                        for ci, c in enumerate(cs):
                            chunks.append((tok0, lo, hi, is_ctx, c, ci == 0))

                    def prep(k):
                        tok0, lo, hi, is_ctx, c, first = chunks[k]
                        if first:
                            si = 0 if is_ctx else 1 + (tok0 - CTXL) // 256
                            nm = "sv%d_%d" % (b, si)
                            sxv = sx_d[b, si].rearrange("p (a t) -> p a t", a=16)
                            sqv = sqk_d[b, si].rearrange("p (a t) -> p a t", a=8)
                            if d == 0:
                                proj_super(tok0, lo, hi)
                                p.dma('sp', sxv, xbcT.t[:], reads=[xbcT], dram_w=[nm + "x"])
                                p.dma('sp', sqv, qkT.t[:], reads=[qkT], dram_w=[nm + "q"])
                                p.dma('sp', sg_d[b, si], glrT.t[:], reads=[glrT], dram_w=[nm + "g"])
                            else:
                                p.dma('sp', xbcT.t[:], sxv, writes=[xbcT], dram_r=[nm + "x"])
                                p.dma('sp', qkT.t[:], sqv, writes=[qkT], dram_r=[nm + "q"])
                                p.dma('sp', glrT.t[:], sg_d[b, si], writes=[glrT], dram_r=[nm + "g"])
                            if dbg.get("dump_xbc") and b == 0 and d == 0 and tok0 == dbg["dump_xbc"]:
                                dump("xbcT", xbcT, xbcT.t[:], [128, 16, 256])
                                dump("qkT", qkT, qkT.t[:], [128, 8, 256])
                        hd = chunk_head(b, d, tok0 + 64 * c, 64 * c, is_ctx)
                        return chunk_mid(b, d, tok0 + 64 * c, 64 * c, is_ctx, hd)

                    hd_cur = prep(0)
                    for k in range(len(chunks)):
                        hd_nxt = prep(k + 1) if k + 1 < len(chunks) else None
                        tok0, lo, hi, is_ctx, c, first = chunks[k]
                        chunk_fin(b, d, tok0 + 64 * c, 64 * c, is_ctx, hd_cur)
                        hd_cur = hd_nxt
                    if dbg.get("dump_state") and b == 0:
                        dump("H%d" % d, Hs, Hs.t[:], [128, 1024])
                        dump("S%d" % d, Ss, Ss.t[:], [128, 1024])
                p.barrier()
            p.es = es
          p.barrier()

        build_rest(nc, p, locals())
        p.finish()
    return nc, dbg_outs


def build_rest(nc, p, L):
    import types
    N = types.SimpleNamespace(**L)
    es, dbg, phases, nseq = N.es, N.dbg, N.phases, N.nseq
    MM, TR, ACT, TT, TS, STT, CP, MEMSET, PS, dump = N.MM, N.TR, N.ACT, N.TT, N.TS, N.STT, N.CP, N.MEMSET, N.PS, N.dump
    consts, vecs, modT, scT, A2, identf = N.consts, N.vecs, N.modT, N.scT, N.A2, N.identf
    x_d, yo_d, x1_d, out_d = N.x_d, N.yo_d, N.x1_d, N.out_d
    A1 = N.A1

    def load_w(dst, src_ap, nk):
        for dk in range(nk):
            p.dma('pool', dst.t[:, dk, :], src_ap[dk * 128:(dk + 1) * 128, :], writes=[dst])

    def compute_G(G, which, rb):
        with ExitStack() as esg:
            p.es = esg
            scR = p.sbuf("scR", [128, 8, 128], BF16)
            wb = p.sbuf("wgb", [128, 8, 1024], BF16)
            rb_t = p.sbuf("rbt", [128, 1024], F32)
            p.dma('sp', rb_t.t[:], N.rowsbig_d[0:1, rb:rb + 1024].partition_broadcast(128), writes=[rb_t])
            load_w(wb, N.w_ada_d[:, which * 1024:(which + 1) * 1024], 8)
            for b in range(NBL):
                CP(scR.t[:], scT.t[:, :, b:b + 1].to_broadcast([128, 8, 128]), [scT], [scR])
                gps = PS()
                for nb in range(2):
                    for dk in range(8):
                        MM(gps.t[:, nb * 512:(nb + 1) * 512], scR.t[:, dk, :], wb.t[:, dk, nb * 512:(nb + 1) * 512],
                           dk == 0, dk == 7, [wb, scR], [gps])
                TT(G.t[:, b, :], gps.t[:], rb_t.t[:], ALU.add, [gps, rb_t], [G])
            p.es = es
        p.barrier()

    def rstd_of(ssq_in, out, n, reads, writes):
        ACT(out, ssq_in, AF.Sqrt, reads, writes, bias=N.epsb.t[:, 0:1], scale=1.0 / n)
        p.op('dve', lambda e: e.reciprocal(out, out), writes, writes)

    if "P" in phases:
      with ExitStack() as esP:
        p.es = esP
        G1 = p.sbuf("G1", [128, NBL, 1024], F32)
        compute_G(G1, 2, RB_BA2)
        p.es = esP
        wzr = p.sbuf("wzr", [128, 8, 2048], BF16)
        wmg = p.sbuf("wmg", [128, 8, 2048], BF16)
        wbs = p.sbuf("wbs", [128, 8, 1024], BF16)
        wbg = p.sbuf("wbg", [128, 8, 1024], BF16)
        wo = p.sbuf("wo", [128, 8, 1024], BF16)
        for dk in range(8):
            p.dma('pool', wzr.t[:, dk, 0:1024], N.w_in_d[dk * 128:(dk + 1) * 128, 0:1024], writes=[wzr])
            p.dma('pool', wzr.t[:, dk, 1024:2048], N.w_in_d[dk * 128:(dk + 1) * 128, 5168:6192], writes=[wzr])
        load_w(wmg, N.w_merge_d, 8)
        load_w(wbs, N.w_brs_d, 8)
        load_w(wbg, N.w_brg_d, 8)
        load_w(wo, N.w_o_d, 8)
        for b in range(nseq):
            for q4 in range(4):
                p.dma('sp', x1_d[b, q4 * 512:(q4 + 1) * 512, :], x_d[b, q4 * 512:(q4 + 1) * 512, :],
                      dram_w=["x1_%d_%d" % (b, q4 * 512 + k * 128) for k in range(4)])
        TW = 256
        NS = TW // 128
        hT4s = [p.sbuf("hT4_%d" % i, [128, 8, TW], BF16) for i in range(2)]
        yT4s = [p.sbuf("yT4_%d" % i, [128, 8, TW], BF16) for i in range(2)]
        oT4s = [p.sbuf("oT4_%d" % i, [128, 8, TW], BF16) for i in range(2)]
        mT4s = [p.sbuf("mT4_%d" % i, [128, 8, TW], BF16) for i in range(2)]
        gT_r = Rot(p, "gT", [128, 2, TW], BF16, 2)
        m12_r = Rot(p, "m12", [128, 2 * TW], F32, 1)
        zrs = [p.sbuf("zr%d" % i, [128, 2048], BF16) for i in range(2)]
        yots = [p.sbuf("yot%d" % i, [128, 2048], F32) for i in range(2)]
        scr8 = p.sbuf("scr8p", [128, 2048], F32)
        tbuf = p.sbuf("tbuf", [128, 1024], F32)
        sq = p.sbuf("sqp", [128, 1024], BF16)
        sq2 = sq
        ssq = p.sbuf("ssqp", [128, 4], F32)
        so = p.sbuf("so", [128, 8], F32)

        def B1fn(j, r):
            return modT.t[:, j, r:r + 1]

        ntile = dbg.get("nt4", SEQ // TW)
        tiles = [(b, t) for b in range(nseq) for t in range(ntile)]

        mhalf = p.sbuf("mhalf", [128, 4], F32)
        MEMSET(mhalf.t[:], -0.5, [mhalf])
        sig_r = Rot(p, "sig", [128, 512], BF16, 2)

        def rstd_pow(ssq_ap, out_ap, n, res):
            ncol = ssq_ap.shape[-1]
            TS(out_ap, ssq_ap, 1.0 / n, EPS, ALU.mult, ALU.add, [res], [res])
            TT(out_ap, out_ap, mhalf.t[:, 0:ncol], ALU.pow, [res, mhalf], [res], eng='pool')

        def PA1a(ti, s):
            b, t = tiles[ti]
            yot = yots[s]
            tok = t * TW + s * 128
            p.dma('sp', yot.t[:], yo_d[b, tok:tok + 128, :], writes=[yot],
                  dram_r=["yo%d_%d%s" % (b, tok + o_, s_) for o_ in (0, 64) for s_ in ("y", "o")])

        def PA1b1(ti, s):
            b, t = tiles[ti]
            hT4 = hT4s[ti % 2]
            tok = t * TW + s * 128
            p.dma('sp', hT4.t[:, :, s * 128:(s + 1) * 128], N.hts_d[b, :, :, tok:tok + 128], writes=[hT4],
                  dram_r=["hts%d_%d" % (b, tok)])

        def PA1b2(ti, s):
            hT4 = hT4s[ti % 2]
            zr = zrs[s]
            for nb in range(4):
                ps = PS()
                for dk in range(8):
                    MM(ps.t[:, 0:512], hT4.t[:, dk, s * 128:(s + 1) * 128], wzr.t[:, dk, nb * 512:(nb + 1) * 512],
                       dk == 0, dk == 7, [hT4, wzr], [ps])
                sig = sig_r.next()
                ACT(sig.t[:], ps.t[:, 0:512], AF.Sigmoid, [ps], [sig])
                TT(zr.t[:, nb * 512:(nb + 1) * 512], ps.t[:, 0:512], sig.t[:], ALU.mult, [ps, sig], [zr])

        def PA2a(ti, s):
            zr, yot = zrs[s], yots[s]
            TT(yot.t[:, 0:1024], yot.t[:, 0:1024], zr.t[:, 0:1024], ALU.mult, [yot, zr], [yot])
            TT(sq2.t[:], yot.t[:, 0:1024], yot.t[:, 0:1024], ALU.mult, [yot], [sq2])
            p.op('dve', lambda e: e.reduce_sum(so.t[:, 0:1], sq2.t[:], AX.X), [sq2], [so])
            rstd_pow(so.t[:, 0:1], so.t[:, 0:1], 1024.0, so)
            TS(yot.t[:, 0:1024], yot.t[:, 0:1024], so.t[:, 0:1], None, ALU.mult, None, [yot, so], [yot])
            TT(sq2.t[:], yot.t[:, 1024:2048], yot.t[:, 1024:2048], ALU.mult, [yot], [sq2], eng='pool')
            p.op('dve', lambda e: e.reduce_sum(so.t[:, 4:8], sq2.t[:].rearrange("p (h v) -> p h v", h=4), AX.X), [sq2], [so])
            rstd_pow(so.t[:, 4:8], so.t[:, 4:8], 256.0, so)
            TT(yot.t[:, 1024:2048].rearrange("p (h v) -> p h v", h=4), yot.t[:, 1024:2048].rearrange("p (h v) -> p h v", h=4),
               so.t[:, 4:8].unsqueeze(2).to_broadcast([128, 4, 256]), ALU.mult, [yot, so], [yot])
            TT(yot.t[:, 1024:2048], yot.t[:, 1024:2048], zr.t[:, 1024:2048], ALU.mult, [yot, zr], [yot], eng='pool')

        def PA2b(ti, s):
            yT4, oT4 = yT4s[ti % 2], oT4s[ti % 2]
            yot = yots[s]
            for (half, dstT, vg) in ((0, yT4, V_SNG), (1, oT4, V_GNG)):
                ps = PS()
                for j in range(8):
                    TR(ps.t[:, j * 128:(j + 1) * 128], yot.t[:, half * 1024 + j * 128:half * 1024 + (j + 1) * 128],
                       identf, [yot, consts], [ps])
                if half == 0:
                    for j in range(8):
                        ACT(dstT.t[:, j, s * 128:(s + 1) * 128], ps.t[:, j * 128:(j + 1) * 128], AF.Identity,
                            [ps, vecs], [dstT], scale=vecs.t[:, vg + j:vg + j + 1])
                else:
                    TT(dstT.t[:, :, s * 128:(s + 1) * 128], ps.t[:].rearrange("p (j t) -> p j t", j=8),
                       vecs.t[:, vg:vg + 8].unsqueeze(2).to_broadcast([128, 8, 128]), ALU.mult, [ps, vecs], [dstT])

        def Bstep(ti, jc):
            hT4, yT4, oT4, mT4 = hT4s[ti % 2], yT4s[ti % 2], oT4s[ti % 2], mT4s[ti % 2]
            gT, m12 = gT_r.next(), m12_r.next()
            ps = PS()
            for gi in range(2):
                gc = gi * 8 + jc
                for dk in range(8):
                    MM(ps.t[:, gi * 512:gi * 512 + TW], wmg.t[:, dk, gc * 128:(gc + 1) * 128], hT4.t[:, dk, :], dk == 0, dk == 7, [wmg, hT4], [ps])
                ACT(gT.t[:, gi, :], ps.t[:, gi * 512:gi * 512 + TW], AF.Sigmoid, [ps, vecs], [gT], bias=vecs.t[:, V_BM + gc:V_BM + gc + 1])
            ps = PS()
            for dk in range(8):
                MM(ps.t[:, 0:TW], wbs.t[:, dk, jc * 128:(jc + 1) * 128], yT4.t[:, dk, :], dk == 0, dk == 7, [wbs, yT4], [ps])
            for dk in range(8):
                MM(ps.t[:, 512:512 + TW], wbg.t[:, dk, jc * 128:(jc + 1) * 128], oT4.t[:, dk, :], dk == 0, dk == 7, [wbg, oT4], [ps])
            TT(m12.t[:].rearrange("p (g t) -> p g t", g=2), ps.t[:].rearrange("p (g t) -> p g t", g=2)[:, :, 0:TW], gT.t[:], ALU.mult, [ps, gT], [m12])
            TT(mT4.t[:, jc, :], m12.t[:, 0:TW], m12.t[:, TW:2 * TW], ALU.add, [m12], [mT4], eng='pool')

        def Cstep(ti, s):
            b, t = tiles[ti]
            mT4 = mT4s[ti % 2]
            tok = t * TW + s * 128
            ps = PS()
            for nb in range(2):
                for dk in range(8):
                    MM(ps.t[:, nb * 512:(nb + 1) * 512], mT4.t[:, dk, s * 128:(s + 1) * 128], wo.t[:, dk, nb * 512:(nb + 1) * 512],
                       dk == 0, dk == 7, [mT4, wo], [ps])
            TT(tbuf.t[:], ps.t[:], G1.t[:, b, :], ALU.mult, [ps, G1], [tbuf])
            nm = "x1_%d_%d" % (b, tok)
            p.dma('pool', x1_d[b, tok:tok + 128, :], tbuf.t[:], reads=[tbuf], dram_r=[nm], dram_w=[nm], accum=ALU.add)

        A1m = N.A1
        for s in range(NS):
            PA1a(0, s)
            PA1b1(0, s)
            PA1b2(0, s)
        for s in range(NS):
            PA2a(0, s)
            PA2b(0, s)
        sched = {0: [(PA1a, 0)], 1: [(PA1b1, 0), (PA1a, 1)], 2: [(PA1b2, 0), (PA1b1, 1)], 3: [(PA2a, 0)],
                 4: [(PA1b2, 1)], 5: [(PA2b, 0), (PA2a, 1)], 7: [(PA2b, 1)]}
        for ti in range(len(tiles)):
            nxt = ti + 1 < len(tiles)
            for jc in range(8):
                Bstep(ti, jc)
                if nxt:
                    for (fn, s) in sched.get(jc, []):
                        fn(ti + 1, s)
            for s in range(NS):
                Cstep(ti, s)
        p.es = es
      p.barrier()

    if "F" in phases:
      with ExitStack() as esF:
        p.es = esF
        G2 = p.sbuf("G2", [128, NBL, 1024], F32)
        compute_G(G2, 5, RB_BA5)
        p.es = esF
        fng = p.sbuf("fng", [128, 1024], F32)
        p.dma('sp', fng.t[:], N.rowsbig_d[0:1, RB_FNG:RB_FNG + 1024].partition_broadcast(128), writes=[fng])
        wdn = p.sbuf("wdn", [128, 22, 1024], BF16)
        load_w(wdn, N.w_down_d, 22)
        wupr = Rot(p, "wup", [128, 8, 256], BF16, 3)
        h2T = p.sbuf("h2T", [128, 8, 1152], BF16)
        aT = p.sbuf("aT", [128, 22, 1024], BF16)
        scr8 = p.sbuf("scr8f", [128, 2048], F32)
        sq = p.sbuf("sqf", [128, 1024], BF16)
        ssq = p.sbuf("ssqf", [128, 4], F32)
        usb_r = Rot(p, "usb", [128, 17 * 66], BF16, 2)
        for _u in usb_r.b:
            MEMSET(_u.t[:], 0.0, [_u])
        dg_r = Rot(p, "dg", [128, 9, 128], BF16, 2)
        sg_r = Rot(p, "sg", [128, 1024], F32, 2)
        identb = N.identb

        def B2fn(j, r):
            return modT.t[:, 24 + j, r:r + 1]

        nj = dbg.get("nj", 22)
        wsrc = N.w_up_d.rearrange("(dk p) n -> p dk n", p=128)
        for b in range(nseq):
            for hf in range(dbg.get("nhalf", 2)):
                base = 0 if hf == 0 else 896
                for i in range(9):
                    tok = base + i * 128
                    xt = scr8.t[:, 0:1024]
                    p.dma('sp', xt, x1_d[b, tok:tok + 128, :], writes=[scr8], dram_r=["x1_%d_%d" % (b, tok)])
                    N.norm_to_T(xt, scr8, A2, B2fn, b, h2T, i * 128, (sq.t[:], sq, ssq, scr8.t[:, 1024:2048], scr8))
                if dbg.get("dump_h2T") and b == 0 and hf == 0:
                    dump("h2T", h2T, h2T.t[:], [128, 8, 1152])
                m0 = 0 if hf == 0 else 128
                h0 = 1024 if hf == 0 else 64
                off = 0 if hf == 0 else 1
                hrow = 16 if hf == 0 else 0
                for j in range(nj):
                    wu = wupr.next()
                    p.dma('pool', wu.t[:, :, 0:128], wsrc[:, :, j * 128:(j + 1) * 128], writes=[wu])
                    p.dma('pool', wu.t[:, :, 128:256], wsrc[:, :, DFF + j * 128:DFF + (j + 1) * 128], writes=[wu])
                    pcs = []
                    for part in range(2):
                        ch = part * 22 + j
                        pm = PS()
                        pc = PS()
                        for nb in range(2):
                            for dk in range(8):
                                MM(pm.t[:, nb * 512:(nb + 1) * 512], wu.t[:, dk, part * 128:(part + 1) * 128],
                                   h2T.t[:, dk, m0 + nb * 512:m0 + (nb + 1) * 512], dk == 0, dk == 7, [wu, h2T], [pm])
                        for dk in range(8):
                            MM(pc.t[:, 0:64], wu.t[:, dk, part * 128:(part + 1) * 128], h2T.t[:, dk, h0:h0 + 64],
                               dk == 0, dk == 7, [wu, h2T], [pc])
                        usb = usb_r.next()
                        u3 = usb.t[:].rearrange("p (r c) -> p r c", c=66)
                        CP(u3[:, off:off + 16, 1:65], pm.t[:].rearrange("p (r c) -> p r c", c=64), [pm], [usb], eng='act')
                        CP(u3[:, hrow, 1:65], pc.t[:, 0:64], [pc], [usb], eng='dve')
                        dg = dg_r.next()
                        wb_ = V_FCW + ch * 9
                        TT(dg.t[:], identb.t[:].unsqueeze(1).to_broadcast([128, 9, 128]),
                           vecs.t[:, wb_:wb_ + 9].unsqueeze(2).to_broadcast([128, 9, 128]), ALU.mult, [identb, vecs], [dg], eng='pool')
                        taps = [(0, 0)] + [(dr, dc) for dr in (-1, 0, 1) for dc in (-1, 0, 1) if (dr, dc) != (0, 0)]
                        for bank in range(2):
                            todo = []
                            for (dr, dc) in taps:
                                mlo = 1 if (hf == 0 and dr == -1) else 0
                                mhi = 15 if (hf == 1 and dr == 1) else 16
                                lo = max(mlo, bank * 8)
                                hi = min(mhi, bank * 8 + 8)
                                if hi <= lo:
                                    continue
                                todo.append((dr, dc, lo, hi))
                            for ti, (dr, dc, lo, hi) in enumerate(todo):
                                k = (dr + 1) * 3 + (dc + 1)
                                MM(pc.t[:, lo * 64:hi * 64], dg.t[:, k, :], u3[:, lo + off + dr:hi + off + dr, 1 + dc:65 + dc],
                                   ti == 0, ti == len(todo) - 1, [dg, usb], [pc])
                        pcs.append((pc, ch))
                    sg = sg_r.next()
                    (pcg, chg), (pcv, chv) = pcs
                    ACT(sg.t[:], pcg.t[:], AF.Silu, [pcg, vecs], [sg], bias=vecs.t[:, V_FCB + chg:V_FCB + chg + 1])
                    STT(aT.t[:, j, :], pcv.t[:], vecs.t[:, V_FCB + chv:V_FCB + chv + 1], sg.t[:], ALU.add, ALU.mult, [pcv, vecs, sg], [aT])
                for s in range(8):
                    tok = hf * 1024 + s * 128
                    ps = PS()
                    for nb in range(2):
                        for j in range(nj):
                            MM(ps.t[:, nb * 512:(nb + 1) * 512], aT.t[:, j, s * 128:(s + 1) * 128], wdn.t[:, j, nb * 512:(nb + 1) * 512],
                               j == 0, j == nj - 1, [aT, wdn], [ps])
                    xt = scr8.t[:, 0:1024]
                    x2 = scr8.t[:, 1024:2048]
                    p.dma('sp', xt, x1_d[b, tok:tok + 128, :], writes=[scr8], dram_r=["x1_%d_%d" % (b, tok)])
                    TT(x2, ps.t[:], G2.t[:, b, :], ALU.mult, [ps, G2], [scr8])
                    TT(x2, x2, xt, ALU.add, [scr8], [scr8])
                    ACT(sq.t[:], x2, AF.Square, [scr8], [sq])
                    p.op('dve', lambda e: e.reduce_sum(ssq.t[:, 0:1], sq.t[:], AX.X), [sq], [ssq])
                    rstd_of(ssq.t[:, 0:1], ssq.t[:, 1:2], 1024.0, [ssq], [ssq])
                    TS(x2, x2, ssq.t[:, 1:2], None, ALU.mult, None, [scr8, ssq], [scr8])
                    TT(xt, x2, fng.t[:], ALU.mult, [scr8, fng], [scr8])
                    p.dma('sp', out_d[b, tok:tok + 128, :], xt, reads=[scr8], final=True)
        p.es = es
      p.barrier()


_CACHE = {}


def kernel(**inputs):
    inp = {k: np.asarray(v) for k, v in inputs.items()}
    if "nc" not in _CACHE:
        _CACHE["nc"] = build_program()[0]
    nc = _CACHE["nc"]
    sh = prep_shared(inp)
    shared = dict(w_ada=inp["w_ada"][0], w_in=inp["w_in"][0], w_merge=inp["w_merge"][0], w_br_ssd=inp["w_br_ssd"][0],
                  w_br_gla=inp["w_br_gla"][0], w_o=inp["w_o"][0], w_up=inp["w_up"][0], w_down=inp["w_down"][0])
    shared.update(sh)
    shared = {k: np.ascontiguousarray(np.asarray(v, np.float32)) for k, v in shared.items()}
    in_maps = []
    for core in range(8):
        b0 = core * NBL
        cT = np.stack([inp["c"][b0], inp["c"][b0 + 1], inp["c_ctx"]], axis=1).astype(np.float32)
        cT = np.ascontiguousarray(cT.reshape(8, 128, 3).transpose(1, 0, 2))
        m = dict(shared)
        m["x"] = np.ascontiguousarray(inp["x"][b0:b0 + NBL], dtype=np.float32)
        m["ctx"] = np.ascontiguousarray(inp["ctx"][b0:b0 + NBL], dtype=np.float32)
        m["cT"] = cT
        in_maps.append(m)
    res = run_bass_kernel_spmd(nc, in_maps, core_ids=list(range(8)))
    out = np.concatenate([np.asarray(r["out"]) for r in res.results], axis=0)
    return out.astype(np.float32)
```

```python
import numpy as np
import concourse.bass as bass
import concourse.mybir as mybir
from concourse.bass_utils import run_bass_kernel_spmd
from contextlib import ExitStack

F32 = mybir.dt.float32
BF16 = mybir.dt.bfloat16
AF = mybir.ActivationFunctionType
ALU = mybir.AluOpType
AX = mybir.AxisListType


class Res:
    def __init__(self, name, t=None):
        self.name = name
        self.t = t
        self.lw = {}
        self.rd = {}


class Prog:
    NDMA = 8

    def __init__(self, nc, es):
        self.nc = nc
        self.es = es
        self.names = ['pe', 'act', 'dve', 'pool', 'sp']
        self.E = {'pe': nc.tensor, 'act': nc.scalar, 'dve': nc.vector, 'pool': nc.gpsimd, 'sp': nc.sync}
        self.cnt = {k: 0 for k in self.names}
        self.h = {}
        for k in self.names:
            self.h[('e', k)] = es.enter_context(nc.semaphore("s_" + k))
        self.dq = ('sp', 'pool', 'act')
        self.dcnt = {q: 0 for q in self.dq}
        self.dval = {}
        for q in self.dq:
            for i in range(self.NDMA):
                self.h[('d', q, i)] = es.enter_context(nc.semaphore("d_%s%d" % (q, i)))
                self.dval[('d', q, i)] = 0
        self.seen = {k: {} for k in self.names}
        self.dram = {}
        self.final = []
        self.nwait = 0

    def sbuf(self, name, shape, dtype):
        self.uid = getattr(self, "uid", 0) + 1
        t = self.es.enter_context(self.nc.sbuf_tensor("sb%d_%s" % (self.uid, name), list(shape), dtype))
        return Res(name, t)

    def psum(self, name, shape, dtype):
        t = self.es.enter_context(self.nc.psum_tensor("ps_" + name, list(shape), dtype))
        return Res(name, t)

    def _dres(self, name):
        if name not in self.dram:
            self.dram[name] = Res(name)
        return self.dram[name]

    def _collect(self, eng, reads, writes):
        need = {}

        def add(tok):
            if tok is None:
                return
            k, v = tok
            if need.get(k, 0) < v:
                need[k] = v

        for r in reads:
            for k, v in r.lw.items():
                add((k, v))
        for w in writes:
            for k, v in w.lw.items():
                add((k, v))
            for k, v in w.rd.items():
                add((k, v))
        out = []
        for k, v in need.items():
            if eng == 'pe' and k == ('e', 'pe'):
                continue
            if self.seen[eng].get(k, 0) >= v:
                continue
            self.seen[eng][k] = v
            out.append((k, v))
        return out

    def _mark(self, tok, reads, writes):
        k, v = tok
        for r in reads:
            if r.rd.get(k, 0) < v:
                r.rd[k] = v
        for w in writes:
            if w.lw.get(k, 0) < v:
                w.lw[k] = v
            w.rd = {}

    def op(self, eng, fn, reads=(), writes=()):
        waits = self._collect(eng, reads, writes)
        self.cnt[eng] += 1
        sem = self.h[('e', eng)]
        hs = [(self.h[k], v) for k, v in waits]
        self.nwait += len(hs)

        e = self.E[eng]
        for hh, v in hs:
            e.wait_ge(hh, v)
        fn(e).then_inc(sem, 1)
        self._mark((('e', eng), self.cnt[eng]), reads, writes)

    def dma(self, q, out_ap, in_ap, reads=(), writes=(), dram_r=(), dram_w=(), final=False, accum=None):
        reads = list(reads) + [self._dres(n) for n in dram_r]
        writes = list(writes) + [self._dres(n) for n in dram_w]
        i = self.dcnt[q] % self.NDMA
        self.dcnt[q] += 1
        key = ('d', q, i)
        prev = self.dval[key]
        waits = self._collect(q, reads, writes)
        if prev > 0 and self.seen[q].get(key, 0) < prev:
            self.seen[q][key] = prev
            waits.append((key, prev))
        self.dval[key] = prev + 16
        sem = self.h[key]
        hs = [(self.h[k], v) for k, v in waits]
        self.nwait += len(hs)

        e = self.E[q]
        for hh, v in hs:
            e.wait_ge(hh, v)
        if accum is not None:
            e.dma_start(out=out_ap, in_=in_ap, accum_op=accum).then_inc(sem, 16)
        else:
            e.dma_start(out=out_ap, in_=in_ap).then_inc(sem, 16)
        tok = (key, prev + 16)
        self._mark(tok, reads, writes)
        if final:
            self.final.append(tok)

    def barrier(self):
        toks = [(('e', k), self.cnt[k]) for k in self.names if self.cnt[k] > 0]
        toks += [(k, v) for k, v in self.dval.items() if v > 0]
        for eng in self.names:
            e = self.E[eng]
            for k, v in toks:
                if eng == 'pe' and k == ('e', 'pe'):
                    continue
                if self.seen[eng].get(k, 0) >= v:
                    continue
                self.seen[eng][k] = v
                e.wait_ge(self.h[k], v)

    def finish(self):
        fin = {}
        for k, v in self.final:
            fin[k] = max(fin.get(k, 0), v)
        hs = [(self.h[k], v) for k, v in fin.items()]

        e = self.E['sp']
        for hh, v in hs:
            e.wait_ge(hh, v)


D = 1024
SEQ = 2048
CTXL = 256
NTOK = CTXL + SEQ
NBL = 2
DFF = 2816
EPS = 1e-6
WM0, WMN = 1024, 4144
O_XBC, O_DT, O_Q, O_K, O_V, O_G = 0, 2048, 2080, 2592, 3104, 4128
V_N1G, V_N2G, V_BADA, V_CW, V_CB, V_BM, V_FCW, V_FCB, V_SNG, V_GNG, NV = 0, 8, 16, 64, 112, 128, 144, 540, 584, 592, 600
R_DTB, R_ALOG, R_DSK, NR = 0, 32, 64, 80
RB_FNG, RB_BA2, RB_BA5, NRB = 0, 1024, 2048, 3072
K_ID, K_ONE, K_L, K_U, K_NM, K_V, NCN = 0, 128, 256, 384, 512, 640, 768


def host_consts():
    c = np.zeros((128, NCN), np.float32)
    c[:, K_ID:K_ID + 128] = np.eye(128, dtype=np.float32)
    c[:, K_ONE:K_ONE + 128] = 1.0
    t = np.arange(64)[:, None]
    i = np.arange(64)[None, :]
    c[0:64, K_L:K_L + 64] = (t <= i)
    c[0:64, K_L + 64:K_L + 128] = (t >= i)
    c[0:64, K_U:K_U + 64] = (t > i)
    c[0:64, K_U + 64:K_U + 128] = (t < i)
    c[0:64, K_NM:K_NM + 64] = np.where(t <= i, 0.0, -30000.0)
    c[0:64, K_NM + 64:K_NM + 128] = np.where(t >= i, 0.0, -30000.0)
    c[0:64, K_V:K_V + 64] = (t <= i)
    c[0:64, K_V + 64:K_V + 128] = (t >= i)
    return c


def fm(v):
    v = np.asarray(v, np.float32)
    return np.ascontiguousarray(v.reshape(-1, 128).T)


def prep_shared(inp):
    vecs = np.zeros((128, NV), np.float32)
    vecs[:, V_N1G:V_N1G + 8] = fm(inp["norm1_g"][0])
    vecs[:, V_N2G:V_N2G + 8] = fm(inp["norm2_g"][0])
    vecs[:, V_BADA:V_BADA + 48] = fm(inp["b_ada"][0])
    cw = inp["ssd_conv_w"][0]
    for k in range(3):
        vecs[:, V_CW + k:V_CW + 48:3] = fm(cw[k])
    vecs[:, V_CB:V_CB + 16] = fm(inp["ssd_conv_b"][0])
    vecs[:, V_BM:V_BM + 16] = fm(inp["b_merge"][0])
    fw = inp["ffn_conv_w"][0].reshape(9, -1)
    for k in range(9):
        vecs[:, V_FCW + k:V_FCW + 396:9] = fm(fw[k])
    vecs[:, V_FCB:V_FCB + 44] = fm(inp["ffn_conv_b"][0])
    vecs[:, V_SNG:V_SNG + 8] = fm(inp["ssd_norm_g"][0])
    vecs[:, V_GNG:V_GNG + 8] = np.tile(fm(inp["gla_norm_g"][0]), (1, 4))
    rows = np.zeros((1, NR), np.float32)
    rowsbig = np.zeros((1, NRB), np.float32)
    rowsbig[0, RB_FNG:RB_FNG + 1024] = inp["final_norm_g"]
    rowsbig[0, RB_BA2:RB_BA2 + 1024] = inp["b_ada"][0][2048:3072]
    rowsbig[0, RB_BA5:RB_BA5 + 1024] = inp["b_ada"][0][5120:6144]
    rows[0, R_DTB:R_DTB + 32] = inp["ssd_dt_bias"][0].reshape(-1)
    rows[0, R_ALOG:R_ALOG + 32] = inp["ssd_a_log"][0].reshape(-1)
    rows[0, R_DSK:R_DSK + 16] = inp["ssd_d"][0]
    gw = np.zeros((17, 1024), np.float32)
    gw[0:16] = np.transpose(inp["gla_gate_w"][0], (1, 0, 2)).reshape(16, 1024)
    gw[16] = inp["gla_gate_b"][0].reshape(-1)
    return dict(vecs=vecs, rows=rows, rowsbig=rowsbig, gw=gw, consts=host_consts())


class Rot:
    def __init__(self, p, name, shape, dtype, n=2):
        self.b = [p.sbuf("%s%d" % (name, i), shape, dtype) for i in range(n)]
        self.i = 0

    def next(self):
        r = self.b[self.i % len(self.b)]
        self.i += 1
        return r


def build_program(dbg=None, nseq=NBL, phases="AMPF"):
    dbg = dbg or {}
    nc = bass.Bass("TRN2", target_bir_lowering=False)
    IN = "ExternalInput"

    def din(name, shape):
        return nc.dram_tensor(name, list(shape), F32, kind=IN).ap()

    x_d = din("x", [NBL, SEQ, D])
    ctx_d = din("ctx", [NBL, CTXL, D])
    cT_d = din("cT", [128, 8, 3])
    w_ada_d = din("w_ada", [D, 6 * D])
    w_in_d = din("w_in", [D, 6192])
    w_merge_d = din("w_merge", [D, 2 * D])
    w_brs_d = din("w_br_ssd", [D, D])
    w_brg_d = din("w_br_gla", [D, D])
    w_o_d = din("w_o", [D, D])
    w_up_d = din("w_up", [D, 2 * DFF])
    w_down_d = din("w_down", [DFF, D])
    gw_d = din("gw", [17, 1024])
    vecs_d = din("vecs", [128, NV])
    rows_d = din("rows", [1, NR])
    rowsbig_d = din("rowsbig", [1, NRB])
    consts_d = din("consts", [128, NCN])
    out_d = nc.dram_tensor("out", [NBL, SEQ, D], F32, kind="ExternalOutput").ap()
    yo_kind = "ExternalOutput" if dbg.get("dump_yo") else "Internal"
    yo_d = nc.dram_tensor("yo", [NBL, SEQ, 2 * D], F32, kind=yo_kind).ap()
    sx_d = nc.dram_tensor("sx", [NBL, 9, 128, 16 * 256], BF16, kind="Internal").ap()
    sqk_d = nc.dram_tensor("sqk", [NBL, 9, 128, 8 * 256], BF16, kind="Internal").ap()
    sg_d = nc.dram_tensor("sg", [NBL, 9, 32, 256], BF16, kind="Internal").ap()
    hts_d = nc.dram_tensor("hts", [NBL, 128, 8, SEQ], BF16, kind="Internal").ap()
    x1_kind = "ExternalOutput" if dbg.get("dump_x1") else "Internal"
    x1_d = nc.dram_tensor("x1", [NBL, SEQ, D], F32, kind=x1_kind).ap()
    dbg_outs = {}

    es = ExitStack()
    with es:
        p = Prog(nc, es)

        def MM(out, lhsT, rhs, start, stop, reads, writes):
            p.op('pe', lambda e: e.matmul(out, lhsT, rhs, start=start, stop=stop), reads, writes)

        def TR(out, in_, ident, reads, writes):
            p.op('pe', lambda e: e.transpose(out, in_, ident), reads, writes)

        def ACT(out, in_, func, reads, writes, bias=None, scale=None):
            kw = {}
            if bias is not None:
                kw['bias'] = bias
            if scale is not None:
                kw['scale'] = scale
            p.op('act', lambda e: e.activation(out, in_, func, **kw), reads, writes)

        def TT(out, in0, in1, op, reads, writes, eng='dve'):
            p.op(eng, lambda e: e.tensor_tensor(out, in0, in1, op), reads, writes)

        def TS(out, in0, s1, s2, op0, op1, reads, writes, eng='dve'):
            if s2 is None:
                p.op(eng, lambda e: e.tensor_scalar(out, in0, s1, None, op0), reads, writes)
            else:
                p.op(eng, lambda e: e.tensor_scalar(out, in0, s1, s2, op0, op1), reads, writes)

        def STT(out, in0, sc, in1, op0, op1, reads, writes):
            p.op('dve', lambda e: e.scalar_tensor_tensor(out, in0, sc, in1, op0, op1), reads, writes)

        def CP(out, in_, reads, writes, eng='dve'):
            if eng == 'act':
                p.op('act', lambda e: e.activation(out, in_, AF.Identity), reads, writes)
            else:
                p.op(eng, lambda e: e.tensor_copy(out, in_), reads, writes)

        def MEMSET(ap, val, writes, eng='dve'):
            p.op(eng, lambda e: e.memset(ap, val), (), writes)

        def dump(name, res, ap, shape):
            t = nc.dram_tensor("dbg_" + name, list(shape), ap.dtype, kind="ExternalOutput").ap()
            p.dma('sp', t, ap, reads=[res], final=True)
            dbg_outs[name] = t

        pp = [p.psum("pp%d" % i, [128, 1024], F32) for i in range(4)]
        pidx = [0]

        def PS():
            r = pp[pidx[0] % 4]
            pidx[0] += 1
            return r

        consts = p.sbuf("consts", [128, NCN], F32)
        vecs = p.sbuf("vecs", [128, NV], F32)
        rowsb = p.sbuf("rowsb", [128, NR], F32)
        gw = p.sbuf("gw", [17, 1024], BF16)
        constb = p.sbuf("constb", [128, 384], BF16)
        p.dma('sp', consts.t[:], consts_d, writes=[consts])
        p.dma('sp', vecs.t[:], vecs_d, writes=[vecs])
        p.dma('sp', rowsb.t[:], rows_d.partition_broadcast(128), writes=[rowsb])
        p.dma('pool', gw.t[:], gw_d, writes=[gw])
        identf = consts.t[:, K_ID:K_ID + 128]
        onesf = consts.t[:, K_ONE:K_ONE + 128]

        def Lm(d):
            return consts.t[0:64, K_L + 64 * d:K_L + 64 * d + 64]

        def Um(d):
            return consts.t[0:64, K_U + 64 * d:K_U + 64 * d + 64]

        def NMm(d):
            return consts.t[0:64, K_NM + 64 * d:K_NM + 64 * d + 64]

        def Vm(d):
            return consts.t[0:64, K_V + 64 * d:K_V + 64 * d + 64]

        identb = p.sbuf("identb", [128, 128], BF16)
        CP(identb.t[:], identf, [consts], [identb])
        CP(constb.t[:], consts.t[:, K_ONE:K_ONE + 384], [consts], [constb])

        def Lb(d):
            return constb.t[0:64, 128 + 64 * d:128 + 64 * d + 64]

        def Ub(d):
            return constb.t[0:64, 256 + 64 * d:256 + 64 * d + 64]
        aneg = p.sbuf("aneg", [64, 32], F32)
        ACT(aneg.t[:], rowsb.t[0:64, R_ALOG:R_ALOG + 32], AF.Exp, [rowsb], [aneg])
        TS(aneg.t[:], aneg.t[:], -1.0, None, ALU.mult, None, [aneg], [aneg])
        dkd = p.sbuf("dkd", [64, 1024], BF16)
        TT(dkd.t[:].rearrange("p (h q) -> p h q", h=16),
           consts.t[0:64, K_ID:K_ID + 64].unsqueeze(1).to_broadcast([64, 16, 64]),
           rowsb.t[0:64, R_DSK:R_DSK + 16].unsqueeze(2).to_broadcast([64, 16, 64]),
           ALU.mult, [consts, rowsb], [dkd])

        modT = p.sbuf("modT", [128, 48, 3], F32)
        scT = p.sbuf("scT", [128, 8, 3], BF16)
        A1 = p.sbuf("A1", [128, 8, 3], F32)
        A2 = p.sbuf("A2", [128, 8, 3], F32)

        with ExitStack() as esA:
            p.es = esA
            cT = p.sbuf("cT", [128, 8, 3], F32)
            p.dma('sp', cT.t[:], cT_d, writes=[cT])
            ACT(scT.t[:], cT.t[:], AF.Silu, [cT], [scT])
            wrot = Rot(p, "wada", [128, 8, 512], BF16, 2)
            mps = PS()
            for nb in range(12):
                wb = wrot.next()
                for dk in range(8):
                    p.dma('pool', wb.t[:, dk, :], w_ada_d[dk * 128:(dk + 1) * 128, nb * 512:(nb + 1) * 512], writes=[wb])
                for cc in range(4):
                    j = nb * 4 + cc
                    for dk in range(8):
                        MM(mps.t[:, j * 4:j * 4 + 3], wb.t[:, dk, cc * 128:(cc + 1) * 128], scT.t[:, dk, :],
                           dk == 0, dk == 7, [wb, scT], [mps])
            TT(modT.t[:], mps.t[:, 0:192].rearrange("p (j r) -> p j r", r=4)[:, :, 0:3],
               vecs.t[:, V_BADA:V_BADA + 48].unsqueeze(2).to_broadcast([128, 48, 3]), ALU.add, [mps, vecs], [modT])
            for (A, vg, so) in ((A1, V_N1G, 8), (A2, V_N2G, 32)):
                TS(A.t[:], modT.t[:, so:so + 8, :], 1.0, None, ALU.add, None, [modT], [A])
                TT(A.t[:], A.t[:], vecs.t[:, vg:vg + 8].unsqueeze(2).to_broadcast([128, 8, 3]), ALU.mult, [A, vecs], [A])
            p.es = es
        p.barrier()
        if dbg.get("dump_mod"):
            dump("modT", modT, modT.t[:], [128, 48, 3])
            dump("A1", A1, A1.t[:], [128, 8, 3])

        def norm_to_T(xt, xtr, A, Bap_fn, r, dst, dst_tok0, tmp):
            sq, sqr, ssq, xn, xnr = tmp
            ACT(sq, xt, AF.Square, [xtr], [sqr])
            p.op('dve', lambda e: e.reduce_sum(ssq.t[:, 0:1], sq, AX.X), [sqr], [ssq])
            ACT(ssq.t[:, 1:2], ssq.t[:, 0:1], AF.Sqrt, [ssq], [ssq], bias=EPS_AP[0], scale=1.0 / D)
            p.op('dve', lambda e: e.reciprocal(ssq.t[:, 2:3], ssq.t[:, 1:2]), [ssq], [ssq])
            TS(xn, xt, ssq.t[:, 2:3], None, ALU.mult, None, [xtr, ssq], [xnr])
            ps = PS()
            for j in range(8):
                TR(ps.t[:, j * 128:(j + 1) * 128], xn[:, j * 128:(j + 1) * 128], identf, [xnr, consts], [ps])
            for j in range(8):
                if j % 2 == 0:
                    TS(dst.t[:, j, dst_tok0:dst_tok0 + 128], ps.t[:, j * 128:(j + 1) * 128],
                       A.t[:, j, r:r + 1], Bap_fn(j, r), ALU.mult, ALU.add, [ps, A, modT], [dst])
                else:
                    ACT(dst.t[:, j, dst_tok0:dst_tok0 + 128], ps.t[:, j * 128:(j + 1) * 128], AF.Identity,
                        [ps, A, modT], [dst], bias=Bap_fn(j, r), scale=A.t[:, j, r:r + 1])

        epsb = p.sbuf("epsb", [128, 1], F32)
        MEMSET(epsb.t[:], EPS, [epsb])
        EPS_AP = [epsb.t[:, 0:1]]

        if "M" in phases:
          with ExitStack() as esM:
            p.es = esM
            wmix = p.sbuf("wmix", [128, 8, WMN], BF16)
            nmrep = p.sbuf("nmrep", [64, 2, 1024], BF16)
            for d_ in range(2):
                CP(nmrep.t[:, d_, :].rearrange("p (h q) -> p h q", h=16),
                   consts.t[0:64, K_NM + 64 * d_:K_NM + 64 * d_ + 64].unsqueeze(1).to_broadcast([64, 16, 64]), [consts], [nmrep])

            for dk in range(8):
                p.dma('pool', wmix.t[:, dk, :], w_in_d[dk * 128:(dk + 1) * 128, WM0:WM0 + WMN], writes=[wmix])
            hT = p.sbuf("hT", [128, 8, NTOK], BF16)
            scr8 = p.sbuf("scr8", [128, 2048], F32)
            sq = p.sbuf("sq", [128, 1024], BF16)
            ssq = p.sbuf("ssq", [128, 4], F32)
            Hs = p.sbuf("Hs", [128, 1024], F32)
            Hb = p.sbuf("Hb", [128, 1024], BF16)
            Ss = p.sbuf("Ss", [128, 1024], F32)
            Sb = p.sbuf("Sb", [128, 1024], BF16)
            xbcT = p.sbuf("xbcT", [128, 16, 256], BF16)
            qkT = p.sbuf("qkT", [128, 8, 256], BF16)
            glrT = p.sbuf("glrT", [32, 256], BF16)
            MEMSET(glrT.t[:], 1.0, [glrT])
            accr = Rot(p, "acc", [128, 256], F32, 2)
            dts_r = Rot(p, "dts", [64, 32], F32, 2)
            da_r = Rot(p, "da", [64, 16], F32, 2)
            cum_r = Rot(p, "cum_sb", [64, 16], F32, 2)
            dtw_r = Rot(p, "dtw", [64, 16], F32, 2)
            ecum_r = Rot(p, "ecum", [64, 16], F32, 2)
            eL_r = Rot(p, "eL", [128, 16], F32, 2)
            dahl_r = Rot(p, "dahl", [64, 32], BF16, 2)
            nlm = p.sbuf("nlm", [64, 2, 64], BF16)
            for d_ in range(2):
                mid_ = 31 if d_ == 0 else 32
                TS(nlm.t[:, d_, :], consts.t[0:64, K_L + 64 * d_ + mid_:K_L + 64 * d_ + mid_ + 1].to_broadcast([64, 64]),
                   -1.0, None, ALU.mult, None, [consts], [nlm])
            ndahl_r = Rot(p, "ndahl", [64, 32], BF16, 2)
            seg = p.sbuf("seg", [64, 1024], F32)
            MTt_r = Rot(p, "MTt", [64, 1024], BF16, 2)
            cmc_r = Rot(p, "cmc", [128, 256], BF16, 2)
            xsb_r = Rot(p, "xsb", [64, 1536], BF16, 2)
            xd_r = Rot(p, "xd", [64, 1024], BF16, 2)
            xdw_r = Rot(p, "xdw", [64, 1024], BF16, 2)
            ybufs = [Res("yb0", scr8.t[0:64, 0:1024]), Res("yb1", scr8.t[0:64, 1024:2048])]
            ycnt = [0]
            obuf_r = Rot(p, "obuf", [64, 1024], F32, 2)
            la_r = Rot(p, "la", [64, 512], F32, 1)
            lah_r = Rot(p, "lah", [64, 512], BF16, 1)
            lal_r = Rot(p, "lal", [64, 512], BF16, 1)
            ref = p.sbuf("ref", [128, 4], F32)
            dl = p.sbuf("dl", [128, 256], F32)
            eq = p.sbuf("eq", [128, 256], F32)
            ek = p.sbuf("ek", [128, 256], F32)
            ec = p.sbuf("ec", [128, 256], F32)
            eT_r = Rot(p, "eT", [128, 4], F32, 2)
            erc = p.sbuf("erc", [64, 512], F32)
            vsb_r = Rot(p, "vsb", [64, 1024], BF16, 2)
            kdec_r = Rot(p, "kdec", [64, 512], BF16, 2)
            qdT_r = Rot(p, "qdT", [128, 256], BF16, 2)
            kdT_r = Rot(p, "kdT", [128, 256], BF16, 2)
            qeT_r = Rot(p, "qeT", [128, 256], BF16, 2)
            scTt = p.sbuf("scTt", [64, 256], BF16)

            def B1fn(j, r):
                return modT.t[:, j, r:r + 1]

            small_pssB = Res("pssB", pp[1].t[:, 0:512])
            small_pscB = Res("pscB", pp[1].t[:, 512:1024])
            chunk_par = [0]
            pj = [0]

            def PSJ():
                r = pp[(0, 2, 3)[pj[0] % 3]]
                pj[0] += 1
                return r

            def proj_super(tok0, lo, hi):
                T = 256
                a = max(tok0 - 1, lo)
                e_ = min(tok0 + T + 1, hi)
                n = e_ - a
                off = a - (tok0 - 1)
                for cc in range(16):
                    ps = PSJ()
                    for dk in range(8):
                        MM(ps.t[:, off:off + n], wmix.t[:, dk, O_XBC + cc * 128:O_XBC + (cc + 1) * 128],
                           hT.t[:, dk, a:e_], dk == 0, dk == 7, [wmix, hT], [ps])
                    acc = accr.next()
                    cwb = V_CW + cc * 3
                    ACT(acc.t[:], ps.t[:, 1:257], AF.Identity, [ps, vecs], [acc],
                        bias=vecs.t[:, V_CB + cc:V_CB + cc + 1], scale=vecs.t[:, cwb + 1:cwb + 2])
                    i0 = 1 if off == 1 else 0
                    STT(acc.t[:, i0:256], ps.t[:, i0:256], vecs.t[:, cwb:cwb + 1], acc.t[:, i0:256],
                        ALU.mult, ALU.add, [ps, vecs, acc], [acc])
                    i1 = 255 if e_ < tok0 + T + 1 else 256
                    STT(acc.t[:, 0:i1], ps.t[:, 2:2 + i1], vecs.t[:, cwb + 2:cwb + 3], acc.t[:, 0:i1],
                        ALU.mult, ALU.add, [ps, vecs, acc], [acc])
                    ACT(xbcT.t[:, cc, :], acc.t[:], AF.Silu, [acc], [xbcT])
                for j in range(8):
                    ps = PSJ()
                    for dk in range(8):
                        MM(ps.t[:, 0:256], wmix.t[:, dk, O_Q + j * 128:O_Q + (j + 1) * 128],
                           hT.t[:, dk, tok0:tok0 + 256], dk == 0, dk == 7, [wmix, hT], [ps])
                    ACT(qkT.t[:, j, :], ps.t[:, 0:256], AF.Identity, [ps], [qkT],
                        scale=(128.0 ** -0.5) if j < 4 else 1.0)
                ps = PSJ()
                for dk in range(8):
                    MM(ps.t[0:16, 0:256], wmix.t[:, dk, O_G:O_G + 16], hT.t[:, dk, tok0:tok0 + 256],
                       dk == 0, dk == 7, [wmix, hT], [ps])
                CP(glrT.t[0:16, :], ps.t[0:16, 0:256], [ps], [glrT])

            def chunk_head(b, d, t0, cl, is_ctx):
                dsl = slice(d * 16, d * 16 + 16)
                dts, da, cum_sb, dtw, ecum, eL = dts_r.next(), da_r.next(), cum_r.next(), dtw_r.next(), ecum_r.next(), eL_r.next()
                xsb, xd, xdw = xsb_r.next(), xd_r.next(), xdw_r.next()
                la, vsb, kdec = la_r.next(), vsb_r.next(), kdec_r.next()
                lah, lal, dahl = lah_r.next(), lal_r.next(), dahl_r.next()
                ybuf = obuf = None
                if not is_ctx:
                    ybuf = ybufs[ycnt[0] % 2]
                    ycnt[0] += 1
                    obuf = obuf_r.next()
                par = chunk_par[0] % 2
                chunk_par[0] += 1
                pss = small_pssB
                psc = small_pscB
                sps = small_pssB
                h16 = lambda ap: ap.rearrange("p (h q) -> p h q", h=16)
                h4 = lambda ap: ap.rearrange("p (h q) -> p h q", h=4)
                lps = pp[0]
                MM(lps.t[0:64, 0:512], glrT.t[0:17, cl:cl + 64], gw.t[0:17, d * 512:(d + 1) * 512], True, True, [glrT, gw], [lps])
                ACT(la.t[:], lps.t[0:64, 0:512], AF.Exp, [lps], [la], scale=-1.0)
                ACT(la.t[:], la.t[:], AF.Ln, [la], [la], bias=1.0)
                CP(lah.t[:], la.t[:], [la], [lah], eng='act')
                TT(lal.t[:], la.t[:], lah.t[:], ALU.subtract, [la, lah], [lal])
                for dk in range(8):
                    MM(pss.t[0:64, 0:32], hT.t[:, dk, t0:t0 + 64], wmix.t[:, dk, O_DT:O_DT + 32],
                       dk == 0, dk == 7, [wmix, hT], [pss])
                TT(dts.t[:], pss.t[0:64, 0:32], rowsb.t[0:64, R_DTB:R_DTB + 32], ALU.add, [pss, rowsb], [dts])
                ACT(dts.t[:], dts.t[:], AF.Exp, [dts], [dts])
                ACT(dts.t[:], dts.t[:], AF.Ln, [dts], [dts], bias=1.0)
                TT(da.t[:], dts.t[:, dsl], aneg.t[:, dsl], ALU.mult, [dts, aneg], [da])
                CP(dahl.t[:, 0:16], da.t[:], [da], [dahl])
                TT(dahl.t[:, 16:32], da.t[:], dahl.t[:, 0:16], ALU.subtract, [da, dahl], [dahl])
                ndahl = ndahl_r.next()
                TS(ndahl.t[:], dahl.t[:], -1.0, None, ALU.mult, None, [dahl], [ndahl])
                vps = pp[2]
                for nb in range(2):
                    for dk in range(8):
                        MM(vps.t[0:64, nb * 512:(nb + 1) * 512], hT.t[:, dk, t0:t0 + 64],
                           wmix.t[:, dk, O_V + nb * 512:O_V + (nb + 1) * 512], dk == 0, dk == 7, [wmix, hT], [vps])
                CP(vsb.t[:], vps.t[0:64, :], [vps], [vsb], eng='act')
                return dict(dts=dts, da=da, cum_sb=cum_sb, dtw=dtw, ecum=ecum, eL=eL, xsb=xsb, xd=xd, xdw=xdw, la=la, vsb=vsb,
                            kdec=kdec, lah=lah, lal=lal, dahl=dahl, ndahl=ndahl, ybuf=ybuf, obuf=obuf, pss=pss, psc=psc, sps=sps, vps=vps, lps=lps)

            def chunk_mid(b, d, t0, cl, is_ctx, hd):
                dsl = slice(d * 16, d * 16 + 16)
                h16 = lambda ap: ap.rearrange("p (h q) -> p h q", h=16)
                h4 = lambda ap: ap.rearrange("p (h q) -> p h q", h=4)
                dts, da, cum_sb, dtw, ecum, eL = hd["dts"], hd["da"], hd["cum_sb"], hd["dtw"], hd["ecum"], hd["eL"]
                xsb, xd, xdw, la, vsb, kdec = hd["xsb"], hd["xd"], hd["xdw"], hd["la"], hd["vsb"], hd["kdec"]
                lah, lal, dahl, ybuf, obuf = hd["lah"], hd["lal"], hd["dahl"], hd["ybuf"], hd["obuf"]
                ndahl = hd["ndahl"]
                pss, psc, sps = hd["pss"], hd["psc"], hd["sps"]
                MTt, cmc, eT, qdT, kdT, qeT = MTt_r.next(), cmc_r.next(), eT_r.next(), qdT_r.next(), kdT_r.next(), qeT_r.next()
                hd.update(MTt=MTt, cmc=cmc, eT=eT, qdT=qdT, kdT=kdT, qeT=qeT)
                if not is_ctx:
                    CP(cmc.t[:].rearrange("p (g q) -> p g q", g=4), xbcT.t[:, 12:16, cl:cl + 64], [xbcT], [cmc], eng='pool')
                for (oc, lt) in ((slice(32, 48), Lb(d)), (slice(48, 64), Ub(d))):
                    MM(pss.t[0:64, oc], lt, dahl.t[:, 0:16], True, False, [constb, dahl], [pss])
                    MM(pss.t[0:64, oc], lt, dahl.t[:, 16:32], False, True, [constb, dahl], [pss])
                MM(pss.t[:, 64:80], constb.t[0:64, 0:128], dahl.t[:, 0:16], True, False, [constb, dahl], [pss])
                MM(pss.t[:, 64:80], constb.t[0:64, 0:128], dahl.t[:, 16:32], False, True, [constb, dahl], [pss])
                psx = pp[3]
                psxb = psx.t[:].bitcast(BF16)
                for j in range(12):
                    TR(psxb[0:64, j * 128:(j + 1) * 128], xbcT.t[:, j, cl:cl + 64], identb.t[:], [xbcT, identb], [psx])
                CP(xsb.t[:], psxb[0:64, 0:1536], [psx], [xsb], eng='act')
                if not is_ctx:
                    cq = pp[0]
                    for hb in range(2):
                        MM(cq.t[0:64, hb * 512:(hb + 1) * 512], identb.t[0:64, 0:64], nmrep.t[:, d, hb * 512:(hb + 1) * 512],
                           True, False, [identb, nmrep], [cq])
                    for h in range(16):
                        for part in range(2):
                            MM(cq.t[0:64, h * 64:(h + 1) * 64], dahl.t[:, part * 16 + h:part * 16 + h + 1].to_broadcast([64, 64]),
                               Lb(d), False, False, [dahl, constb], [cq])
                        for part in range(2):
                            MM(cq.t[0:64, h * 64:(h + 1) * 64], Lb(d),
                               ndahl.t[:, part * 16 + h:part * 16 + h + 1].to_broadcast([64, 64]),
                               False, (h % 8 == 7) and part == 1, [ndahl, constb], [cq])
                cps = pp[2]
                for h in range(4):
                    MM(cps.t[:, h * 64:(h + 1) * 64], lah.t[:, h * 128:(h + 1) * 128], Lb(d), True, False, [lah, constb], [cps])
                    MM(cps.t[:, h * 64:(h + 1) * 64], lal.t[:, h * 128:(h + 1) * 128], Lb(d), False, True, [lal, constb], [cps])
                for h in range(4):
                    co = slice(256 + h * 64, 256 + (h + 1) * 64)
                    MM(cps.t[:, co], lah.t[:, h * 128:(h + 1) * 128], Lb(d), True, False, [lah, constb], [cps])
                    MM(cps.t[:, co], lal.t[:, h * 128:(h + 1) * 128], Lb(d), False, False, [lal, constb], [cps])
                    MM(cps.t[:, co], lah.t[:, h * 128:(h + 1) * 128], nlm.t[:, d, :], False, False, [lah, nlm], [cps])
                    MM(cps.t[:, co], lal.t[:, h * 128:(h + 1) * 128], nlm.t[:, d, :], False, True, [lal, nlm], [cps])
                MM(cps.t[0:64, 512:1024], Ub(d), lah.t[:], True, False, [lah, constb], [cps])
                MM(cps.t[0:64, 512:1024], Ub(d), lal.t[:], False, True, [lal, constb], [cps])
                kps = pp[3]
                kpsb = kps.t[:].bitcast(BF16)
                for h in range(4):
                    TR(kpsb[0:64, h * 128:(h + 1) * 128], qkT.t[:, 4 + h, cl:cl + 64], identb.t[:], [qkT, identb], [kps])
                ACT(dtw.t[:], pss.t[0:64, 48:64], AF.Exp, [pss], [dtw])
                TT(dtw.t[:], dtw.t[:], dts.t[:, dsl], ALU.mult, [dtw, dts], [dtw])
                ACT(eL.t[:], pss.t[:, 64:80], AF.Exp, [pss], [eL])
                TT(h16(xd.t[:]), h16(xsb.t[:, 0:1024]), dts.t[:, dsl].unsqueeze(2).to_broadcast([64, 16, 64]), ALU.mult, [xsb, dts], [xd], eng='pool')
                TT(h16(xdw.t[:]), h16(xsb.t[:, 0:1024]), dtw.t[:].unsqueeze(2).to_broadcast([64, 16, 64]), ALU.mult, [xsb, dtw], [xdw], eng='pool')
                if not is_ctx:
                    ACT(seg.t[:], cq.t[0:64, :], AF.Exp, [cq], [seg])
                    ACT(ecum.t[:], pss.t[0:64, 32:48], AF.Exp, [pss], [ecum])
                    for g in range(4):
                        MM(psc.t[0:64, g * 64:(g + 1) * 64], xbcT.t[:, 8 + g, cl:cl + 64], xbcT.t[:, 12 + g, cl:cl + 64],
                           True, True, [xbcT], [psc])
                if not is_ctx:
                    TT(MTt.t[:].rearrange("p (g r q) -> p g r q", g=4, r=4),
                       seg.t[:].rearrange("p (g r q) -> p g r q", g=4, r=4),
                       h4(psc.t[0:64, 0:256]).unsqueeze(2).to_broadcast([64, 4, 4, 64]),
                       ALU.mult, [seg, psc], [MTt])
                cv = h4(cps.t[:, 0:256])
                mid = 31 if d == 0 else 32
                last = 63 if d == 0 else 0
                ACT(erc.t[:], cps.t[0:64, 512:1024], AF.Exp, [cps], [erc], scale=-1.0 / 16)
                ACT(ek.t[:], cps.t[:, 256:512], AF.Exp, [cps], [ek], scale=1.0 / 16)
                ACT(eT.t[:], cv[:, :, last], AF.Exp, [cps], [eT], scale=-1.0 / 16)
                TT(kdec.t[:], kpsb[0:64, 0:512], erc.t[:], ALU.mult, [kps, erc], [kdec])
                TT(h4(kdT.t[:]), qkT.t[:, 4:8, cl:cl + 64], h4(ek.t[:]), ALU.mult, [qkT, ek], [kdT], eng='pool')
                if not is_ctx:
                    ACT(eq.t[:], cps.t[:, 256:512], AF.Exp, [cps], [eq], scale=-1.0 / 16)
                    ACT(ec.t[:], cps.t[:, 0:256], AF.Exp, [cps], [ec], scale=-1.0 / 16)
                    TT(h4(qdT.t[:]), qkT.t[:, 0:4, cl:cl + 64], h4(eq.t[:]), ALU.mult, [qkT, eq], [qdT], eng='pool')
                    TT(h4(qeT.t[:]), qkT.t[:, 0:4, cl:cl + 64], h4(ec.t[:]), ALU.mult, [qkT, ec], [qeT], eng='pool')
                return hd

            def chunk_fin(b, d, t0, cl, is_ctx, hd):
                dsl = slice(d * 16, d * 16 + 16)
                h16 = lambda ap: ap.rearrange("p (h q) -> p h q", h=16)
                h4 = lambda ap: ap.rearrange("p (h q) -> p h q", h=4)
                ecum, eL, xsb, xd, xdw, vsb, kdec = hd["ecum"], hd["eL"], hd["xsb"], hd["xd"], hd["xdw"], hd["vsb"], hd["kdec"]
                ybuf, obuf, sps = hd["ybuf"], hd["obuf"], hd["sps"]
                MTt, cmc, eT, qdT, kdT, qeT = hd["MTt"], hd["cmc"], hd["eT"], hd["qdT"], hd["kdT"], hd["qeT"]
                TT(h16(Hs.t[:]), h16(Hs.t[:]), eL.t[:].unsqueeze(2).to_broadcast([128, 16, 64]), ALU.mult, [Hs, eL], [Hs], eng='pool')
                if not is_ctx:
                    for h in range(4):
                        hs = slice(h * 64, h * 64 + 64)
                        MM(sps.t[0:64, 256 + h * 64:256 + (h + 1) * 64], kdT.t[:, hs], qdT.t[:, hs], True, True, [kdT, qdT], [sps])
                    TT(h4(scTt.t[:]), h4(sps.t[0:64, 256:512]), Vm(d).unsqueeze(1).to_broadcast([64, 4, 64]), ALU.mult, [sps, consts], [scTt])
                    yi = pp[0]
                    for h in range(16):
                        hs = slice(h * 64, h * 64 + 64)
                        MM(yi.t[0:64, hs], MTt.t[:, hs], xd.t[:, hs], True, d == 1, [MTt, xd], [yi])
                        if d == 0:
                            MM(yi.t[0:64, hs], dkd.t[:, hs], xsb.t[:, hs], False, True, [dkd, xsb], [yi])
                    yh = pp[2]
                    for g in range(4):
                        gs = slice(g * 256, g * 256 + 256)
                        MM(yh.t[0:64, gs], cmc.t[:, g * 64:(g + 1) * 64], Hb.t[:, gs], True, True, [cmc, Hb], [yh])
                scp = pp[3]
                for h in range(16):
                    g = h // 4
                    hs = slice(h * 64, h * 64 + 64)
                    MM(scp.t[:, hs], xsb.t[:, 1024 + g * 128:1024 + (g + 1) * 128], xdw.t[:, hs], True, True, [xsb, xdw], [scp])
                if not is_ctx:
                    TT(h16(ybuf.t[:]), h16(yh.t[0:64, :]), ecum.t[:].unsqueeze(2).to_broadcast([64, 16, 64]), ALU.mult, [yh, ecum], [ybuf])
                    ops_ = pp[2]
                    for h in range(4):
                        hs = slice(h * 64, h * 64 + 64)
                        vs = slice(h * 256, h * 256 + 256)
                        MM(ops_.t[0:64, vs], scTt.t[:, hs], vsb.t[:, vs], True, False, [scTt, vsb], [ops_])
                        MM(ops_.t[0:64, vs], qeT.t[:, hs], Sb.t[:, vs], False, True, [qeT, Sb], [ops_])
                TT(Hs.t[:], Hs.t[:], scp.t[:], ALU.add, [Hs, scp], [Hs])
                CP(Hb.t[:], Hs.t[:], [Hs], [Hb], eng='act')
                sgp = pp[3]
                for h in range(4):
                    vs = slice(h * 256, h * 256 + 256)
                    MM(sgp.t[:, vs], kdec.t[:, h * 128:(h + 1) * 128], vsb.t[:, vs], True, True, [kdec, vsb], [sgp])
                if not is_ctx:
                    TT(ybuf.t[:], ybuf.t[:], yi.t[0:64, :], ALU.add, [ybuf, yi], [ybuf])
                    CP(obuf.t[:], ops_.t[0:64, :], [ops_], [obuf], eng='act')
                for h in range(4):
                    vs = slice(h * 256, h * 256 + 256)
                    STT(Ss.t[:, vs], Ss.t[:, vs], eT.t[:, h:h + 1], sgp.t[:, vs], ALU.mult, ALU.add, [Ss, eT, sgp], [Ss])
                CP(Sb.t[:], Ss.t[:], [Ss], [Sb], eng='act')
                if not is_ctx:
                    tx = t0 - CTXL
                    nm = "yo%d_%d" % (b, tx)
                    if d == 0:
                        p.dma('sp', yo_d[b, tx:tx + 64, 0:1024], ybuf.t[:], reads=[ybuf], dram_w=[nm + "y"])
                        p.dma('sp', yo_d[b, tx:tx + 64, 1024:2048], obuf.t[:], reads=[obuf], dram_w=[nm + "o"])
                    else:
                        p.dma('pool', yo_d[b, tx:tx + 64, 0:1024], ybuf.t[:], reads=[ybuf], dram_r=[nm + "y"], dram_w=[nm + "y"], accum=ALU.add)
                        p.dma('pool', yo_d[b, tx:tx + 64, 1024:2048], obuf.t[:], reads=[obuf], dram_r=[nm + "o"], dram_w=[nm + "o"], accum=ALU.add)

            nsc = dbg.get("nsc", 8)
            for b in range(nseq):
                for i in range(2 + 2 * nsc):
                    xt = scr8.t[:, 0:1024]
                    if i < 2:
                        p.dma('sp', xt, ctx_d[b, i * 128:(i + 1) * 128, :], writes=[scr8])
                        r = 2
                    else:
                        p.dma('sp', xt, x_d[b, (i - 2) * 128:(i - 1) * 128, :], writes=[scr8])
                        r = b
                    norm_to_T(xt, scr8, A1, B1fn, r, hT, i * 128, (sq.t[:], sq, ssq, scr8.t[:, 1024:2048], scr8))
                    if i >= 2:
                        p.dma('sp', hts_d[b, :, :, (i - 2) * 128:(i - 1) * 128], hT.t[:, :, i * 128:(i + 1) * 128],
                              reads=[hT], dram_w=["hts%d_%d" % (b, (i - 2) * 128)])
                if dbg.get("dump_hT") and b == 0:
                    dump("hT", hT, hT.t[:], [128, 8, NTOK])
                xhi = CTXL + 256 * nsc
                p.barrier()
                for d in range(2):
                    for st in (Hs, Ss):
                        MEMSET(st.t[:], 0.0, [st])
                    for st in (Hb, Sb):
                        MEMSET(st.t[:], 0.0, [st])
                    supers = [(0, 0, CTXL, True)] + [(CTXL + 256 * i, CTXL, xhi, False) for i in range(nsc)]
                    if d == 1:
                        supers = [supers[0]] + supers[1:][::-1]
                    chunks = []
                    for si, (tok0, lo, hi, is_ctx) in enumerate(supers):
                        cs = list(range(4) if d == 0 else range(3, -1, -1))
                        for ci, c in enumerate(cs):
                            chunks.append((tok0, lo, hi, is_ctx, c, ci == 0))

                    def prep(k):
                        tok0, lo, hi, is_ctx, c, first = chunks[k]
                        if first:
                            si = 0 if is_ctx else 1 + (tok0 - CTXL) // 256
                            nm = "sv%d_%d" % (b, si)
                            sxv = sx_d[b, si].rearrange("p (a t) -> p a t", a=16)
                            sqv = sqk_d[b, si].rearrange("p (a t) -> p a t", a=8)
                            if d == 0:
                                proj_super(tok0, lo, hi)
                                p.dma('sp', sxv, xbcT.t[:], reads=[xbcT], dram_w=[nm + "x"])
                                p.dma('sp', sqv, qkT.t[:], reads=[qkT], dram_w=[nm + "q"])
                                p.dma('sp', sg_d[b, si], glrT.t[:], reads=[glrT], dram_w=[nm + "g"])
                            else:
                                p.dma('sp', xbcT.t[:], sxv, writes=[xbcT], dram_r=[nm + "x"])
                                p.dma('sp', qkT.t[:], sqv, writes=[qkT], dram_r=[nm + "q"])
                                p.dma('sp', glrT.t[:], sg_d[b, si], writes=[glrT], dram_r=[nm + "g"])
                            if dbg.get("dump_xbc") and b == 0 and d == 0 and tok0 == dbg["dump_xbc"]:
                                dump("xbcT", xbcT, xbcT.t[:], [128, 16, 256])
                                dump("qkT", qkT, qkT.t[:], [128, 8, 256])
                        hd = chunk_head(b, d, tok0 + 64 * c, 64 * c, is_ctx)
                        return chunk_mid(b, d, tok0 + 64 * c, 64 * c, is_ctx, hd)

                    hd_cur = prep(0)
                    for k in range(len(chunks)):
                        hd_nxt = prep(k + 1) if k + 1 < len(chunks) else None
                        tok0, lo, hi, is_ctx, c, first = chunks[k]
                        chunk_fin(b, d, tok0 + 64 * c, 64 * c, is_ctx, hd_cur)
                        hd_cur = hd_nxt
                    if dbg.get("dump_state") and b == 0:
                        dump("H%d" % d, Hs, Hs.t[:], [128, 1024])
                        dump("S%d" % d, Ss, Ss.t[:], [128, 1024])
                p.barrier()
            p.es = es
          p.barrier()

        build_rest(nc, p, locals())
        p.finish()
    return nc, dbg_outs


def build_rest(nc, p, L):
    import types
    N = types.SimpleNamespace(**L)
    es, dbg, phases, nseq = N.es, N.dbg, N.phases, N.nseq
    MM, TR, ACT, TT, TS, STT, CP, MEMSET, PS, dump = N.MM, N.TR, N.ACT, N.TT, N.TS, N.STT, N.CP, N.MEMSET, N.PS, N.dump
    consts, vecs, modT, scT, A2, identf = N.consts, N.vecs, N.modT, N.scT, N.A2, N.identf
    x_d, yo_d, x1_d, out_d = N.x_d, N.yo_d, N.x1_d, N.out_d
    A1 = N.A1

    def load_w(dst, src_ap, nk):
        for dk in range(nk):
            p.dma('pool', dst.t[:, dk, :], src_ap[dk * 128:(dk + 1) * 128, :], writes=[dst])

    def compute_G(G, which, rb):
        with ExitStack() as esg:
            p.es = esg
            scR = p.sbuf("scR", [128, 8, 128], BF16)
            wb = p.sbuf("wgb", [128, 8, 1024], BF16)
            rb_t = p.sbuf("rbt", [128, 1024], F32)
            p.dma('sp', rb_t.t[:], N.rowsbig_d[0:1, rb:rb + 1024].partition_broadcast(128), writes=[rb_t])
            load_w(wb, N.w_ada_d[:, which * 1024:(which + 1) * 1024], 8)
            for b in range(NBL):
                CP(scR.t[:], scT.t[:, :, b:b + 1].to_broadcast([128, 8, 128]), [scT], [scR])
                gps = PS()
                for nb in range(2):
                    for dk in range(8):
                        MM(gps.t[:, nb * 512:(nb + 1) * 512], scR.t[:, dk, :], wb.t[:, dk, nb * 512:(nb + 1) * 512],
                           dk == 0, dk == 7, [wb, scR], [gps])
                TT(G.t[:, b, :], gps.t[:], rb_t.t[:], ALU.add, [gps, rb_t], [G])
            p.es = es
        p.barrier()

    def rstd_of(ssq_in, out, n, reads, writes):
        ACT(out, ssq_in, AF.Sqrt, reads, writes, bias=N.epsb.t[:, 0:1], scale=1.0 / n)
        p.op('dve', lambda e: e.reciprocal(out, out), writes, writes)

    if "P" in phases:
      with ExitStack() as esP:
        p.es = esP
        G1 = p.sbuf("G1", [128, NBL, 1024], F32)
        compute_G(G1, 2, RB_BA2)
        p.es = esP
        wzr = p.sbuf("wzr", [128, 8, 2048], BF16)
        wmg = p.sbuf("wmg", [128, 8, 2048], BF16)
        wbs = p.sbuf("wbs", [128, 8, 1024], BF16)
        wbg = p.sbuf("wbg", [128, 8, 1024], BF16)
        wo = p.sbuf("wo", [128, 8, 1024], BF16)
        for dk in range(8):
            p.dma('pool', wzr.t[:, dk, 0:1024], N.w_in_d[dk * 128:(dk + 1) * 128, 0:1024], writes=[wzr])
            p.dma('pool', wzr.t[:, dk, 1024:2048], N.w_in_d[dk * 128:(dk + 1) * 128, 5168:6192], writes=[wzr])
        load_w(wmg, N.w_merge_d, 8)
        load_w(wbs, N.w_brs_d, 8)
        load_w(wbg, N.w_brg_d, 8)
        load_w(wo, N.w_o_d, 8)
        for b in range(nseq):
            for q4 in range(4):
                p.dma('sp', x1_d[b, q4 * 512:(q4 + 1) * 512, :], x_d[b, q4 * 512:(q4 + 1) * 512, :],
                      dram_w=["x1_%d_%d" % (b, q4 * 512 + k * 128) for k in range(4)])
        TW = 256
        NS = TW // 128
        hT4s = [p.sbuf("hT4_%d" % i, [128, 8, TW], BF16) for i in range(2)]
        yT4s = [p.sbuf("yT4_%d" % i, [128, 8, TW], BF16) for i in range(2)]
        oT4s = [p.sbuf("oT4_%d" % i, [128, 8, TW], BF16) for i in range(2)]
        mT4s = [p.sbuf("mT4_%d" % i, [128, 8, TW], BF16) for i in range(2)]
        gT_r = Rot(p, "gT", [128, 2, TW], BF16, 2)
        m12_r = Rot(p, "m12", [128, 2 * TW], F32, 1)
        zrs = [p.sbuf("zr%d" % i, [128, 2048], BF16) for i in range(2)]
        yots = [p.sbuf("yot%d" % i, [128, 2048], F32) for i in range(2)]
        scr8 = p.sbuf("scr8p", [128, 2048], F32)
        tbuf = p.sbuf("tbuf", [128, 1024], F32)
        sq = p.sbuf("sqp", [128, 1024], BF16)
        sq2 = sq
        ssq = p.sbuf("ssqp", [128, 4], F32)
        so = p.sbuf("so", [128, 8], F32)

        def B1fn(j, r):
            return modT.t[:, j, r:r + 1]

        ntile = dbg.get("nt4", SEQ // TW)
        tiles = [(b, t) for b in range(nseq) for t in range(ntile)]

        mhalf = p.sbuf("mhalf", [128, 4], F32)
        MEMSET(mhalf.t[:], -0.5, [mhalf])
        sig_r = Rot(p, "sig", [128, 512], BF16, 2)

        def rstd_pow(ssq_ap, out_ap, n, res):
            ncol = ssq_ap.shape[-1]
            TS(out_ap, ssq_ap, 1.0 / n, EPS, ALU.mult, ALU.add, [res], [res])
            TT(out_ap, out_ap, mhalf.t[:, 0:ncol], ALU.pow, [res, mhalf], [res], eng='pool')

        def PA1a(ti, s):
            b, t = tiles[ti]
            yot = yots[s]
            tok = t * TW + s * 128
            p.dma('sp', yot.t[:], yo_d[b, tok:tok + 128, :], writes=[yot],
                  dram_r=["yo%d_%d%s" % (b, tok + o_, s_) for o_ in (0, 64) for s_ in ("y", "o")])

        def PA1b1(ti, s):
            b, t = tiles[ti]
            hT4 = hT4s[ti % 2]
            tok = t * TW + s * 128
            p.dma('sp', hT4.t[:, :, s * 128:(s + 1) * 128], N.hts_d[b, :, :, tok:tok + 128], writes=[hT4],
                  dram_r=["hts%d_%d" % (b, tok)])

        def PA1b2(ti, s):
            hT4 = hT4s[ti % 2]
            zr = zrs[s]
            for nb in range(4):
                ps = PS()
                for dk in range(8):
                    MM(ps.t[:, 0:512], hT4.t[:, dk, s * 128:(s + 1) * 128], wzr.t[:, dk, nb * 512:(nb + 1) * 512],
                       dk == 0, dk == 7, [hT4, wzr], [ps])
                sig = sig_r.next()
                ACT(sig.t[:], ps.t[:, 0:512], AF.Sigmoid, [ps], [sig])
                TT(zr.t[:, nb * 512:(nb + 1) * 512], ps.t[:, 0:512], sig.t[:], ALU.mult, [ps, sig], [zr])

        def PA2a(ti, s):
            zr, yot = zrs[s], yots[s]
            TT(yot.t[:, 0:1024], yot.t[:, 0:1024], zr.t[:, 0:1024], ALU.mult, [yot, zr], [yot])
            TT(sq2.t[:], yot.t[:, 0:1024], yot.t[:, 0:1024], ALU.mult, [yot], [sq2])
            p.op('dve', lambda e: e.reduce_sum(so.t[:, 0:1], sq2.t[:], AX.X), [sq2], [so])
            rstd_pow(so.t[:, 0:1], so.t[:, 0:1], 1024.0, so)
            TS(yot.t[:, 0:1024], yot.t[:, 0:1024], so.t[:, 0:1], None, ALU.mult, None, [yot, so], [yot])
            TT(sq2.t[:], yot.t[:, 1024:2048], yot.t[:, 1024:2048], ALU.mult, [yot], [sq2], eng='pool')
            p.op('dve', lambda e: e.reduce_sum(so.t[:, 4:8], sq2.t[:].rearrange("p (h v) -> p h v", h=4), AX.X), [sq2], [so])
            rstd_pow(so.t[:, 4:8], so.t[:, 4:8], 256.0, so)
            TT(yot.t[:, 1024:2048].rearrange("p (h v) -> p h v", h=4), yot.t[:, 1024:2048].rearrange("p (h v) -> p h v", h=4),
               so.t[:, 4:8].unsqueeze(2).to_broadcast([128, 4, 256]), ALU.mult, [yot, so], [yot])
            TT(yot.t[:, 1024:2048], yot.t[:, 1024:2048], zr.t[:, 1024:2048], ALU.mult, [yot, zr], [yot], eng='pool')

        def PA2b(ti, s):
            yT4, oT4 = yT4s[ti % 2], oT4s[ti % 2]
            yot = yots[s]
            for (half, dstT, vg) in ((0, yT4, V_SNG), (1, oT4, V_GNG)):
                ps = PS()
                for j in range(8):
                    TR(ps.t[:, j * 128:(j + 1) * 128], yot.t[:, half * 1024 + j * 128:half * 1024 + (j + 1) * 128],
                       identf, [yot, consts], [ps])
                if half == 0:
                    for j in range(8):
                        ACT(dstT.t[:, j, s * 128:(s + 1) * 128], ps.t[:, j * 128:(j + 1) * 128], AF.Identity,
                            [ps, vecs], [dstT], scale=vecs.t[:, vg + j:vg + j + 1])
                else:
                    TT(dstT.t[:, :, s * 128:(s + 1) * 128], ps.t[:].rearrange("p (j t) -> p j t", j=8),
                       vecs.t[:, vg:vg + 8].unsqueeze(2).to_broadcast([128, 8, 128]), ALU.mult, [ps, vecs], [dstT])

        def Bstep(ti, jc):
            hT4, yT4, oT4, mT4 = hT4s[ti % 2], yT4s[ti % 2], oT4s[ti % 2], mT4s[ti % 2]
            gT, m12 = gT_r.next(), m12_r.next()
            ps = PS()
            for gi in range(2):
                gc = gi * 8 + jc
                for dk in range(8):
                    MM(ps.t[:, gi * 512:gi * 512 + TW], wmg.t[:, dk, gc * 128:(gc + 1) * 128], hT4.t[:, dk, :], dk == 0, dk == 7, [wmg, hT4], [ps])
                ACT(gT.t[:, gi, :], ps.t[:, gi * 512:gi * 512 + TW], AF.Sigmoid, [ps, vecs], [gT], bias=vecs.t[:, V_BM + gc:V_BM + gc + 1])
            ps = PS()
            for dk in range(8):
                MM(ps.t[:, 0:TW], wbs.t[:, dk, jc * 128:(jc + 1) * 128], yT4.t[:, dk, :], dk == 0, dk == 7, [wbs, yT4], [ps])
            for dk in range(8):
                MM(ps.t[:, 512:512 + TW], wbg.t[:, dk, jc * 128:(jc + 1) * 128], oT4.t[:, dk, :], dk == 0, dk == 7, [wbg, oT4], [ps])
            TT(m12.t[:].rearrange("p (g t) -> p g t", g=2), ps.t[:].rearrange("p (g t) -> p g t", g=2)[:, :, 0:TW], gT.t[:], ALU.mult, [ps, gT], [m12])
            TT(mT4.t[:, jc, :], m12.t[:, 0:TW], m12.t[:, TW:2 * TW], ALU.add, [m12], [mT4], eng='pool')

        def Cstep(ti, s):
            b, t = tiles[ti]
            mT4 = mT4s[ti % 2]
            tok = t * TW + s * 128
            ps = PS()
            for nb in range(2):
                for dk in range(8):
                    MM(ps.t[:, nb * 512:(nb + 1) * 512], mT4.t[:, dk, s * 128:(s + 1) * 128], wo.t[:, dk, nb * 512:(nb + 1) * 512],
                       dk == 0, dk == 7, [mT4, wo], [ps])
            TT(tbuf.t[:], ps.t[:], G1.t[:, b, :], ALU.mult, [ps, G1], [tbuf])
            nm = "x1_%d_%d" % (b, tok)
            p.dma('pool', x1_d[b, tok:tok + 128, :], tbuf.t[:], reads=[tbuf], dram_r=[nm], dram_w=[nm], accum=ALU.add)

        A1m = N.A1
        for s in range(NS):
            PA1a(0, s)
            PA1b1(0, s)
            PA1b2(0, s)
        for s in range(NS):
            PA2a(0, s)
            PA2b(0, s)
        sched = {0: [(PA1a, 0)], 1: [(PA1b1, 0), (PA1a, 1)], 2: [(PA1b2, 0), (PA1b1, 1)], 3: [(PA2a, 0)],
                 4: [(PA1b2, 1)], 5: [(PA2b, 0), (PA2a, 1)], 7: [(PA2b, 1)]}
        for ti in range(len(tiles)):
            nxt = ti + 1 < len(tiles)
            for jc in range(8):
                Bstep(ti, jc)
                if nxt:
                    for (fn, s) in sched.get(jc, []):
                        fn(ti + 1, s)
            for s in range(NS):
                Cstep(ti, s)
        p.es = es
      p.barrier()

    if "F" in phases:
      with ExitStack() as esF:
        p.es = esF
        G2 = p.sbuf("G2", [128, NBL, 1024], F32)
        compute_G(G2, 5, RB_BA5)
        p.es = esF
        fng = p.sbuf("fng", [128, 1024], F32)
        p.dma('sp', fng.t[:], N.rowsbig_d[0:1, RB_FNG:RB_FNG + 1024].partition_broadcast(128), writes=[fng])
        wdn = p.sbuf("wdn", [128, 22, 1024], BF16)
        load_w(wdn, N.w_down_d, 22)
        wupr = Rot(p, "wup", [128, 8, 256], BF16, 3)
        h2T = p.sbuf("h2T", [128, 8, 1152], BF16)
        aT = p.sbuf("aT", [128, 22, 1024], BF16)
        scr8 = p.sbuf("scr8f", [128, 2048], F32)
        sq = p.sbuf("sqf", [128, 1024], BF16)
        ssq = p.sbuf("ssqf", [128, 4], F32)
        usb_r = Rot(p, "usb", [128, 17 * 66], BF16, 2)
        for _u in usb_r.b:
            MEMSET(_u.t[:], 0.0, [_u])
        dg_r = Rot(p, "dg", [128, 9, 128], BF16, 2)
        sg_r = Rot(p, "sg", [128, 1024], F32, 2)
        identb = N.identb

        def B2fn(j, r):
            return modT.t[:, 24 + j, r:r + 1]

        nj = dbg.get("nj", 22)
        wsrc = N.w_up_d.rearrange("(dk p) n -> p dk n", p=128)
        for b in range(nseq):
            for hf in range(dbg.get("nhalf", 2)):
                base = 0 if hf == 0 else 896
                for i in range(9):
                    tok = base + i * 128
                    xt = scr8.t[:, 0:1024]
                    p.dma('sp', xt, x1_d[b, tok:tok + 128, :], writes=[scr8], dram_r=["x1_%d_%d" % (b, tok)])
                    N.norm_to_T(xt, scr8, A2, B2fn, b, h2T, i * 128, (sq.t[:], sq, ssq, scr8.t[:, 1024:2048], scr8))
                if dbg.get("dump_h2T") and b == 0 and hf == 0:
                    dump("h2T", h2T, h2T.t[:], [128, 8, 1152])
                m0 = 0 if hf == 0 else 128
                h0 = 1024 if hf == 0 else 64
                off = 0 if hf == 0 else 1
                hrow = 16 if hf == 0 else 0
                for j in range(nj):
                    wu = wupr.next()
                    p.dma('pool', wu.t[:, :, 0:128], wsrc[:, :, j * 128:(j + 1) * 128], writes=[wu])
                    p.dma('pool', wu.t[:, :, 128:256], wsrc[:, :, DFF + j * 128:DFF + (j + 1) * 128], writes=[wu])
                    pcs = []
                    for part in range(2):
                        ch = part * 22 + j
                        pm = PS()
                        pc = PS()
                        for nb in range(2):
                            for dk in range(8):
                                MM(pm.t[:, nb * 512:(nb + 1) * 512], wu.t[:, dk, part * 128:(part + 1) * 128],
                                   h2T.t[:, dk, m0 + nb * 512:m0 + (nb + 1) * 512], dk == 0, dk == 7, [wu, h2T], [pm])
                        for dk in range(8):
                            MM(pc.t[:, 0:64], wu.t[:, dk, part * 128:(part + 1) * 128], h2T.t[:, dk, h0:h0 + 64],
                               dk == 0, dk == 7, [wu, h2T], [pc])
                        usb = usb_r.next()
                        u3 = usb.t[:].rearrange("p (r c) -> p r c", c=66)
                        CP(u3[:, off:off + 16, 1:65], pm.t[:].rearrange("p (r c) -> p r c", c=64), [pm], [usb], eng='act')
                        CP(u3[:, hrow, 1:65], pc.t[:, 0:64], [pc], [usb], eng='dve')
                        dg = dg_r.next()
                        wb_ = V_FCW + ch * 9
                        TT(dg.t[:], identb.t[:].unsqueeze(1).to_broadcast([128, 9, 128]),
                           vecs.t[:, wb_:wb_ + 9].unsqueeze(2).to_broadcast([128, 9, 128]), ALU.mult, [identb, vecs], [dg], eng='pool')
                        taps = [(0, 0)] + [(dr, dc) for dr in (-1, 0, 1) for dc in (-1, 0, 1) if (dr, dc) != (0, 0)]
                        for bank in range(2):
                            todo = []
                            for (dr, dc) in taps:
                                mlo = 1 if (hf == 0 and dr == -1) else 0
                                mhi = 15 if (hf == 1 and dr == 1) else 16
                                lo = max(mlo, bank * 8)
                                hi = min(mhi, bank * 8 + 8)
                                if hi <= lo:
                                    continue
                                todo.append((dr, dc, lo, hi))
                            for ti, (dr, dc, lo, hi) in enumerate(todo):
                                k = (dr + 1) * 3 + (dc + 1)
                                MM(pc.t[:, lo * 64:hi * 64], dg.t[:, k, :], u3[:, lo + off + dr:hi + off + dr, 1 + dc:65 + dc],
                                   ti == 0, ti == len(todo) - 1, [dg, usb], [pc])
                        pcs.append((pc, ch))
                    sg = sg_r.next()
                    (pcg, chg), (pcv, chv) = pcs
                    ACT(sg.t[:], pcg.t[:], AF.Silu, [pcg, vecs], [sg], bias=vecs.t[:, V_FCB + chg:V_FCB + chg + 1])
                    STT(aT.t[:, j, :], pcv.t[:], vecs.t[:, V_FCB + chv:V_FCB + chv + 1], sg.t[:], ALU.add, ALU.mult, [pcv, vecs, sg], [aT])
                for s in range(8):
                    tok = hf * 1024 + s * 128
                    ps = PS()
                    for nb in range(2):
                        for j in range(nj):
                            MM(ps.t[:, nb * 512:(nb + 1) * 512], aT.t[:, j, s * 128:(s + 1) * 128], wdn.t[:, j, nb * 512:(nb + 1) * 512],
                               j == 0, j == nj - 1, [aT, wdn], [ps])
                    xt = scr8.t[:, 0:1024]
                    x2 = scr8.t[:, 1024:2048]
                    p.dma('sp', xt, x1_d[b, tok:tok + 128, :], writes=[scr8], dram_r=["x1_%d_%d" % (b, tok)])
                    TT(x2, ps.t[:], G2.t[:, b, :], ALU.mult, [ps, G2], [scr8])
                    TT(x2, x2, xt, ALU.add, [scr8], [scr8])
                    ACT(sq.t[:], x2, AF.Square, [scr8], [sq])
                    p.op('dve', lambda e: e.reduce_sum(ssq.t[:, 0:1], sq.t[:], AX.X), [sq], [ssq])
                    rstd_of(ssq.t[:, 0:1], ssq.t[:, 1:2], 1024.0, [ssq], [ssq])
                    TS(x2, x2, ssq.t[:, 1:2], None, ALU.mult, None, [scr8, ssq], [scr8])
                    TT(xt, x2, fng.t[:], ALU.mult, [scr8, fng], [scr8])
                    p.dma('sp', out_d[b, tok:tok + 128, :], xt, reads=[scr8], final=True)
        p.es = es
      p.barrier()


_CACHE = {}


def kernel(**inputs):
    inp = {k: np.asarray(v) for k, v in inputs.items()}
    if "nc" not in _CACHE:
        _CACHE["nc"] = build_program()[0]
    nc = _CACHE["nc"]
    sh = prep_shared(inp)
    shared = dict(w_ada=inp["w_ada"][0], w_in=inp["w_in"][0], w_merge=inp["w_merge"][0], w_br_ssd=inp["w_br_ssd"][0],
                  w_br_gla=inp["w_br_gla"][0], w_o=inp["w_o"][0], w_up=inp["w_up"][0], w_down=inp["w_down"][0])
    shared.update(sh)
    shared = {k: np.ascontiguousarray(np.asarray(v, np.float32)) for k, v in shared.items()}
    in_maps = []
    for core in range(8):
        b0 = core * NBL
        cT = np.stack([inp["c"][b0], inp["c"][b0 + 1], inp["c_ctx"]], axis=1).astype(np.float32)
        cT = np.ascontiguousarray(cT.reshape(8, 128, 3).transpose(1, 0, 2))
        m = dict(shared)
        m["x"] = np.ascontiguousarray(inp["x"][b0:b0 + NBL], dtype=np.float32)
        m["ctx"] = np.ascontiguousarray(inp["ctx"][b0:b0 + NBL], dtype=np.float32)
        m["cT"] = cT
        in_maps.append(m)
    res = run_bass_kernel_spmd(nc, in_maps, core_ids=list(range(8)))
    out = np.concatenate([np.asarray(r["out"]) for r in res.results], axis=0)
    return out.astype(np.float32)
```

```python
import numpy as np
import concourse.bass as bass
import concourse.mybir as mybir
from concourse.bass_utils import run_bass_kernel_spmd
from contextlib import ExitStack

F32 = mybir.dt.float32
BF16 = mybir.dt.bfloat16
AF = mybir.ActivationFunctionType
ALU = mybir.AluOpType
AX = mybir.AxisListType


class Res:
    def __init__(self, name, t=None):
        self.name = name
        self.t = t
        self.lw = {}
        self.rd = {}


class Prog:
    NDMA = 8

    def __init__(self, nc, es):
        self.nc = nc
        self.es = es
        self.names = ['pe', 'act', 'dve', 'pool', 'sp']
        self.E = {'pe': nc.tensor, 'act': nc.scalar, 'dve': nc.vector, 'pool': nc.gpsimd, 'sp': nc.sync}
        self.cnt = {k: 0 for k in self.names}
        self.h = {}
        for k in self.names:
            self.h[('e', k)] = es.enter_context(nc.semaphore("s_" + k))
        self.dq = ('sp', 'pool', 'act')
        self.dcnt = {q: 0 for q in self.dq}
        self.dval = {}
        for q in self.dq:
            for i in range(self.NDMA):
                self.h[('d', q, i)] = es.enter_context(nc.semaphore("d_%s%d" % (q, i)))
                self.dval[('d', q, i)] = 0
        self.seen = {k: {} for k in self.names}
        self.dram = {}
        self.final = []
        self.nwait = 0

    def sbuf(self, name, shape, dtype):
        self.uid = getattr(self, "uid", 0) + 1
        t = self.es.enter_context(self.nc.sbuf_tensor("sb%d_%s" % (self.uid, name), list(shape), dtype))
        return Res(name, t)

    def psum(self, name, shape, dtype):
        t = self.es.enter_context(self.nc.psum_tensor("ps_" + name, list(shape), dtype))
        return Res(name, t)

    def _dres(self, name):
        if name not in self.dram:
            self.dram[name] = Res(name)
        return self.dram[name]

    def _collect(self, eng, reads, writes):
        need = {}

        def add(tok):
            if tok is None:
                return
            k, v = tok
            if need.get(k, 0) < v:
                need[k] = v

        for r in reads:
            for k, v in r.lw.items():
                add((k, v))
        for w in writes:
            for k, v in w.lw.items():
                add((k, v))
            for k, v in w.rd.items():
                add((k, v))
        out = []
        for k, v in need.items():
            if eng == 'pe' and k == ('e', 'pe'):
                continue
            if self.seen[eng].get(k, 0) >= v:
                continue
            self.seen[eng][k] = v
            out.append((k, v))
        return out

    def _mark(self, tok, reads, writes):
        k, v = tok
        for r in reads:
            if r.rd.get(k, 0) < v:
                r.rd[k] = v
        for w in writes:
            if w.lw.get(k, 0) < v:
                w.lw[k] = v
            w.rd = {}

    def op(self, eng, fn, reads=(), writes=()):
        waits = self._collect(eng, reads, writes)
        self.cnt[eng] += 1
        sem = self.h[('e', eng)]
        hs = [(self.h[k], v) for k, v in waits]
        self.nwait += len(hs)

        e = self.E[eng]
        for hh, v in hs:
            e.wait_ge(hh, v)
        fn(e).then_inc(sem, 1)
        self._mark((('e', eng), self.cnt[eng]), reads, writes)

    def dma(self, q, out_ap, in_ap, reads=(), writes=(), dram_r=(), dram_w=(), final=False, accum=None):
        reads = list(reads) + [self._dres(n) for n in dram_r]
        writes = list(writes) + [self._dres(n) for n in dram_w]
        i = self.dcnt[q] % self.NDMA
        self.dcnt[q] += 1
        key = ('d', q, i)
        prev = self.dval[key]
        waits = self._collect(q, reads, writes)
        if prev > 0 and self.seen[q].get(key, 0) < prev:
            self.seen[q][key] = prev
            waits.append((key, prev))
        self.dval[key] = prev + 16
        sem = self.h[key]
        hs = [(self.h[k], v) for k, v in waits]
        self.nwait += len(hs)

        e = self.E[q]
        for hh, v in hs:
            e.wait_ge(hh, v)
        if accum is not None:
            e.dma_start(out=out_ap, in_=in_ap, accum_op=accum).then_inc(sem, 16)
        else:
            e.dma_start(out=out_ap, in_=in_ap).then_inc(sem, 16)
        tok = (key, prev + 16)
        self._mark(tok, reads, writes)
        if final:
            self.final.append(tok)

    def barrier(self):
        toks = [(('e', k), self.cnt[k]) for k in self.names if self.cnt[k] > 0]
        toks += [(k, v) for k, v in self.dval.items() if v > 0]
        for eng in self.names:
            e = self.E[eng]
            for k, v in toks:
                if eng == 'pe' and k == ('e', 'pe'):
                    continue
                if self.seen[eng].get(k, 0) >= v:
                    continue
                self.seen[eng][k] = v
                e.wait_ge(self.h[k], v)

    def finish(self):
        fin = {}
        for k, v in self.final:
            fin[k] = max(fin.get(k, 0), v)
        hs = [(self.h[k], v) for k, v in fin.items()]

        e = self.E['sp']
        for hh, v in hs:
            e.wait_ge(hh, v)


D = 1024
SEQ = 2048
CTXL = 256
NTOK = CTXL + SEQ
NBL = 2
DFF = 2816
EPS = 1e-6
WM0, WMN = 1024, 4144
O_XBC, O_DT, O_Q, O_K, O_V, O_G = 0, 2048, 2080, 2592, 3104, 4128
V_N1G, V_N2G, V_BADA, V_CW, V_CB, V_BM, V_FCW, V_FCB, V_SNG, V_GNG, NV = 0, 8, 16, 64, 112, 128, 144, 540, 584, 592, 600
R_DTB, R_ALOG, R_DSK, NR = 0, 32, 64, 80
RB_FNG, RB_BA2, RB_BA5, NRB = 0, 1024, 2048, 3072
K_ID, K_ONE, K_L, K_U, K_NM, K_V, NCN = 0, 128, 256, 384, 512, 640, 768


def host_consts():
    c = np.zeros((128, NCN), np.float32)
    c[:, K_ID:K_ID + 128] = np.eye(128, dtype=np.float32)
    c[:, K_ONE:K_ONE + 128] = 1.0
    t = np.arange(64)[:, None]
    i = np.arange(64)[None, :]
    c[0:64, K_L:K_L + 64] = (t <= i)
    c[0:64, K_L + 64:K_L + 128] = (t >= i)
    c[0:64, K_U:K_U + 64] = (t > i)
    c[0:64, K_U + 64:K_U + 128] = (t < i)
    c[0:64, K_NM:K_NM + 64] = np.where(t <= i, 0.0, -30000.0)
    c[0:64, K_NM + 64:K_NM + 128] = np.where(t >= i, 0.0, -30000.0)
    c[0:64, K_V:K_V + 64] = (t <= i)
    c[0:64, K_V + 64:K_V + 128] = (t >= i)
    return c


def fm(v):
    v = np.asarray(v, np.float32)
    return np.ascontiguousarray(v.reshape(-1, 128).T)


def prep_shared(inp):
    vecs = np.zeros((128, NV), np.float32)
    vecs[:, V_N1G:V_N1G + 8] = fm(inp["norm1_g"][0])
    vecs[:, V_N2G:V_N2G + 8] = fm(inp["norm2_g"][0])
    vecs[:, V_BADA:V_BADA + 48] = fm(inp["b_ada"][0])
    cw = inp["ssd_conv_w"][0]
    for k in range(3):
        vecs[:, V_CW + k:V_CW + 48:3] = fm(cw[k])
    vecs[:, V_CB:V_CB + 16] = fm(inp["ssd_conv_b"][0])
    vecs[:, V_BM:V_BM + 16] = fm(inp["b_merge"][0])
    fw = inp["ffn_conv_w"][0].reshape(9, -1)
    for k in range(9):
        vecs[:, V_FCW + k:V_FCW + 396:9] = fm(fw[k])
    vecs[:, V_FCB:V_FCB + 44] = fm(inp["ffn_conv_b"][0])
    vecs[:, V_SNG:V_SNG + 8] = fm(inp["ssd_norm_g"][0])
    vecs[:, V_GNG:V_GNG + 8] = np.tile(fm(inp["gla_norm_g"][0]), (1, 4))
    rows = np.zeros((1, NR), np.float32)
    rowsbig = np.zeros((1, NRB), np.float32)
    rowsbig[0, RB_FNG:RB_FNG + 1024] = inp["final_norm_g"]
    rowsbig[0, RB_BA2:RB_BA2 + 1024] = inp["b_ada"][0][2048:3072]
    rowsbig[0, RB_BA5:RB_BA5 + 1024] = inp["b_ada"][0][5120:6144]
    rows[0, R_DTB:R_DTB + 32] = inp["ssd_dt_bias"][0].reshape(-1)
    rows[0, R_ALOG:R_ALOG + 32] = inp["ssd_a_log"][0].reshape(-1)
    rows[0, R_DSK:R_DSK + 16] = inp["ssd_d"][0]
    gw = np.zeros((17, 1024), np.float32)
    gw[0:16] = np.transpose(inp["gla_gate_w"][0], (1, 0, 2)).reshape(16, 1024)
    gw[16] = inp["gla_gate_b"][0].reshape(-1)
    return dict(vecs=vecs, rows=rows, rowsbig=rowsbig, gw=gw, consts=host_consts())


class Rot:
    def __init__(self, p, name, shape, dtype, n=2):
        self.b = [p.sbuf("%s%d" % (name, i), shape, dtype) for i in range(n)]
        self.i = 0

    def next(self):
        r = self.b[self.i % len(self.b)]
        self.i += 1
        return r


def build_program(dbg=None, nseq=NBL, phases="AMPF"):
    dbg = dbg or {}
    nc = bass.Bass("TRN2", target_bir_lowering=False)
    IN = "ExternalInput"

    def din(name, shape):
        return nc.dram_tensor(name, list(shape), F32, kind=IN).ap()

    x_d = din("x", [NBL, SEQ, D])
    ctx_d = din("ctx", [NBL, CTXL, D])
    cT_d = din("cT", [128, 8, 3])
    w_ada_d = din("w_ada", [D, 6 * D])
    w_in_d = din("w_in", [D, 6192])
    w_merge_d = din("w_merge", [D, 2 * D])
    w_brs_d = din("w_br_ssd", [D, D])
    w_brg_d = din("w_br_gla", [D, D])
    w_o_d = din("w_o", [D, D])
    w_up_d = din("w_up", [D, 2 * DFF])
    w_down_d = din("w_down", [DFF, D])
    gw_d = din("gw", [17, 1024])
    vecs_d = din("vecs", [128, NV])
    rows_d = din("rows", [1, NR])
    rowsbig_d = din("rowsbig", [1, NRB])
    consts_d = din("consts", [128, NCN])
    out_d = nc.dram_tensor("out", [NBL, SEQ, D], F32, kind="ExternalOutput").ap()
    yo_kind = "ExternalOutput" if dbg.get("dump_yo") else "Internal"
    yo_d = nc.dram_tensor("yo", [NBL, SEQ, 2 * D], F32, kind=yo_kind).ap()
    sx_d = nc.dram_tensor("sx", [NBL, 9, 128, 16 * 256], BF16, kind="Internal").ap()
    sqk_d = nc.dram_tensor("sqk", [NBL, 9, 128, 8 * 256], BF16, kind="Internal").ap()
    sg_d = nc.dram_tensor("sg", [NBL, 9, 32, 256], BF16, kind="Internal").ap()
    hts_d = nc.dram_tensor("hts", [NBL, 128, 8, SEQ], BF16, kind="Internal").ap()
    x1_kind = "ExternalOutput" if dbg.get("dump_x1") else "Internal"
    x1_d = nc.dram_tensor("x1", [NBL, SEQ, D], F32, kind=x1_kind).ap()
    dbg_outs = {}

    es = ExitStack()
    with es:
        p = Prog(nc, es)

        def MM(out, lhsT, rhs, start, stop, reads, writes):
            p.op('pe', lambda e: e.matmul(out, lhsT, rhs, start=start, stop=stop), reads, writes)

        def TR(out, in_, ident, reads, writes):
            p.op('pe', lambda e: e.transpose(out, in_, ident), reads, writes)

        def ACT(out, in_, func, reads, writes, bias=None, scale=None):
            kw = {}
            if bias is not None:
                kw['bias'] = bias
            if scale is not None:
                kw['scale'] = scale
            p.op('act', lambda e: e.activation(out, in_, func, **kw), reads, writes)

        def TT(out, in0, in1, op, reads, writes, eng='dve'):
            p.op(eng, lambda e: e.tensor_tensor(out, in0, in1, op), reads, writes)

        def TS(out, in0, s1, s2, op0, op1, reads, writes, eng='dve'):
            if s2 is None:
                p.op(eng, lambda e: e.tensor_scalar(out, in0, s1, None, op0), reads, writes)
            else:
                p.op(eng, lambda e: e.tensor_scalar(out, in0, s1, s2, op0, op1), reads, writes)

        def STT(out, in0, sc, in1, op0, op1, reads, writes):
            p.op('dve', lambda e: e.scalar_tensor_tensor(out, in0, sc, in1, op0, op1), reads, writes)

        def CP(out, in_, reads, writes, eng='dve'):
            if eng == 'act':
                p.op('act', lambda e: e.activation(out, in_, AF.Identity), reads, writes)
            else:
                p.op(eng, lambda e: e.tensor_copy(out, in_), reads, writes)

        def MEMSET(ap, val, writes, eng='dve'):
            p.op(eng, lambda e: e.memset(ap, val), (), writes)

        def dump(name, res, ap, shape):
            t = nc.dram_tensor("dbg_" + name, list(shape), ap.dtype, kind="ExternalOutput").ap()
            p.dma('sp', t, ap, reads=[res], final=True)
            dbg_outs[name] = t

        pp = [p.psum("pp%d" % i, [128, 1024], F32) for i in range(4)]
        pidx = [0]

        def PS():
            r = pp[pidx[0] % 4]
            pidx[0] += 1
            return r

        consts = p.sbuf("consts", [128, NCN], F32)
        vecs = p.sbuf("vecs", [128, NV], F32)
        rowsb = p.sbuf("rowsb", [128, NR], F32)
        gw = p.sbuf("gw", [17, 1024], BF16)
        constb = p.sbuf("constb", [128, 384], BF16)
        p.dma('sp', consts.t[:], consts_d, writes=[consts])
        p.dma('sp', vecs.t[:], vecs_d, writes=[vecs])
        p.dma('sp', rowsb.t[:], rows_d.partition_broadcast(128), writes=[rowsb])
        p.dma('pool', gw.t[:], gw_d, writes=[gw])
        identf = consts.t[:, K_ID:K_ID + 128]
        onesf = consts.t[:, K_ONE:K_ONE + 128]

        def Lm(d):
            return consts.t[0:64, K_L + 64 * d:K_L + 64 * d + 64]

        def Um(d):
            return consts.t[0:64, K_U + 64 * d:K_U + 64 * d + 64]

        def NMm(d):
            return consts.t[0:64, K_NM + 64 * d:K_NM + 64 * d + 64]

        def Vm(d):
            return consts.t[0:64, K_V + 64 * d:K_V + 64 * d + 64]

        identb = p.sbuf("identb", [128, 128], BF16)
        CP(identb.t[:], identf, [consts], [identb])
        CP(constb.t[:], consts.t[:, K_ONE:K_ONE + 384], [consts], [constb])

        def Lb(d):
            return constb.t[0:64, 128 + 64 * d:128 + 64 * d + 64]

        def Ub(d):
            return constb.t[0:64, 256 + 64 * d:256 + 64 * d + 64]
        aneg = p.sbuf("aneg", [64, 32], F32)
        ACT(aneg.t[:], rowsb.t[0:64, R_ALOG:R_ALOG + 32], AF.Exp, [rowsb], [aneg])
        TS(aneg.t[:], aneg.t[:], -1.0, None, ALU.mult, None, [aneg], [aneg])
        dkd = p.sbuf("dkd", [64, 1024], BF16)
        TT(dkd.t[:].rearrange("p (h q) -> p h q", h=16),
           consts.t[0:64, K_ID:K_ID + 64].unsqueeze(1).to_broadcast([64, 16, 64]),
           rowsb.t[0:64, R_DSK:R_DSK + 16].unsqueeze(2).to_broadcast([64, 16, 64]),
           ALU.mult, [consts, rowsb], [dkd])

        modT = p.sbuf("modT", [128, 48, 3], F32)
        scT = p.sbuf("scT", [128, 8, 3], BF16)
        A1 = p.sbuf("A1", [128, 8, 3], F32)
        A2 = p.sbuf("A2", [128, 8, 3], F32)

        with ExitStack() as esA:
            p.es = esA
            cT = p.sbuf("cT", [128, 8, 3], F32)
            p.dma('sp', cT.t[:], cT_d, writes=[cT])
            ACT(scT.t[:], cT.t[:], AF.Silu, [cT], [scT])
            wrot = Rot(p, "wada", [128, 8, 512], BF16, 2)
            mps = PS()
            for nb in range(12):
                wb = wrot.next()
                for dk in range(8):
                    p.dma('pool', wb.t[:, dk, :], w_ada_d[dk * 128:(dk + 1) * 128, nb * 512:(nb + 1) * 512], writes=[wb])
                for cc in range(4):
                    j = nb * 4 + cc
                    for dk in range(8):
                        MM(mps.t[:, j * 4:j * 4 + 3], wb.t[:, dk, cc * 128:(cc + 1) * 128], scT.t[:, dk, :],
                           dk == 0, dk == 7, [wb, scT], [mps])
            TT(modT.t[:], mps.t[:, 0:192].rearrange("p (j r) -> p j r", r=4)[:, :, 0:3],
               vecs.t[:, V_BADA:V_BADA + 48].unsqueeze(2).to_broadcast([128, 48, 3]), ALU.add, [mps, vecs], [modT])
            for (A, vg, so) in ((A1, V_N1G, 8), (A2, V_N2G, 32)):
                TS(A.t[:], modT.t[:, so:so + 8, :], 1.0, None, ALU.add, None, [modT], [A])
                TT(A.t[:], A.t[:], vecs.t[:, vg:vg + 8].unsqueeze(2).to_broadcast([128, 8, 3]), ALU.mult, [A, vecs], [A])
            p.es = es
        p.barrier()
        if dbg.get("dump_mod"):
            dump("modT", modT, modT.t[:], [128, 48, 3])
            dump("A1", A1, A1.t[:], [128, 8, 3])

        def norm_to_T(xt, xtr, A, Bap_fn, r, dst, dst_tok0, tmp):
            sq, sqr, ssq, xn, xnr = tmp
            ACT(sq, xt, AF.Square, [xtr], [sqr])
            p.op('dve', lambda e: e.reduce_sum(ssq.t[:, 0:1], sq, AX.X), [sqr], [ssq])
            ACT(ssq.t[:, 1:2], ssq.t[:, 0:1], AF.Sqrt, [ssq], [ssq], bias=EPS_AP[0], scale=1.0 / D)
            p.op('dve', lambda e: e.reciprocal(ssq.t[:, 2:3], ssq.t[:, 1:2]), [ssq], [ssq])
            TS(xn, xt, ssq.t[:, 2:3], None, ALU.mult, None, [xtr, ssq], [xnr])
            ps = PS()
            for j in range(8):
                TR(ps.t[:, j * 128:(j + 1) * 128], xn[:, j * 128:(j + 1) * 128], identf, [xnr, consts], [ps])
            for j in range(8):
                if j % 2 == 0:
                    TS(dst.t[:, j, dst_tok0:dst_tok0 + 128], ps.t[:, j * 128:(j + 1) * 128],
                       A.t[:, j, r:r + 1], Bap_fn(j, r), ALU.mult, ALU.add, [ps, A, modT], [dst])
                else:
                    ACT(dst.t[:, j, dst_tok0:dst_tok0 + 128], ps.t[:, j * 128:(j + 1) * 128], AF.Identity,
                        [ps, A, modT], [dst], bias=Bap_fn(j, r), scale=A.t[:, j, r:r + 1])

        epsb = p.sbuf("epsb", [128, 1], F32)
        MEMSET(epsb.t[:], EPS, [epsb])
        EPS_AP = [epsb.t[:, 0:1]]

        if "M" in phases:
          with ExitStack() as esM:
            p.es = esM
            wmix = p.sbuf("wmix", [128, 8, WMN], BF16)
            nmrep = p.sbuf("nmrep", [64, 2, 1024], BF16)
            for d_ in range(2):
                CP(nmrep.t[:, d_, :].rearrange("p (h q) -> p h q", h=16),
                   consts.t[0:64, K_NM + 64 * d_:K_NM + 64 * d_ + 64].unsqueeze(1).to_broadcast([64, 16, 64]), [consts], [nmrep])

            for dk in range(8):
                p.dma('pool', wmix.t[:, dk, :], w_in_d[dk * 128:(dk + 1) * 128, WM0:WM0 + WMN], writes=[wmix])
            hT = p.sbuf("hT", [128, 8, NTOK], BF16)
            scr8 = p.sbuf("scr8", [128, 2048], F32)
            sq = p.sbuf("sq", [128, 1024], BF16)
            ssq = p.sbuf("ssq", [128, 4], F32)
            Hs = p.sbuf("Hs", [128, 1024], F32)
            Hb = p.sbuf("Hb", [128, 1024], BF16)
            Ss = p.sbuf("Ss", [128, 1024], F32)
            Sb = p.sbuf("Sb", [128, 1024], BF16)
            xbcT = p.sbuf("xbcT", [128, 16, 256], BF16)
            qkT = p.sbuf("qkT", [128, 8, 256], BF16)
            glrT = p.sbuf("glrT", [32, 256], BF16)
            MEMSET(glrT.t[:], 1.0, [glrT])
            accr = Rot(p, "acc", [128, 256], F32, 2)
            dts_r = Rot(p, "dts", [64, 32], F32, 2)
            da_r = Rot(p, "da", [64, 16], F32, 2)
            cum_r = Rot(p, "cum_sb", [64, 16], F32, 2)
            dtw_r = Rot(p, "dtw", [64, 16], F32, 2)
            ecum_r = Rot(p, "ecum", [64, 16], F32, 2)
            eL_r = Rot(p, "eL", [128, 16], F32, 2)
            dahl_r = Rot(p, "dahl", [64, 32], BF16, 2)
            dtbhl = p.sbuf("dtbhl", [1, 64], BF16)
            CP(dtbhl.t[0:1, 0:32], rowsb.t[0:1, R_DTB:R_DTB + 32], [rowsb], [dtbhl])
            TT(dtbhl.t[0:1, 32:64], rowsb.t[0:1, R_DTB:R_DTB + 32], dtbhl.t[0:1, 0:32], ALU.subtract, [rowsb, dtbhl], [dtbhl])
            nlm = p.sbuf("nlm", [64, 2, 64], BF16)
            for d_ in range(2):
                mid_ = 31 if d_ == 0 else 32
                TS(nlm.t[:, d_, :], consts.t[0:64, K_L + 64 * d_ + mid_:K_L + 64 * d_ + mid_ + 1].to_broadcast([64, 64]),
                   -1.0, None, ALU.mult, None, [consts], [nlm])
            ndahl_r = Rot(p, "ndahl", [64, 32], BF16, 2)
            seg = p.sbuf("seg", [64, 1024], F32)
            MTt_r = Rot(p, "MTt", [64, 1024], BF16, 2)
            cmc_r = Rot(p, "cmc", [128, 256], BF16, 2)
            xsb_r = Rot(p, "xsb", [64, 1536], BF16, 2)
            xd_r = Rot(p, "xd", [64, 1024], BF16, 2)
            xdw_r = Rot(p, "xdw", [64, 1024], BF16, 2)
            ybufs = [Res("yb0", scr8.t[0:64, 0:1024]), Res("yb1", scr8.t[0:64, 1024:2048])]
            ycnt = [0]
            obuf_r = Rot(p, "obuf", [64, 1024], F32, 2)
            la_r = Rot(p, "la", [64, 512], F32, 1)
            lah_r = Rot(p, "lah", [64, 512], BF16, 1)
            lal_r = Rot(p, "lal", [64, 512], BF16, 1)
            ref = p.sbuf("ref", [128, 4], F32)
            dl = p.sbuf("dl", [128, 256], F32)
            eq = p.sbuf("eq", [128, 256], F32)
            ek = p.sbuf("ek", [128, 256], F32)
            ec = p.sbuf("ec", [128, 256], F32)
            eT_r = Rot(p, "eT", [128, 4], F32, 2)
            erc = p.sbuf("erc", [64, 512], F32)
            vsb_r = Rot(p, "vsb", [64, 1024], BF16, 2)
            kdec_r = Rot(p, "kdec", [64, 512], BF16, 2)
            qdT_r = Rot(p, "qdT", [128, 256], BF16, 2)
            kdT_r = Rot(p, "kdT", [128, 256], BF16, 2)
            qeT_r = Rot(p, "qeT", [128, 256], BF16, 2)
            scTt = p.sbuf("scTt", [64, 256], BF16)

            def B1fn(j, r):
                return modT.t[:, j, r:r + 1]

            small_pssB = Res("pssB", pp[1].t[:, 0:512])
            small_pscB = Res("pscB", pp[1].t[:, 512:1024])
            chunk_par = [0]
            pj = [0]

            def PSJ():
                r = pp[(0, 2, 3)[pj[0] % 3]]
                pj[0] += 1
                return r

            def proj_super(tok0, lo, hi):
                T = 256
                a = max(tok0 - 1, lo)
                e_ = min(tok0 + T + 1, hi)
                n = e_ - a
                off = a - (tok0 - 1)
                for cc in range(16):
                    ps = PSJ()
                    for dk in range(8):
                        MM(ps.t[:, off:off + n], wmix.t[:, dk, O_XBC + cc * 128:O_XBC + (cc + 1) * 128],
                           hT.t[:, dk, a:e_], dk == 0, dk == 7, [wmix, hT], [ps])
                    acc = accr.next()
                    cwb = V_CW + cc * 3
                    ACT(acc.t[:], ps.t[:, 1:257], AF.Identity, [ps, vecs], [acc],
                        bias=vecs.t[:, V_CB + cc:V_CB + cc + 1], scale=vecs.t[:, cwb + 1:cwb + 2])
                    i0 = 1 if off == 1 else 0
                    STT(acc.t[:, i0:256], ps.t[:, i0:256], vecs.t[:, cwb:cwb + 1], acc.t[:, i0:256],
                        ALU.mult, ALU.add, [ps, vecs, acc], [acc])
                    i1 = 255 if e_ < tok0 + T + 1 else 256
                    STT(acc.t[:, 0:i1], ps.t[:, 2:2 + i1], vecs.t[:, cwb + 2:cwb + 3], acc.t[:, 0:i1],
                        ALU.mult, ALU.add, [ps, vecs, acc], [acc])
                    ACT(xbcT.t[:, cc, :], acc.t[:], AF.Silu, [acc], [xbcT])
                for j in range(8):
                    ps = PSJ()
                    for dk in range(8):
                        MM(ps.t[:, 0:256], wmix.t[:, dk, O_Q + j * 128:O_Q + (j + 1) * 128],
                           hT.t[:, dk, tok0:tok0 + 256], dk == 0, dk == 7, [wmix, hT], [ps])
                    ACT(qkT.t[:, j, :], ps.t[:, 0:256], AF.Identity, [ps], [qkT],
                        scale=(128.0 ** -0.5) if j < 4 else 1.0)
                ps = PSJ()
                for dk in range(8):
                    MM(ps.t[0:16, 0:256], wmix.t[:, dk, O_G:O_G + 16], hT.t[:, dk, tok0:tok0 + 256],
                       dk == 0, dk == 7, [wmix, hT], [ps])
                CP(glrT.t[0:16, :], ps.t[0:16, 0:256], [ps], [glrT])

            def chunk_head(b, d, t0, cl, is_ctx):
                dsl = slice(d * 16, d * 16 + 16)
                dts, da, cum_sb, dtw, ecum, eL = dts_r.next(), da_r.next(), cum_r.next(), dtw_r.next(), ecum_r.next(), eL_r.next()
                xsb, xd, xdw = xsb_r.next(), xd_r.next(), xdw_r.next()
                la, vsb, kdec = la_r.next(), vsb_r.next(), kdec_r.next()
                lah, lal, dahl = lah_r.next(), lal_r.next(), dahl_r.next()
                ybuf = obuf = None
                if not is_ctx:
                    ybuf = ybufs[ycnt[0] % 2]
                    ycnt[0] += 1
                    obuf = obuf_r.next()
                par = chunk_par[0] % 2
                chunk_par[0] += 1
                pss = small_pssB
                psc = small_pscB
                sps = small_pssB
                h16 = lambda ap: ap.rearrange("p (h q) -> p h q", h=16)
                h4 = lambda ap: ap.rearrange("p (h q) -> p h q", h=4)
                lps = pp[0]
                MM(lps.t[0:64, 0:512], glrT.t[0:17, cl:cl + 64], gw.t[0:17, d * 512:(d + 1) * 512], True, True, [glrT, gw], [lps])
                ACT(la.t[:], lps.t[0:64, 0:512], AF.Exp, [lps], [la], scale=-1.0)
                ACT(la.t[:], la.t[:], AF.Ln, [la], [la], bias=1.0)
                CP(lah.t[:], la.t[:], [la], [lah], eng='act')
                TT(lal.t[:], la.t[:], lah.t[:], ALU.subtract, [la, lah], [lal])
                for dk in range(8):
                    MM(pss.t[0:64, 0:32], hT.t[:, dk, t0:t0 + 64], wmix.t[:, dk, O_DT:O_DT + 32],
                       dk == 0, False, [wmix, hT], [pss])
                MM(pss.t[0:64, 0:32], constb.t[0:1, 0:64], dtbhl.t[0:1, 0:32], False, False, [constb, dtbhl], [pss])
                MM(pss.t[0:64, 0:32], constb.t[0:1, 0:64], dtbhl.t[0:1, 32:64], False, True, [constb, dtbhl], [pss])
                ACT(dts.t[:], pss.t[0:64, 0:32], AF.Exp, [pss], [dts])
                ACT(dts.t[:], dts.t[:], AF.Ln, [dts], [dts], bias=1.0)
                TT(da.t[:], dts.t[:, dsl], aneg.t[:, dsl], ALU.mult, [dts, aneg], [da])
                CP(dahl.t[:, 0:16], da.t[:], [da], [dahl])
                TT(dahl.t[:, 16:32], da.t[:], dahl.t[:, 0:16], ALU.subtract, [da, dahl], [dahl])
                ndahl = ndahl_r.next()
                TS(ndahl.t[:], dahl.t[:], -1.0, None, ALU.mult, None, [dahl], [ndahl])
                vps = pp[2]
                for nb in range(2):
                    for dk in range(8):
                        MM(vps.t[0:64, nb * 512:(nb + 1) * 512], hT.t[:, dk, t0:t0 + 64],
                           wmix.t[:, dk, O_V + nb * 512:O_V + (nb + 1) * 512], dk == 0, dk == 7, [wmix, hT], [vps])
                CP(vsb.t[:], vps.t[0:64, :], [vps], [vsb], eng='act')
                return dict(dts=dts, da=da, cum_sb=cum_sb, dtw=dtw, ecum=ecum, eL=eL, xsb=xsb, xd=xd, xdw=xdw, la=la, vsb=vsb,
                            kdec=kdec, lah=lah, lal=lal, dahl=dahl, ndahl=ndahl, ybuf=ybuf, obuf=obuf, pss=pss, psc=psc, sps=sps, vps=vps, lps=lps)

            def chunk_mid(b, d, t0, cl, is_ctx, hd):
                dsl = slice(d * 16, d * 16 + 16)
                h16 = lambda ap: ap.rearrange("p (h q) -> p h q", h=16)
                h4 = lambda ap: ap.rearrange("p (h q) -> p h q", h=4)
                dts, da, cum_sb, dtw, ecum, eL = hd["dts"], hd["da"], hd["cum_sb"], hd["dtw"], hd["ecum"], hd["eL"]
                xsb, xd, xdw, la, vsb, kdec = hd["xsb"], hd["xd"], hd["xdw"], hd["la"], hd["vsb"], hd["kdec"]
                lah, lal, dahl, ybuf, obuf = hd["lah"], hd["lal"], hd["dahl"], hd["ybuf"], hd["obuf"]
                ndahl = hd["ndahl"]
                pss, psc, sps = hd["pss"], hd["psc"], hd["sps"]
                MTt, cmc, eT, qdT, kdT, qeT = MTt_r.next(), cmc_r.next(), eT_r.next(), qdT_r.next(), kdT_r.next(), qeT_r.next()
                hd.update(MTt=MTt, cmc=cmc, eT=eT, qdT=qdT, kdT=kdT, qeT=qeT)
                if not is_ctx:
                    CP(cmc.t[:].rearrange("p (g q) -> p g q", g=4), xbcT.t[:, 12:16, cl:cl + 64], [xbcT], [cmc], eng='pool')
                for (oc, lt) in ((slice(32, 48), Lb(d)), (slice(48, 64), Ub(d))):
                    MM(pss.t[0:64, oc], lt, dahl.t[:, 0:16], True, False, [constb, dahl], [pss])
                    MM(pss.t[0:64, oc], lt, dahl.t[:, 16:32], False, True, [constb, dahl], [pss])
                MM(pss.t[:, 64:80], constb.t[0:64, 0:128], dahl.t[:, 0:16], True, False, [constb, dahl], [pss])
                MM(pss.t[:, 64:80], constb.t[0:64, 0:128], dahl.t[:, 16:32], False, True, [constb, dahl], [pss])
                psx = pp[3]
                psxb = psx.t[:].bitcast(BF16)
                for j in range(12):
                    TR(psxb[0:64, j * 128:(j + 1) * 128], xbcT.t[:, j, cl:cl + 64], identb.t[:], [xbcT, identb], [psx])
                CP(xsb.t[:], psxb[0:64, 0:1536], [psx], [xsb], eng='act')
                if not is_ctx:
                    cq = pp[0]
                    for hb in range(2):
                        MM(cq.t[0:64, hb * 512:(hb + 1) * 512], identb.t[0:64, 0:64], nmrep.t[:, d, hb * 512:(hb + 1) * 512],
                           True, False, [identb, nmrep], [cq])
                    for h in range(16):
                        for part in range(2):
                            MM(cq.t[0:64, h * 64:(h + 1) * 64], dahl.t[:, part * 16 + h:part * 16 + h + 1].to_broadcast([64, 64]),
                               Lb(d), False, False, [dahl, constb], [cq])
                        for part in range(2):
                            MM(cq.t[0:64, h * 64:(h + 1) * 64], Lb(d),
                               ndahl.t[:, part * 16 + h:part * 16 + h + 1].to_broadcast([64, 64]),
                               False, (h % 8 == 7) and part == 1, [ndahl, constb], [cq])
                cps = pp[2]
                for h in range(4):
                    MM(cps.t[:, h * 64:(h + 1) * 64], lah.t[:, h * 128:(h + 1) * 128], Lb(d), True, False, [lah, constb], [cps])
                    MM(cps.t[:, h * 64:(h + 1) * 64], lal.t[:, h * 128:(h + 1) * 128], Lb(d), False, True, [lal, constb], [cps])
                for h in range(4):
                    co = slice(256 + h * 64, 256 + (h + 1) * 64)
                    MM(cps.t[:, co], lah.t[:, h * 128:(h + 1) * 128], Lb(d), True, False, [lah, constb], [cps])
                    MM(cps.t[:, co], lal.t[:, h * 128:(h + 1) * 128], Lb(d), False, False, [lal, constb], [cps])
                    MM(cps.t[:, co], lah.t[:, h * 128:(h + 1) * 128], nlm.t[:, d, :], False, False, [lah, nlm], [cps])
                    MM(cps.t[:, co], lal.t[:, h * 128:(h + 1) * 128], nlm.t[:, d, :], False, True, [lal, nlm], [cps])
                MM(cps.t[0:64, 512:1024], Ub(d), lah.t[:], True, False, [lah, constb], [cps])
                MM(cps.t[0:64, 512:1024], Ub(d), lal.t[:], False, True, [lal, constb], [cps])
                kps = pp[3]
                kpsb = kps.t[:].bitcast(BF16)
                for h in range(4):
                    TR(kpsb[0:64, h * 128:(h + 1) * 128], qkT.t[:, 4 + h, cl:cl + 64], identb.t[:], [qkT, identb], [kps])
                ACT(dtw.t[:], pss.t[0:64, 48:64], AF.Exp, [pss], [dtw])
                TT(dtw.t[:], dtw.t[:], dts.t[:, dsl], ALU.mult, [dtw, dts], [dtw])
                ACT(eL.t[:], pss.t[:, 64:80], AF.Exp, [pss], [eL])
                TT(h16(xd.t[:]), h16(xsb.t[:, 0:1024]), dts.t[:, dsl].unsqueeze(2).to_broadcast([64, 16, 64]), ALU.mult, [xsb, dts], [xd])
                TT(h16(xdw.t[:]), h16(xsb.t[:, 0:1024]), dtw.t[:].unsqueeze(2).to_broadcast([64, 16, 64]), ALU.mult, [xsb, dtw], [xdw], eng='pool')
                if not is_ctx:
                    ACT(seg.t[:], cq.t[0:64, :], AF.Exp, [cq], [seg])
                    ACT(ecum.t[:], pss.t[0:64, 32:48], AF.Exp, [pss], [ecum])
                    for g in range(4):
                        MM(psc.t[0:64, g * 64:(g + 1) * 64], xbcT.t[:, 8 + g, cl:cl + 64], xbcT.t[:, 12 + g, cl:cl + 64],
                           True, True, [xbcT], [psc])
                if not is_ctx:
                    TT(MTt.t[:].rearrange("p (g r q) -> p g r q", g=4, r=4),
                       seg.t[:].rearrange("p (g r q) -> p g r q", g=4, r=4),
                       h4(psc.t[0:64, 0:256]).unsqueeze(2).to_broadcast([64, 4, 4, 64]),
                       ALU.mult, [seg, psc], [MTt])
                cv = h4(cps.t[:, 0:256])
                mid = 31 if d == 0 else 32
                last = 63 if d == 0 else 0
                ACT(erc.t[:], cps.t[0:64, 512:1024], AF.Exp, [cps], [erc], scale=-1.0 / 16)
                ACT(ek.t[:], cps.t[:, 256:512], AF.Exp, [cps], [ek], scale=1.0 / 16)
                ACT(eT.t[:], cv[:, :, last], AF.Exp, [cps], [eT], scale=-1.0 / 16)
                TT(kdec.t[:], kpsb[0:64, 0:512], erc.t[:], ALU.mult, [kps, erc], [kdec])
                TT(h4(kdT.t[:]), qkT.t[:, 4:8, cl:cl + 64], h4(ek.t[:]), ALU.mult, [qkT, ek], [kdT], eng='pool')
                if not is_ctx:
                    ACT(eq.t[:], cps.t[:, 256:512], AF.Exp, [cps], [eq], scale=-1.0 / 16)
                    ACT(ec.t[:], cps.t[:, 0:256], AF.Exp, [cps], [ec], scale=-1.0 / 16)
                    TT(h4(qdT.t[:]), qkT.t[:, 0:4, cl:cl + 64], h4(eq.t[:]), ALU.mult, [qkT, eq], [qdT], eng='pool')
                    TT(h4(qeT.t[:]), qkT.t[:, 0:4, cl:cl + 64], h4(ec.t[:]), ALU.mult, [qkT, ec], [qeT], eng='pool')
                return hd

            def chunk_fin(b, d, t0, cl, is_ctx, hd):
                dsl = slice(d * 16, d * 16 + 16)
                h16 = lambda ap: ap.rearrange("p (h q) -> p h q", h=16)
                h4 = lambda ap: ap.rearrange("p (h q) -> p h q", h=4)
                ecum, eL, xsb, xd, xdw, vsb, kdec = hd["ecum"], hd["eL"], hd["xsb"], hd["xd"], hd["xdw"], hd["vsb"], hd["kdec"]
                ybuf, obuf, sps = hd["ybuf"], hd["obuf"], hd["sps"]
                MTt, cmc, eT, qdT, kdT, qeT = hd["MTt"], hd["cmc"], hd["eT"], hd["qdT"], hd["kdT"], hd["qeT"]
                TT(h16(Hs.t[:]), h16(Hs.t[:]), eL.t[:].unsqueeze(2).to_broadcast([128, 16, 64]), ALU.mult, [Hs, eL], [Hs], eng='pool')
                if not is_ctx:
                    for h in range(4):
                        hs = slice(h * 64, h * 64 + 64)
                        MM(sps.t[0:64, 256 + h * 64:256 + (h + 1) * 64], kdT.t[:, hs], qdT.t[:, hs], True, True, [kdT, qdT], [sps])
                    TT(h4(scTt.t[:]), h4(sps.t[0:64, 256:512]), Vm(d).unsqueeze(1).to_broadcast([64, 4, 64]), ALU.mult, [sps, consts], [scTt])
                    yi = pp[0]
                    for h in range(16):
                        hs = slice(h * 64, h * 64 + 64)
                        MM(yi.t[0:64, hs], MTt.t[:, hs], xd.t[:, hs], True, d == 1, [MTt, xd], [yi])
                        if d == 0:
                            MM(yi.t[0:64, hs], dkd.t[:, hs], xsb.t[:, hs], False, True, [dkd, xsb], [yi])
                    yh = pp[2]
                    for g in range(4):
                        gs = slice(g * 256, g * 256 + 256)
                        MM(yh.t[0:64, gs], cmc.t[:, g * 64:(g + 1) * 64], Hb.t[:, gs], True, True, [cmc, Hb], [yh])
                scp = pp[3]
                for h in range(16):
                    g = h // 4
                    hs = slice(h * 64, h * 64 + 64)
                    MM(scp.t[:, hs], xsb.t[:, 1024 + g * 128:1024 + (g + 1) * 128], xdw.t[:, hs], True, True, [xsb, xdw], [scp])
                if not is_ctx:
                    TT(h16(ybuf.t[:]), h16(yh.t[0:64, :]), ecum.t[:].unsqueeze(2).to_broadcast([64, 16, 64]), ALU.mult, [yh, ecum], [ybuf])
                    ops_ = pp[2]
                    for h in range(4):
                        hs = slice(h * 64, h * 64 + 64)
                        vs = slice(h * 256, h * 256 + 256)
                        MM(ops_.t[0:64, vs], scTt.t[:, hs], vsb.t[:, vs], True, False, [scTt, vsb], [ops_])
                        MM(ops_.t[0:64, vs], qeT.t[:, hs], Sb.t[:, vs], False, True, [qeT, Sb], [ops_])
                TT(Hs.t[:], Hs.t[:], scp.t[:], ALU.add, [Hs, scp], [Hs])
                CP(Hb.t[:], Hs.t[:], [Hs], [Hb], eng='act')
                sgp = pp[3]
                for h in range(4):
                    vs = slice(h * 256, h * 256 + 256)
                    MM(sgp.t[:, vs], kdec.t[:, h * 128:(h + 1) * 128], vsb.t[:, vs], True, True, [kdec, vsb], [sgp])
                if not is_ctx:
                    TT(ybuf.t[:], ybuf.t[:], yi.t[0:64, :], ALU.add, [ybuf, yi], [ybuf])
                    CP(obuf.t[:], ops_.t[0:64, :], [ops_], [obuf], eng='act')
                for h in range(4):
                    vs = slice(h * 256, h * 256 + 256)
                    STT(Ss.t[:, vs], Ss.t[:, vs], eT.t[:, h:h + 1], sgp.t[:, vs], ALU.mult, ALU.add, [Ss, eT, sgp], [Ss])
                CP(Sb.t[:], Ss.t[:], [Ss], [Sb], eng='act')
                if not is_ctx:
                    tx = t0 - CTXL
                    nm = "yo%d_%d" % (b, tx)
                    if d == 0:
                        p.dma('sp', yo_d[b, tx:tx + 64, 0:1024], ybuf.t[:], reads=[ybuf], dram_w=[nm + "y"])
                        p.dma('sp', yo_d[b, tx:tx + 64, 1024:2048], obuf.t[:], reads=[obuf], dram_w=[nm + "o"])
                    else:
                        p.dma('pool', yo_d[b, tx:tx + 64, 0:1024], ybuf.t[:], reads=[ybuf], dram_r=[nm + "y"], dram_w=[nm + "y"], accum=ALU.add)
                        p.dma('pool', yo_d[b, tx:tx + 64, 1024:2048], obuf.t[:], reads=[obuf], dram_r=[nm + "o"], dram_w=[nm + "o"], accum=ALU.add)

            nsc = dbg.get("nsc", 8)
            for b in range(nseq):
                for i in range(2 + 2 * nsc):
                    xt = scr8.t[:, 0:1024]
                    if i < 2:
                        p.dma('sp', xt, ctx_d[b, i * 128:(i + 1) * 128, :], writes=[scr8])
                        r = 2
                    else:
                        p.dma('sp', xt, x_d[b, (i - 2) * 128:(i - 1) * 128, :], writes=[scr8])
                        r = b
                    norm_to_T(xt, scr8, A1, B1fn, r, hT, i * 128, (sq.t[:], sq, ssq, scr8.t[:, 1024:2048], scr8))
                    if i >= 2:
                        p.dma('sp', hts_d[b, :, :, (i - 2) * 128:(i - 1) * 128], hT.t[:, :, i * 128:(i + 1) * 128],
                              reads=[hT], dram_w=["hts%d_%d" % (b, (i - 2) * 128)])
                if dbg.get("dump_hT") and b == 0:
                    dump("hT", hT, hT.t[:], [128, 8, NTOK])
                xhi = CTXL + 256 * nsc
                p.barrier()
                for d in range(2):
                    for st in (Hs, Ss):
                        MEMSET(st.t[:], 0.0, [st])
                    for st in (Hb, Sb):
                        MEMSET(st.t[:], 0.0, [st])
                    supers = [(0, 0, CTXL, True)] + [(CTXL + 256 * i, CTXL, xhi, False) for i in range(nsc)]
                    if d == 1:
                        supers = [supers[0]] + supers[1:][::-1]
                    chunks = []
                    for si, (tok0, lo, hi, is_ctx) in enumerate(supers):
                        cs = list(range(4) if d == 0 else range(3, -1, -1))
                        for ci, c in enumerate(cs):
                            chunks.append((tok0, lo, hi, is_ctx, c, ci == 0))

                    def prep(k):
                        tok0, lo, hi, is_ctx, c, first = chunks[k]
                        if first:
                            si = 0 if is_ctx else 1 + (tok0 - CTXL) // 256
                            nm = "sv%d_%d" % (b, si)
                            sxv = sx_d[b, si].rearrange("p (a t) -> p a t", a=16)
                            sqv = sqk_d[b, si].rearrange("p (a t) -> p a t", a=8)
                            if d == 0:
                                proj_super(tok0, lo, hi)
                                p.dma('sp', sxv, xbcT.t[:], reads=[xbcT], dram_w=[nm + "x"])
                                p.dma('sp', sqv, qkT.t[:], reads=[qkT], dram_w=[nm + "q"])
                                p.dma('sp', sg_d[b, si], glrT.t[:], reads=[glrT], dram_w=[nm + "g"])
                            else:
                                p.dma('sp', xbcT.t[:], sxv, writes=[xbcT], dram_r=[nm + "x"])
                                p.dma('sp', qkT.t[:], sqv, writes=[qkT], dram_r=[nm + "q"])
                                p.dma('sp', glrT.t[:], sg_d[b, si], writes=[glrT], dram_r=[nm + "g"])
                            if dbg.get("dump_xbc") and b == 0 and d == 0 and tok0 == dbg["dump_xbc"]:
                                dump("xbcT", xbcT, xbcT.t[:], [128, 16, 256])
                                dump("qkT", qkT, qkT.t[:], [128, 8, 256])
                        hd = chunk_head(b, d, tok0 + 64 * c, 64 * c, is_ctx)
                        return chunk_mid(b, d, tok0 + 64 * c, 64 * c, is_ctx, hd)

                    hd_cur = prep(0)
                    for k in range(len(chunks)):
                        hd_nxt = prep(k + 1) if k + 1 < len(chunks) else None
                        tok0, lo, hi, is_ctx, c, first = chunks[k]
                        chunk_fin(b, d, tok0 + 64 * c, 64 * c, is_ctx, hd_cur)
                        hd_cur = hd_nxt
                    if dbg.get("dump_state") and b == 0:
                        dump("H%d" % d, Hs, Hs.t[:], [128, 1024])
                        dump("S%d" % d, Ss, Ss.t[:], [128, 1024])
                p.barrier()
            p.es = es
          p.barrier()

        build_rest(nc, p, locals())
        p.finish()
    return nc, dbg_outs


def build_rest(nc, p, L):
    import types
    N = types.SimpleNamespace(**L)
    es, dbg, phases, nseq = N.es, N.dbg, N.phases, N.nseq
    MM, TR, ACT, TT, TS, STT, CP, MEMSET, PS, dump = N.MM, N.TR, N.ACT, N.TT, N.TS, N.STT, N.CP, N.MEMSET, N.PS, N.dump
    consts, vecs, modT, scT, A2, identf = N.consts, N.vecs, N.modT, N.scT, N.A2, N.identf
    x_d, yo_d, x1_d, out_d = N.x_d, N.yo_d, N.x1_d, N.out_d
    A1 = N.A1

    def load_w(dst, src_ap, nk):
        for dk in range(nk):
            p.dma('pool', dst.t[:, dk, :], src_ap[dk * 128:(dk + 1) * 128, :], writes=[dst])

    def compute_G(G, which, rb):
        with ExitStack() as esg:
            p.es = esg
            scR = p.sbuf("scR", [128, 8, 128], BF16)
            wb = p.sbuf("wgb", [128, 8, 1024], BF16)
            rb_t = p.sbuf("rbt", [128, 1024], F32)
            p.dma('sp', rb_t.t[:], N.rowsbig_d[0:1, rb:rb + 1024].partition_broadcast(128), writes=[rb_t])
            load_w(wb, N.w_ada_d[:, which * 1024:(which + 1) * 1024], 8)
            for b in range(NBL):
                CP(scR.t[:], scT.t[:, :, b:b + 1].to_broadcast([128, 8, 128]), [scT], [scR])
                gps = PS()
                for nb in range(2):
                    for dk in range(8):
                        MM(gps.t[:, nb * 512:(nb + 1) * 512], scR.t[:, dk, :], wb.t[:, dk, nb * 512:(nb + 1) * 512],
                           dk == 0, dk == 7, [wb, scR], [gps])
                TT(G.t[:, b, :], gps.t[:], rb_t.t[:], ALU.add, [gps, rb_t], [G])
            p.es = es
        p.barrier()

    def rstd_of(ssq_in, out, n, reads, writes):
        ACT(out, ssq_in, AF.Sqrt, reads, writes, bias=N.epsb.t[:, 0:1], scale=1.0 / n)
        p.op('dve', lambda e: e.reciprocal(out, out), writes, writes)

    if "P" in phases:
      with ExitStack() as esP:
        p.es = esP
        G1 = p.sbuf("G1", [128, NBL, 1024], F32)
        compute_G(G1, 2, RB_BA2)
        p.es = esP
        wzr = p.sbuf("wzr", [128, 8, 2048], BF16)
        wmg = p.sbuf("wmg", [128, 8, 2048], BF16)
        wbs = p.sbuf("wbs", [128, 8, 1024], BF16)
        wbg = p.sbuf("wbg", [128, 8, 1024], BF16)
        wo = p.sbuf("wo", [128, 8, 1024], BF16)
        for dk in range(8):
            p.dma('pool', wzr.t[:, dk, 0:1024], N.w_in_d[dk * 128:(dk + 1) * 128, 0:1024], writes=[wzr])
            p.dma('pool', wzr.t[:, dk, 1024:2048], N.w_in_d[dk * 128:(dk + 1) * 128, 5168:6192], writes=[wzr])
        load_w(wmg, N.w_merge_d, 8)
        load_w(wbs, N.w_brs_d, 8)
        load_w(wbg, N.w_brg_d, 8)
        load_w(wo, N.w_o_d, 8)
        for b in range(nseq):
            for q4 in range(4):
                p.dma('sp', x1_d[b, q4 * 512:(q4 + 1) * 512, :], x_d[b, q4 * 512:(q4 + 1) * 512, :],
                      dram_w=["x1_%d_%d" % (b, q4 * 512 + k * 128) for k in range(4)])
        TW = 256
        NS = TW // 128
        hT4s = [p.sbuf("hT4_%d" % i, [128, 8, TW], BF16) for i in range(2)]
        yT4s = [p.sbuf("yT4_%d" % i, [128, 8, TW], BF16) for i in range(2)]
        oT4s = [p.sbuf("oT4_%d" % i, [128, 8, TW], BF16) for i in range(2)]
        mT4s = [p.sbuf("mT4_%d" % i, [128, 8, TW], BF16) for i in range(2)]
        gT_r = Rot(p, "gT", [128, 2, TW], BF16, 2)
        m12_r = Rot(p, "m12", [128, 2 * TW], F32, 1)
        zrs = [p.sbuf("zr%d" % i, [128, 2048], BF16) for i in range(2)]
        yots = [p.sbuf("yot%d" % i, [128, 2048], F32) for i in range(2)]
        scr8 = p.sbuf("scr8p", [128, 2048], F32)
        tbuf = p.sbuf("tbuf", [128, 1024], F32)
        sq = p.sbuf("sqp", [128, 1024], BF16)
        sq2 = sq
        ssq = p.sbuf("ssqp", [128, 4], F32)
        so = p.sbuf("so", [128, 8], F32)

        def B1fn(j, r):
            return modT.t[:, j, r:r + 1]

        ntile = dbg.get("nt4", SEQ // TW)
        tiles = [(b, t) for b in range(nseq) for t in range(ntile)]

        mhalf = p.sbuf("mhalf", [128, 4], F32)
        MEMSET(mhalf.t[:], -0.5, [mhalf])
        sig_r = Rot(p, "sig", [128, 512], BF16, 2)

        def rstd_pow(ssq_ap, out_ap, n, res):
            ncol = ssq_ap.shape[-1]
            TS(out_ap, ssq_ap, 1.0 / n, EPS, ALU.mult, ALU.add, [res], [res])
            TT(out_ap, out_ap, mhalf.t[:, 0:ncol], ALU.pow, [res, mhalf], [res], eng='pool')

        def PA1a(ti, s):
            b, t = tiles[ti]
            yot = yots[s]
            tok = t * TW + s * 128
            p.dma('sp', yot.t[:], yo_d[b, tok:tok + 128, :], writes=[yot],
                  dram_r=["yo%d_%d%s" % (b, tok + o_, s_) for o_ in (0, 64) for s_ in ("y", "o")])

        def PA1b1(ti, s):
            b, t = tiles[ti]
            hT4 = hT4s[ti % 2]
            tok = t * TW + s * 128
            p.dma('sp', hT4.t[:, :, s * 128:(s + 1) * 128], N.hts_d[b, :, :, tok:tok + 128], writes=[hT4],
                  dram_r=["hts%d_%d" % (b, tok)])

        def PA1b2(ti, s):
            hT4 = hT4s[ti % 2]
            zr = zrs[s]
            for nb in range(4):
                ps = PS()
                for dk in range(8):
                    MM(ps.t[:, 0:512], hT4.t[:, dk, s * 128:(s + 1) * 128], wzr.t[:, dk, nb * 512:(nb + 1) * 512],
                       dk == 0, dk == 7, [hT4, wzr], [ps])
                sig = sig_r.next()
                ACT(sig.t[:], ps.t[:, 0:512], AF.Sigmoid, [ps], [sig])
                TT(zr.t[:, nb * 512:(nb + 1) * 512], ps.t[:, 0:512], sig.t[:], ALU.mult, [ps, sig], [zr])

        def PA2a(ti, s):
            zr, yot = zrs[s], yots[s]
            TT(yot.t[:, 0:1024], yot.t[:, 0:1024], zr.t[:, 0:1024], ALU.mult, [yot, zr], [yot])
            TT(sq2.t[:], yot.t[:, 0:1024], yot.t[:, 0:1024], ALU.mult, [yot], [sq2])
            p.op('dve', lambda e: e.reduce_sum(so.t[:, 0:1], sq2.t[:], AX.X), [sq2], [so])
            rstd_pow(so.t[:, 0:1], so.t[:, 0:1], 1024.0, so)
            TS(yot.t[:, 0:1024], yot.t[:, 0:1024], so.t[:, 0:1], None, ALU.mult, None, [yot, so], [yot])
            TT(sq2.t[:], yot.t[:, 1024:2048], yot.t[:, 1024:2048], ALU.mult, [yot], [sq2], eng='pool')
            p.op('dve', lambda e: e.reduce_sum(so.t[:, 4:8], sq2.t[:].rearrange("p (h v) -> p h v", h=4), AX.X), [sq2], [so])
            rstd_pow(so.t[:, 4:8], so.t[:, 4:8], 256.0, so)
            TT(yot.t[:, 1024:2048].rearrange("p (h v) -> p h v", h=4), yot.t[:, 1024:2048].rearrange("p (h v) -> p h v", h=4),
               so.t[:, 4:8].unsqueeze(2).to_broadcast([128, 4, 256]), ALU.mult, [yot, so], [yot])
            TT(yot.t[:, 1024:2048], yot.t[:, 1024:2048], zr.t[:, 1024:2048], ALU.mult, [yot, zr], [yot], eng='pool')

        def PA2b(ti, s):
            yT4, oT4 = yT4s[ti % 2], oT4s[ti % 2]
            yot = yots[s]
            for (half, dstT, vg) in ((0, yT4, V_SNG), (1, oT4, V_GNG)):
                ps = PS()
                for j in range(8):
                    TR(ps.t[:, j * 128:(j + 1) * 128], yot.t[:, half * 1024 + j * 128:half * 1024 + (j + 1) * 128],
                       identf, [yot, consts], [ps])
                if half == 0:
                    for j in range(8):
                        ACT(dstT.t[:, j, s * 128:(s + 1) * 128], ps.t[:, j * 128:(j + 1) * 128], AF.Identity,
                            [ps, vecs], [dstT], scale=vecs.t[:, vg + j:vg + j + 1])
                else:
                    TT(dstT.t[:, :, s * 128:(s + 1) * 128], ps.t[:].rearrange("p (j t) -> p j t", j=8),
                       vecs.t[:, vg:vg + 8].unsqueeze(2).to_broadcast([128, 8, 128]), ALU.mult, [ps, vecs], [dstT])

        def Bstep(ti, jc):
            hT4, yT4, oT4, mT4 = hT4s[ti % 2], yT4s[ti % 2], oT4s[ti % 2], mT4s[ti % 2]
            gT, m12 = gT_r.next(), m12_r.next()
            ps = PS()
            for gi in range(2):
                gc = gi * 8 + jc
                for dk in range(8):
                    MM(ps.t[:, gi * 512:gi * 512 + TW], wmg.t[:, dk, gc * 128:(gc + 1) * 128], hT4.t[:, dk, :], dk == 0, dk == 7, [wmg, hT4], [ps])
                ACT(gT.t[:, gi, :], ps.t[:, gi * 512:gi * 512 + TW], AF.Sigmoid, [ps, vecs], [gT], bias=vecs.t[:, V_BM + gc:V_BM + gc + 1])
            ps = PS()
            for dk in range(8):
                MM(ps.t[:, 0:TW], wbs.t[:, dk, jc * 128:(jc + 1) * 128], yT4.t[:, dk, :], dk == 0, dk == 7, [wbs, yT4], [ps])
            for dk in range(8):
                MM(ps.t[:, 512:512 + TW], wbg.t[:, dk, jc * 128:(jc + 1) * 128], oT4.t[:, dk, :], dk == 0, dk == 7, [wbg, oT4], [ps])
            TT(m12.t[:].rearrange("p (g t) -> p g t", g=2), ps.t[:].rearrange("p (g t) -> p g t", g=2)[:, :, 0:TW], gT.t[:], ALU.mult, [ps, gT], [m12])
            TT(mT4.t[:, jc, :], m12.t[:, 0:TW], m12.t[:, TW:2 * TW], ALU.add, [m12], [mT4], eng='pool')

        def Cstep(ti, s):
            b, t = tiles[ti]
            mT4 = mT4s[ti % 2]
            tok = t * TW + s * 128
            ps = PS()
            for nb in range(2):
                for dk in range(8):
                    MM(ps.t[:, nb * 512:(nb + 1) * 512], mT4.t[:, dk, s * 128:(s + 1) * 128], wo.t[:, dk, nb * 512:(nb + 1) * 512],
                       dk == 0, dk == 7, [mT4, wo], [ps])
            TT(tbuf.t[:], ps.t[:], G1.t[:, b, :], ALU.mult, [ps, G1], [tbuf])
            nm = "x1_%d_%d" % (b, tok)
            p.dma('pool', x1_d[b, tok:tok + 128, :], tbuf.t[:], reads=[tbuf], dram_r=[nm], dram_w=[nm], accum=ALU.add)

        A1m = N.A1
        for s in range(NS):
            PA1a(0, s)
            PA1b1(0, s)
            PA1b2(0, s)
        for s in range(NS):
            PA2a(0, s)
            PA2b(0, s)
        sched = {0: [(PA1a, 0)], 1: [(PA1b1, 0), (PA1a, 1)], 2: [(PA1b2, 0), (PA1b1, 1)], 3: [(PA2a, 0)],
                 4: [(PA1b2, 1)], 5: [(PA2b, 0), (PA2a, 1)], 7: [(PA2b, 1)]}
        for ti in range(len(tiles)):
            nxt = ti + 1 < len(tiles)
            for jc in range(8):
                Bstep(ti, jc)
                if nxt:
                    for (fn, s) in sched.get(jc, []):
                        fn(ti + 1, s)
            for s in range(NS):
                Cstep(ti, s)
        p.es = es
      p.barrier()

    if "F" in phases:
      with ExitStack() as esF:
        p.es = esF
        G2 = p.sbuf("G2", [128, NBL, 1024], F32)
        compute_G(G2, 5, RB_BA5)
        p.es = esF
        fng = p.sbuf("fng", [128, 1024], F32)
        p.dma('sp', fng.t[:], N.rowsbig_d[0:1, RB_FNG:RB_FNG + 1024].partition_broadcast(128), writes=[fng])
        wdn = p.sbuf("wdn", [128, 22, 1024], BF16)
        load_w(wdn, N.w_down_d, 22)
        wupr = Rot(p, "wup", [128, 8, 256], BF16, 3)
        h2T = p.sbuf("h2T", [128, 8, 1152], BF16)
        aT = p.sbuf("aT", [128, 22, 1024], BF16)
        scr8 = p.sbuf("scr8f", [128, 2048], F32)
        sq = p.sbuf("sqf", [128, 1024], BF16)
        ssq = p.sbuf("ssqf", [128, 4], F32)
        usb_r = Rot(p, "usb", [128, 17 * 66], BF16, 2)
        for _u in usb_r.b:
            MEMSET(_u.t[:], 0.0, [_u])
        dg_r = Rot(p, "dg", [128, 9, 128], BF16, 2)
        sg_r = Rot(p, "sg", [128, 1024], F32, 2)
        identb = N.identb

        def B2fn(j, r):
            return modT.t[:, 24 + j, r:r + 1]

        nj = dbg.get("nj", 22)
        wsrc = N.w_up_d.rearrange("(dk p) n -> p dk n", p=128)
        for b in range(nseq):
            for hf in range(dbg.get("nhalf", 2)):
                base = 0 if hf == 0 else 896
                for i in range(9):
                    tok = base + i * 128
                    xt = scr8.t[:, 0:1024]
                    p.dma('sp', xt, x1_d[b, tok:tok + 128, :], writes=[scr8], dram_r=["x1_%d_%d" % (b, tok)])
                    N.norm_to_T(xt, scr8, A2, B2fn, b, h2T, i * 128, (sq.t[:], sq, ssq, scr8.t[:, 1024:2048], scr8))
                if dbg.get("dump_h2T") and b == 0 and hf == 0:
                    dump("h2T", h2T, h2T.t[:], [128, 8, 1152])
                m0 = 0 if hf == 0 else 128
                h0 = 1024 if hf == 0 else 64
                off = 0 if hf == 0 else 1
                hrow = 16 if hf == 0 else 0
                for j in range(nj):
                    wu = wupr.next()
                    p.dma('pool', wu.t[:, :, 0:128], wsrc[:, :, j * 128:(j + 1) * 128], writes=[wu])
                    p.dma('pool', wu.t[:, :, 128:256], wsrc[:, :, DFF + j * 128:DFF + (j + 1) * 128], writes=[wu])
                    pcs = []
                    for part in range(2):
                        ch = part * 22 + j
                        pm = PS()
                        pc = PS()
                        for nb in range(2):
                            for dk in range(8):
                                MM(pm.t[:, nb * 512:(nb + 1) * 512], wu.t[:, dk, part * 128:(part + 1) * 128],
                                   h2T.t[:, dk, m0 + nb * 512:m0 + (nb + 1) * 512], dk == 0, dk == 7, [wu, h2T], [pm])
                        for dk in range(8):
                            MM(pc.t[:, 0:64], wu.t[:, dk, part * 128:(part + 1) * 128], h2T.t[:, dk, h0:h0 + 64],
                               dk == 0, dk == 7, [wu, h2T], [pc])
                        usb = usb_r.next()
                        u3 = usb.t[:].rearrange("p (r c) -> p r c", c=66)
                        CP(u3[:, off:off + 16, 1:65], pm.t[:].rearrange("p (r c) -> p r c", c=64), [pm], [usb], eng='act')
                        CP(u3[:, hrow, 1:65], pc.t[:, 0:64], [pc], [usb], eng='dve')
                        dg = dg_r.next()
                        wb_ = V_FCW + ch * 9
                        TT(dg.t[:], identb.t[:].unsqueeze(1).to_broadcast([128, 9, 128]),
                           vecs.t[:, wb_:wb_ + 9].unsqueeze(2).to_broadcast([128, 9, 128]), ALU.mult, [identb, vecs], [dg], eng='pool')
                        taps = [(0, 0)] + [(dr, dc) for dr in (-1, 0, 1) for dc in (-1, 0, 1) if (dr, dc) != (0, 0)]
                        for bank in range(2):
                            todo = []
                            for (dr, dc) in taps:
                                mlo = 1 if (hf == 0 and dr == -1) else 0
                                mhi = 15 if (hf == 1 and dr == 1) else 16
                                lo = max(mlo, bank * 8)
                                hi = min(mhi, bank * 8 + 8)
                                if hi <= lo:
                                    continue
                                todo.append((dr, dc, lo, hi))
                            for ti, (dr, dc, lo, hi) in enumerate(todo):
                                k = (dr + 1) * 3 + (dc + 1)
                                MM(pc.t[:, lo * 64:hi * 64], dg.t[:, k, :], u3[:, lo + off + dr:hi + off + dr, 1 + dc:65 + dc],
                                   ti == 0, ti == len(todo) - 1, [dg, usb], [pc])
                        pcs.append((pc, ch))
                    sg = sg_r.next()
                    (pcg, chg), (pcv, chv) = pcs
                    ACT(sg.t[:], pcg.t[:], AF.Silu, [pcg, vecs], [sg], bias=vecs.t[:, V_FCB + chg:V_FCB + chg + 1])
                    STT(aT.t[:, j, :], pcv.t[:], vecs.t[:, V_FCB + chv:V_FCB + chv + 1], sg.t[:], ALU.add, ALU.mult, [pcv, vecs, sg], [aT])
                for s in range(8):
                    tok = hf * 1024 + s * 128
                    ps = PS()
                    for nb in range(2):
                        for j in range(nj):
                            MM(ps.t[:, nb * 512:(nb + 1) * 512], aT.t[:, j, s * 128:(s + 1) * 128], wdn.t[:, j, nb * 512:(nb + 1) * 512],
                               j == 0, j == nj - 1, [aT, wdn], [ps])
                    xt = scr8.t[:, 0:1024]
                    x2 = scr8.t[:, 1024:2048]
                    p.dma('sp', xt, x1_d[b, tok:tok + 128, :], writes=[scr8], dram_r=["x1_%d_%d" % (b, tok)])
                    TT(x2, ps.t[:], G2.t[:, b, :], ALU.mult, [ps, G2], [scr8])
                    TT(x2, x2, xt, ALU.add, [scr8], [scr8])
                    ACT(sq.t[:], x2, AF.Square, [scr8], [sq])
                    p.op('dve', lambda e: e.reduce_sum(ssq.t[:, 0:1], sq.t[:], AX.X), [sq], [ssq])
                    rstd_of(ssq.t[:, 0:1], ssq.t[:, 1:2], 1024.0, [ssq], [ssq])
                    TS(x2, x2, ssq.t[:, 1:2], None, ALU.mult, None, [scr8, ssq], [scr8])
                    TT(xt, x2, fng.t[:], ALU.mult, [scr8, fng], [scr8])
                    p.dma('sp', out_d[b, tok:tok + 128, :], xt, reads=[scr8], final=True)
        p.es = es
      p.barrier()


_CACHE = {}


def kernel(**inputs):
    inp = {k: np.asarray(v) for k, v in inputs.items()}
    if "nc" not in _CACHE:
        _CACHE["nc"] = build_program()[0]
    nc = _CACHE["nc"]
    sh = prep_shared(inp)
    shared = dict(w_ada=inp["w_ada"][0], w_in=inp["w_in"][0], w_merge=inp["w_merge"][0], w_br_ssd=inp["w_br_ssd"][0],
                  w_br_gla=inp["w_br_gla"][0], w_o=inp["w_o"][0], w_up=inp["w_up"][0], w_down=inp["w_down"][0])
    shared.update(sh)
    shared = {k: np.ascontiguousarray(np.asarray(v, np.float32)) for k, v in shared.items()}
    in_maps = []
    for core in range(8):
        b0 = core * NBL
        cT = np.stack([inp["c"][b0], inp["c"][b0 + 1], inp["c_ctx"]], axis=1).astype(np.float32)
        cT = np.ascontiguousarray(cT.reshape(8, 128, 3).transpose(1, 0, 2))
        m = dict(shared)
        m["x"] = np.ascontiguousarray(inp["x"][b0:b0 + NBL], dtype=np.float32)
        m["ctx"] = np.ascontiguousarray(inp["ctx"][b0:b0 + NBL], dtype=np.float32)
        m["cT"] = cT
        in_maps.append(m)
    res = run_bass_kernel_spmd(nc, in_maps, core_ids=list(range(8)))
    out = np.concatenate([np.asarray(r["out"]) for r in res.results], axis=0)
    return out.astype(np.float32)
```

```python
import numpy as np
import concourse.bass as bass
import concourse.mybir as mybir
from concourse.bass_utils import run_bass_kernel_spmd
from contextlib import ExitStack

F32 = mybir.dt.float32
BF16 = mybir.dt.bfloat16
AF = mybir.ActivationFunctionType
ALU = mybir.AluOpType
AX = mybir.AxisListType


class Res:
    def __init__(self, name, t=None):
        self.name = name
        self.t = t
        self.lw = {}
        self.rd = {}


class Prog:
    NDMA = 8

    def __init__(self, nc, es):
        self.nc = nc
        self.es = es
        self.names = ['pe', 'act', 'dve', 'pool', 'sp']
        self.E = {'pe': nc.tensor, 'act': nc.scalar, 'dve': nc.vector, 'pool': nc.gpsimd, 'sp': nc.sync}
        self.cnt = {k: 0 for k in self.names}
        self.h = {}
        for k in self.names:
            self.h[('e', k)] = es.enter_context(nc.semaphore("s_" + k))
        self.dq = ('sp', 'pool', 'act')
        self.dcnt = {q: 0 for q in self.dq}
        self.dval = {}
        for q in self.dq:
            for i in range(self.NDMA):
                self.h[('d', q, i)] = es.enter_context(nc.semaphore("d_%s%d" % (q, i)))
                self.dval[('d', q, i)] = 0
        self.seen = {k: {} for k in self.names}
        self.dram = {}
        self.final = []
        self.nwait = 0

    def sbuf(self, name, shape, dtype):
        self.uid = getattr(self, "uid", 0) + 1
        t = self.es.enter_context(self.nc.sbuf_tensor("sb%d_%s" % (self.uid, name), list(shape), dtype))
        return Res(name, t)

    def psum(self, name, shape, dtype):
        t = self.es.enter_context(self.nc.psum_tensor("ps_" + name, list(shape), dtype))
        return Res(name, t)

    def _dres(self, name):
        if name not in self.dram:
            self.dram[name] = Res(name)
        return self.dram[name]

    def _collect(self, eng, reads, writes):
        need = {}

        def add(tok):
            if tok is None:
                return
            k, v = tok
            if need.get(k, 0) < v:
                need[k] = v

        for r in reads:
            for k, v in r.lw.items():
                add((k, v))
        for w in writes:
            for k, v in w.lw.items():
                add((k, v))
            for k, v in w.rd.items():
                add((k, v))
        out = []
        for k, v in need.items():
            if eng == 'pe' and k == ('e', 'pe'):
                continue
            if self.seen[eng].get(k, 0) >= v:
                continue
            self.seen[eng][k] = v
            out.append((k, v))
        return out

    def _mark(self, tok, reads, writes):
        k, v = tok
        for r in reads:
            if r.rd.get(k, 0) < v:
                r.rd[k] = v
        for w in writes:
            if w.lw.get(k, 0) < v:
                w.lw[k] = v
            w.rd = {}

    def op(self, eng, fn, reads=(), writes=()):
        waits = self._collect(eng, reads, writes)
        self.cnt[eng] += 1
        sem = self.h[('e', eng)]
        hs = [(self.h[k], v) for k, v in waits]
        self.nwait += len(hs)

        e = self.E[eng]
        for hh, v in hs:
            e.wait_ge(hh, v)
        fn(e).then_inc(sem, 1)
        self._mark((('e', eng), self.cnt[eng]), reads, writes)

    def dma(self, q, out_ap, in_ap, reads=(), writes=(), dram_r=(), dram_w=(), final=False, accum=None):
        reads = list(reads) + [self._dres(n) for n in dram_r]
        writes = list(writes) + [self._dres(n) for n in dram_w]
        i = self.dcnt[q] % self.NDMA
        self.dcnt[q] += 1
        key = ('d', q, i)
        prev = self.dval[key]
        waits = self._collect(q, reads, writes)
        if prev > 0 and self.seen[q].get(key, 0) < prev:
            self.seen[q][key] = prev
            waits.append((key, prev))
        self.dval[key] = prev + 16
        sem = self.h[key]
        hs = [(self.h[k], v) for k, v in waits]
        self.nwait += len(hs)

        e = self.E[q]
        for hh, v in hs:
            e.wait_ge(hh, v)
        if accum is not None:
            e.dma_start(out=out_ap, in_=in_ap, accum_op=accum).then_inc(sem, 16)
        else:
            e.dma_start(out=out_ap, in_=in_ap).then_inc(sem, 16)
        tok = (key, prev + 16)
        self._mark(tok, reads, writes)
        if final:
            self.final.append(tok)

    def barrier(self):
        toks = [(('e', k), self.cnt[k]) for k in self.names if self.cnt[k] > 0]
        toks += [(k, v) for k, v in self.dval.items() if v > 0]
        for eng in self.names:
            e = self.E[eng]
            for k, v in toks:
                if eng == 'pe' and k == ('e', 'pe'):
                    continue
                if self.seen[eng].get(k, 0) >= v:
                    continue
                self.seen[eng][k] = v
                e.wait_ge(self.h[k], v)

    def finish(self):
        fin = {}
        for k, v in self.final:
            fin[k] = max(fin.get(k, 0), v)
        hs = [(self.h[k], v) for k, v in fin.items()]

        e = self.E['sp']
        for hh, v in hs:
            e.wait_ge(hh, v)


D = 1024
SEQ = 2048
CTXL = 256
NTOK = CTXL + SEQ
NBL = 2
DFF = 2816
EPS = 1e-6
WM0, WMN = 1024, 4144
O_XBC, O_DT, O_Q, O_K, O_V, O_G = 0, 2048, 2080, 2592, 3104, 4128
V_N1G, V_N2G, V_BADA, V_CW, V_CB, V_BM, V_FCW, V_FCB, V_SNG, V_GNG, NV = 0, 8, 16, 64, 112, 128, 144, 540, 584, 592, 600
R_DTB, R_ALOG, R_DSK, NR = 0, 32, 64, 80
RB_FNG, RB_BA2, RB_BA5, NRB = 0, 1024, 2048, 3072
K_ID, K_ONE, K_L, K_U, K_NM, K_V, NCN = 0, 128, 256, 384, 512, 640, 768


def host_consts():
    c = np.zeros((128, NCN), np.float32)
    c[:, K_ID:K_ID + 128] = np.eye(128, dtype=np.float32)
    c[:, K_ONE:K_ONE + 128] = 1.0
    t = np.arange(64)[:, None]
    i = np.arange(64)[None, :]
    c[0:64, K_L:K_L + 64] = (t <= i)
    c[0:64, K_L + 64:K_L + 128] = (t >= i)
    c[0:64, K_U:K_U + 64] = (t > i)
    c[0:64, K_U + 64:K_U + 128] = (t < i)
    c[0:64, K_NM:K_NM + 64] = np.where(t <= i, 0.0, -30000.0)
    c[0:64, K_NM + 64:K_NM + 128] = np.where(t >= i, 0.0, -30000.0)
    c[0:64, K_V:K_V + 64] = (t <= i)
    c[0:64, K_V + 64:K_V + 128] = (t >= i)
    return c


def fm(v):
    v = np.asarray(v, np.float32)
    return np.ascontiguousarray(v.reshape(-1, 128).T)


def prep_shared(inp):
    vecs = np.zeros((128, NV), np.float32)
    vecs[:, V_N1G:V_N1G + 8] = fm(inp["norm1_g"][0])
    vecs[:, V_N2G:V_N2G + 8] = fm(inp["norm2_g"][0])
    vecs[:, V_BADA:V_BADA + 48] = fm(inp["b_ada"][0])
    cw = inp["ssd_conv_w"][0]
    for k in range(3):
        vecs[:, V_CW + k:V_CW + 48:3] = fm(cw[k])
    vecs[:, V_CB:V_CB + 16] = fm(inp["ssd_conv_b"][0])
    vecs[:, V_BM:V_BM + 16] = fm(inp["b_merge"][0])
    fw = inp["ffn_conv_w"][0].reshape(9, -1)
    for k in range(9):
        vecs[:, V_FCW + k:V_FCW + 396:9] = fm(fw[k])
    vecs[:, V_FCB:V_FCB + 44] = fm(inp["ffn_conv_b"][0])
    vecs[:, V_SNG:V_SNG + 8] = fm(inp["ssd_norm_g"][0])
    vecs[:, V_GNG:V_GNG + 8] = np.tile(fm(inp["gla_norm_g"][0]), (1, 4))
    rows = np.zeros((1, NR), np.float32)
    rowsbig = np.zeros((1, NRB), np.float32)
    rowsbig[0, RB_FNG:RB_FNG + 1024] = inp["final_norm_g"]
    rowsbig[0, RB_BA2:RB_BA2 + 1024] = inp["b_ada"][0][2048:3072]
    rowsbig[0, RB_BA5:RB_BA5 + 1024] = inp["b_ada"][0][5120:6144]
    rows[0, R_DTB:R_DTB + 32] = inp["ssd_dt_bias"][0].reshape(-1)
    rows[0, R_ALOG:R_ALOG + 32] = inp["ssd_a_log"][0].reshape(-1)
    rows[0, R_DSK:R_DSK + 16] = inp["ssd_d"][0]
    gw = np.zeros((17, 1024), np.float32)
    gw[0:16] = np.transpose(inp["gla_gate_w"][0], (1, 0, 2)).reshape(16, 1024)
    gw[16] = inp["gla_gate_b"][0].reshape(-1)
    return dict(vecs=vecs, rows=rows, rowsbig=rowsbig, gw=gw, consts=host_consts())


class Rot:
    def __init__(self, p, name, shape, dtype, n=2):
        self.b = [p.sbuf("%s%d" % (name, i), shape, dtype) for i in range(n)]
        self.i = 0

    def next(self):
        r = self.b[self.i % len(self.b)]
        self.i += 1
        return r


def build_program(dbg=None, nseq=NBL, phases="AMPF"):
    dbg = dbg or {}
    nc = bass.Bass("TRN2", target_bir_lowering=False)
    IN = "ExternalInput"

    def din(name, shape):
        return nc.dram_tensor(name, list(shape), F32, kind=IN).ap()

    x_d = din("x", [NBL, SEQ, D])
    ctx_d = din("ctx", [NBL, CTXL, D])
    cT_d = din("cT", [128, 8, 3])
    w_ada_d = din("w_ada", [D, 6 * D])
    w_in_d = din("w_in", [D, 6192])
    w_merge_d = din("w_merge", [D, 2 * D])
    w_brs_d = din("w_br_ssd", [D, D])
    w_brg_d = din("w_br_gla", [D, D])
    w_o_d = din("w_o", [D, D])
    w_up_d = din("w_up", [D, 2 * DFF])
    w_down_d = din("w_down", [DFF, D])
    gw_d = din("gw", [17, 1024])
    vecs_d = din("vecs", [128, NV])
    rows_d = din("rows", [1, NR])
    rowsbig_d = din("rowsbig", [1, NRB])
    consts_d = din("consts", [128, NCN])
    out_d = nc.dram_tensor("out", [NBL, SEQ, D], F32, kind="ExternalOutput").ap()
    yo_kind = "ExternalOutput" if dbg.get("dump_yo") else "Internal"
    yo_d = nc.dram_tensor("yo", [NBL, SEQ, 2 * D], F32, kind=yo_kind).ap()
    sx_d = nc.dram_tensor("sx", [NBL, 9, 128, 16 * 256], BF16, kind="Internal").ap()
    sqk_d = nc.dram_tensor("sqk", [NBL, 9, 128, 8 * 256], BF16, kind="Internal").ap()
    sg_d = nc.dram_tensor("sg", [NBL, 9, 32, 256], BF16, kind="Internal").ap()
    hts_d = nc.dram_tensor("hts", [NBL, 128, 8, SEQ], BF16, kind="Internal").ap()
    x1_kind = "ExternalOutput" if dbg.get("dump_x1") else "Internal"
    x1_d = nc.dram_tensor("x1", [NBL, SEQ, D], F32, kind=x1_kind).ap()
    dbg_outs = {}

    es = ExitStack()
    with es:
        p = Prog(nc, es)

        def MM(out, lhsT, rhs, start, stop, reads, writes):
            p.op('pe', lambda e: e.matmul(out, lhsT, rhs, start=start, stop=stop), reads, writes)

        def TR(out, in_, ident, reads, writes):
            p.op('pe', lambda e: e.transpose(out, in_, ident), reads, writes)

        def ACT(out, in_, func, reads, writes, bias=None, scale=None):
            kw = {}
            if bias is not None:
                kw['bias'] = bias
            if scale is not None:
                kw['scale'] = scale
            p.op('act', lambda e: e.activation(out, in_, func, **kw), reads, writes)

        def TT(out, in0, in1, op, reads, writes, eng='dve'):
            p.op(eng, lambda e: e.tensor_tensor(out, in0, in1, op), reads, writes)

        def TS(out, in0, s1, s2, op0, op1, reads, writes, eng='dve'):
            if s2 is None:
                p.op(eng, lambda e: e.tensor_scalar(out, in0, s1, None, op0), reads, writes)
            else:
                p.op(eng, lambda e: e.tensor_scalar(out, in0, s1, s2, op0, op1), reads, writes)

        def STT(out, in0, sc, in1, op0, op1, reads, writes):
            p.op('dve', lambda e: e.scalar_tensor_tensor(out, in0, sc, in1, op0, op1), reads, writes)

        def CP(out, in_, reads, writes, eng='dve'):
            if eng == 'act':
                p.op('act', lambda e: e.activation(out, in_, AF.Identity), reads, writes)
            else:
                p.op(eng, lambda e: e.tensor_copy(out, in_), reads, writes)

        def MEMSET(ap, val, writes, eng='dve'):
            p.op(eng, lambda e: e.memset(ap, val), (), writes)

        def dump(name, res, ap, shape):
            t = nc.dram_tensor("dbg_" + name, list(shape), ap.dtype, kind="ExternalOutput").ap()
            p.dma('sp', t, ap, reads=[res], final=True)
            dbg_outs[name] = t

        pp = [p.psum("pp%d" % i, [128, 1024], F32) for i in range(4)]
        pidx = [0]

        def PS():
            r = pp[pidx[0] % 4]
            pidx[0] += 1
            return r

        consts = p.sbuf("consts", [128, NCN], F32)
        vecs = p.sbuf("vecs", [128, NV], F32)
        rowsb = p.sbuf("rowsb", [128, NR], F32)
        gw = p.sbuf("gw", [17, 1024], BF16)
        constb = p.sbuf("constb", [128, 384], BF16)
        p.dma('sp', consts.t[:], consts_d, writes=[consts])
        p.dma('sp', vecs.t[:], vecs_d, writes=[vecs])
        p.dma('sp', rowsb.t[:], rows_d.partition_broadcast(128), writes=[rowsb])
        p.dma('pool', gw.t[:], gw_d, writes=[gw])
        identf = consts.t[:, K_ID:K_ID + 128]
        onesf = consts.t[:, K_ONE:K_ONE + 128]

        def Lm(d):
            return consts.t[0:64, K_L + 64 * d:K_L + 64 * d + 64]

        def Um(d):
            return consts.t[0:64, K_U + 64 * d:K_U + 64 * d + 64]

        def NMm(d):
            return consts.t[0:64, K_NM + 64 * d:K_NM + 64 * d + 64]

        def Vm(d):
            return consts.t[0:64, K_V + 64 * d:K_V + 64 * d + 64]

        identb = p.sbuf("identb", [128, 128], BF16)
        CP(identb.t[:], identf, [consts], [identb])
        CP(constb.t[:], consts.t[:, K_ONE:K_ONE + 384], [consts], [constb])

        def Lb(d):
            return constb.t[0:64, 128 + 64 * d:128 + 64 * d + 64]

        def Ub(d):
            return constb.t[0:64, 256 + 64 * d:256 + 64 * d + 64]
        aneg = p.sbuf("aneg", [64, 32], F32)
        ACT(aneg.t[:], rowsb.t[0:64, R_ALOG:R_ALOG + 32], AF.Exp, [rowsb], [aneg])
        TS(aneg.t[:], aneg.t[:], -1.0, None, ALU.mult, None, [aneg], [aneg])
        dkd = p.sbuf("dkd", [64, 1024], BF16)
        TT(dkd.t[:].rearrange("p (h q) -> p h q", h=16),
           consts.t[0:64, K_ID:K_ID + 64].unsqueeze(1).to_broadcast([64, 16, 64]),
           rowsb.t[0:64, R_DSK:R_DSK + 16].unsqueeze(2).to_broadcast([64, 16, 64]),
           ALU.mult, [consts, rowsb], [dkd])

        modT = p.sbuf("modT", [128, 48, 3], F32)
        scT = p.sbuf("scT", [128, 8, 3], BF16)
        A1 = p.sbuf("A1", [128, 8, 3], F32)
        A2 = p.sbuf("A2", [128, 8, 3], F32)

        with ExitStack() as esA:
            p.es = esA
            cT = p.sbuf("cT", [128, 8, 3], F32)
            p.dma('sp', cT.t[:], cT_d, writes=[cT])
            ACT(scT.t[:], cT.t[:], AF.Silu, [cT], [scT])
            wrot = Rot(p, "wada", [128, 8, 512], BF16, 2)
            mps = PS()
            for nb in range(12):
                wb = wrot.next()
                for dk in range(8):
                    p.dma('pool', wb.t[:, dk, :], w_ada_d[dk * 128:(dk + 1) * 128, nb * 512:(nb + 1) * 512], writes=[wb])
                for cc in range(4):
                    j = nb * 4 + cc
                    for dk in range(8):
                        MM(mps.t[:, j * 4:j * 4 + 3], wb.t[:, dk, cc * 128:(cc + 1) * 128], scT.t[:, dk, :],
                           dk == 0, dk == 7, [wb, scT], [mps])
            TT(modT.t[:], mps.t[:, 0:192].rearrange("p (j r) -> p j r", r=4)[:, :, 0:3],
               vecs.t[:, V_BADA:V_BADA + 48].unsqueeze(2).to_broadcast([128, 48, 3]), ALU.add, [mps, vecs], [modT])
            for (A, vg, so) in ((A1, V_N1G, 8), (A2, V_N2G, 32)):
                TS(A.t[:], modT.t[:, so:so + 8, :], 1.0, None, ALU.add, None, [modT], [A])
                TT(A.t[:], A.t[:], vecs.t[:, vg:vg + 8].unsqueeze(2).to_broadcast([128, 8, 3]), ALU.mult, [A, vecs], [A])
            p.es = es
        p.barrier()
        if dbg.get("dump_mod"):
            dump("modT", modT, modT.t[:], [128, 48, 3])
            dump("A1", A1, A1.t[:], [128, 8, 3])

        def norm_to_T(xt, xtr, A, Bap_fn, r, dst, dst_tok0, tmp):
            sq, sqr, ssq, xn, xnr = tmp
            ACT(sq, xt, AF.Square, [xtr], [sqr])
            p.op('dve', lambda e: e.reduce_sum(ssq.t[:, 0:1], sq, AX.X), [sqr], [ssq])
            ACT(ssq.t[:, 1:2], ssq.t[:, 0:1], AF.Sqrt, [ssq], [ssq], bias=EPS_AP[0], scale=1.0 / D)
            p.op('dve', lambda e: e.reciprocal(ssq.t[:, 2:3], ssq.t[:, 1:2]), [ssq], [ssq])
            TS(xn, xt, ssq.t[:, 2:3], None, ALU.mult, None, [xtr, ssq], [xnr])
            ps = PS()
            for j in range(8):
                TR(ps.t[:, j * 128:(j + 1) * 128], xn[:, j * 128:(j + 1) * 128], identf, [xnr, consts], [ps])
            for j in range(8):
                if j % 2 == 0:
                    TS(dst.t[:, j, dst_tok0:dst_tok0 + 128], ps.t[:, j * 128:(j + 1) * 128],
                       A.t[:, j, r:r + 1], Bap_fn(j, r), ALU.mult, ALU.add, [ps, A, modT], [dst])
                else:
                    ACT(dst.t[:, j, dst_tok0:dst_tok0 + 128], ps.t[:, j * 128:(j + 1) * 128], AF.Identity,
                        [ps, A, modT], [dst], bias=Bap_fn(j, r), scale=A.t[:, j, r:r + 1])

        epsb = p.sbuf("epsb", [128, 1], F32)
        MEMSET(epsb.t[:], EPS, [epsb])
        EPS_AP = [epsb.t[:, 0:1]]

        if "M" in phases:
          with ExitStack() as esM:
            p.es = esM
            wmix = p.sbuf("wmix", [128, 8, WMN], BF16)
            nmrep = p.sbuf("nmrep", [64, 2, 1024], BF16)
            for d_ in range(2):
                CP(nmrep.t[:, d_, :].rearrange("p (h q) -> p h q", h=16),
                   consts.t[0:64, K_NM + 64 * d_:K_NM + 64 * d_ + 64].unsqueeze(1).to_broadcast([64, 16, 64]), [consts], [nmrep])

            for dk in range(8):
                p.dma('pool', wmix.t[:, dk, :], w_in_d[dk * 128:(dk + 1) * 128, WM0:WM0 + WMN], writes=[wmix])
            hT = p.sbuf("hT", [128, 8, NTOK], BF16)
            scr8 = p.sbuf("scr8", [128, 2048], F32)
            sq = p.sbuf("sq", [128, 1024], BF16)
            ssq = p.sbuf("ssq", [128, 4], F32)
            Hs = p.sbuf("Hs", [128, 1024], F32)
            Hb = p.sbuf("Hb", [128, 1024], BF16)
            Ss = p.sbuf("Ss", [128, 1024], F32)
            Sb = p.sbuf("Sb", [128, 1024], BF16)
            xbcT = p.sbuf("xbcT", [128, 16, 256], BF16)
            qkT = p.sbuf("qkT", [128, 8, 256], BF16)
            glrT = p.sbuf("glrT", [32, 256], BF16)
            MEMSET(glrT.t[:], 1.0, [glrT])
            accr = Rot(p, "acc", [128, 256], F32, 2)
            dts_r = Rot(p, "dts", [64, 32], F32, 2)
            da_r = Rot(p, "da", [64, 16], F32, 2)
            cum_r = Rot(p, "cum_sb", [64, 16], F32, 2)
            dtw_r = Rot(p, "dtw", [64, 16], F32, 2)
            ecum_r = Rot(p, "ecum", [64, 16], F32, 2)
            eL_r = Rot(p, "eL", [128, 16], F32, 2)
            dahl_r = Rot(p, "dahl", [64, 32], BF16, 2)
            nLb = p.sbuf("nLb", [64, 2, 64], BF16)
            for d_ in range(2):
                TS(nLb.t[:, d_, :], consts.t[0:64, K_L + 64 * d_:K_L + 64 * d_ + 64], -1.0, None, ALU.mult, None, [consts], [nLb])
            dtbhl = p.sbuf("dtbhl", [1, 64], BF16)
            CP(dtbhl.t[0:1, 0:32], rowsb.t[0:1, R_DTB:R_DTB + 32], [rowsb], [dtbhl])
            TT(dtbhl.t[0:1, 32:64], rowsb.t[0:1, R_DTB:R_DTB + 32], dtbhl.t[0:1, 0:32], ALU.subtract, [rowsb, dtbhl], [dtbhl])
            nlm = p.sbuf("nlm", [64, 2, 64], BF16)
            for d_ in range(2):
                mid_ = 31 if d_ == 0 else 32
                TS(nlm.t[:, d_, :], consts.t[0:64, K_L + 64 * d_ + mid_:K_L + 64 * d_ + mid_ + 1].to_broadcast([64, 64]),
                   -1.0, None, ALU.mult, None, [consts], [nlm])
            ndahl_r = Rot(p, "ndahl", [64, 32], BF16, 2)
            seg = p.sbuf("seg", [64, 1024], F32)
            MTt_r = Rot(p, "MTt", [64, 1024], BF16, 2)
            cmc_r = Rot(p, "cmc", [128, 256], BF16, 2)
            xsb_r = Rot(p, "xsb", [64, 1536], BF16, 2)
            xd_r = Rot(p, "xd", [64, 1024], BF16, 2)
            xdw_r = Rot(p, "xdw", [64, 1024], BF16, 2)
            ybufs = [Res("yb0", scr8.t[0:64, 0:1024]), Res("yb1", scr8.t[0:64, 1024:2048])]
            ycnt = [0]
            obuf_r = Rot(p, "obuf", [64, 1024], F32, 2)
            la_r = Rot(p, "la", [64, 512], F32, 1)
            lah_r = Rot(p, "lah", [64, 512], BF16, 1)
            lal_r = Rot(p, "lal", [64, 512], BF16, 1)
            ref = p.sbuf("ref", [128, 4], F32)
            dl = p.sbuf("dl", [128, 256], F32)
            eq = p.sbuf("eq", [128, 256], F32)
            ek = p.sbuf("ek", [128, 256], F32)
            ec = p.sbuf("ec", [128, 256], F32)
            eT_r = Rot(p, "eT", [128, 4], F32, 2)
            erc = p.sbuf("erc", [64, 512], F32)
            vsb_r = Rot(p, "vsb", [64, 1024], BF16, 2)
            kdec_r = Rot(p, "kdec", [64, 512], BF16, 2)
            qdT_r = Rot(p, "qdT", [128, 256], BF16, 2)
            kdT_r = Rot(p, "kdT", [128, 256], BF16, 2)
            qeT_r = Rot(p, "qeT", [128, 256], BF16, 2)
            scTt = p.sbuf("scTt", [64, 256], BF16)

            def B1fn(j, r):
                return modT.t[:, j, r:r + 1]

            small_pssB = Res("pssB", pp[1].t[:, 0:512])
            small_pscB = Res("pscB", pp[1].t[:, 512:1024])
            chunk_par = [0]
            pj = [0]

            def PSJ():
                r = pp[(0, 2, 3)[pj[0] % 3]]
                pj[0] += 1
                return r

            def proj_super(tok0, lo, hi):
                T = 256
                a = max(tok0 - 1, lo)
                e_ = min(tok0 + T + 1, hi)
                n = e_ - a
                off = a - (tok0 - 1)
                for cc in range(16):
                    ps = PSJ()
                    for dk in range(8):
                        MM(ps.t[:, off:off + n], wmix.t[:, dk, O_XBC + cc * 128:O_XBC + (cc + 1) * 128],
                           hT.t[:, dk, a:e_], dk == 0, dk == 7, [wmix, hT], [ps])
                    acc = accr.next()
                    cwb = V_CW + cc * 3
                    ACT(acc.t[:], ps.t[:, 1:257], AF.Identity, [ps, vecs], [acc],
                        bias=vecs.t[:, V_CB + cc:V_CB + cc + 1], scale=vecs.t[:, cwb + 1:cwb + 2])
                    i0 = 1 if off == 1 else 0
                    STT(acc.t[:, i0:256], ps.t[:, i0:256], vecs.t[:, cwb:cwb + 1], acc.t[:, i0:256],
                        ALU.mult, ALU.add, [ps, vecs, acc], [acc])
                    i1 = 255 if e_ < tok0 + T + 1 else 256
                    STT(acc.t[:, 0:i1], ps.t[:, 2:2 + i1], vecs.t[:, cwb + 2:cwb + 3], acc.t[:, 0:i1],
                        ALU.mult, ALU.add, [ps, vecs, acc], [acc])
                    ACT(xbcT.t[:, cc, :], acc.t[:], AF.Silu, [acc], [xbcT])
                for j in range(8):
                    ps = PSJ()
                    for dk in range(8):
                        MM(ps.t[:, 0:256], wmix.t[:, dk, O_Q + j * 128:O_Q + (j + 1) * 128],
                           hT.t[:, dk, tok0:tok0 + 256], dk == 0, dk == 7, [wmix, hT], [ps])
                    ACT(qkT.t[:, j, :], ps.t[:, 0:256], AF.Identity, [ps], [qkT],
                        scale=(128.0 ** -0.5) if j < 4 else 1.0)
                ps = PSJ()
                for dk in range(8):
                    MM(ps.t[0:16, 0:256], wmix.t[:, dk, O_G:O_G + 16], hT.t[:, dk, tok0:tok0 + 256],
                       dk == 0, dk == 7, [wmix, hT], [ps])
                CP(glrT.t[0:16, :], ps.t[0:16, 0:256], [ps], [glrT])

            def chunk_head(b, d, t0, cl, is_ctx):
                dsl = slice(d * 16, d * 16 + 16)
                dts, da, cum_sb, dtw, ecum, eL = dts_r.next(), da_r.next(), cum_r.next(), dtw_r.next(), ecum_r.next(), eL_r.next()
                xsb, xd, xdw = xsb_r.next(), xd_r.next(), xdw_r.next()
                la, vsb, kdec = la_r.next(), vsb_r.next(), kdec_r.next()
                lah, lal, dahl = lah_r.next(), lal_r.next(), dahl_r.next()
                ybuf = obuf = None
                if not is_ctx:
                    ybuf = ybufs[ycnt[0] % 2]
                    ycnt[0] += 1
                    obuf = obuf_r.next()
                par = chunk_par[0] % 2
                chunk_par[0] += 1
                pss = small_pssB
                psc = small_pscB
                sps = small_pssB
                h16 = lambda ap: ap.rearrange("p (h q) -> p h q", h=16)
                h4 = lambda ap: ap.rearrange("p (h q) -> p h q", h=4)
                lps = pp[0]
                MM(lps.t[0:64, 0:512], glrT.t[0:17, cl:cl + 64], gw.t[0:17, d * 512:(d + 1) * 512], True, True, [glrT, gw], [lps])
                ACT(la.t[:], lps.t[0:64, 0:512], AF.Exp, [lps], [la], scale=-1.0)
                ACT(la.t[:], la.t[:], AF.Ln, [la], [la], bias=1.0)
                CP(lah.t[:], la.t[:], [la], [lah], eng='act')
                TT(lal.t[:], la.t[:], lah.t[:], ALU.subtract, [la, lah], [lal])
                for dk in range(8):
                    MM(pss.t[0:64, 0:32], hT.t[:, dk, t0:t0 + 64], wmix.t[:, dk, O_DT:O_DT + 32],
                       dk == 0, False, [wmix, hT], [pss])
                MM(pss.t[0:64, 0:32], constb.t[0:1, 0:64], dtbhl.t[0:1, 0:32], False, False, [constb, dtbhl], [pss])
                MM(pss.t[0:64, 0:32], constb.t[0:1, 0:64], dtbhl.t[0:1, 32:64], False, True, [constb, dtbhl], [pss])
                ACT(dts.t[:], pss.t[0:64, 0:32], AF.Exp, [pss], [dts])
                ACT(dts.t[:], dts.t[:], AF.Ln, [dts], [dts], bias=1.0)
                TT(da.t[:], dts.t[:, dsl], aneg.t[:, dsl], ALU.mult, [dts, aneg], [da])
                CP(dahl.t[:, 0:16], da.t[:], [da], [dahl])
                TT(dahl.t[:, 16:32], da.t[:], dahl.t[:, 0:16], ALU.subtract, [da, dahl], [dahl])
                ndahl = None
                vps = pp[2]
                for nb in range(2):
                    for dk in range(8):
                        MM(vps.t[0:64, nb * 512:(nb + 1) * 512], hT.t[:, dk, t0:t0 + 64],
                           wmix.t[:, dk, O_V + nb * 512:O_V + (nb + 1) * 512], dk == 0, dk == 7, [wmix, hT], [vps])
                CP(vsb.t[:], vps.t[0:64, :], [vps], [vsb], eng='act')
                return dict(dts=dts, da=da, cum_sb=cum_sb, dtw=dtw, ecum=ecum, eL=eL, xsb=xsb, xd=xd, xdw=xdw, la=la, vsb=vsb,
                            kdec=kdec, lah=lah, lal=lal, dahl=dahl, ndahl=ndahl, ybuf=ybuf, obuf=obuf, pss=pss, psc=psc, sps=sps, vps=vps, lps=lps)

            def chunk_mid(b, d, t0, cl, is_ctx, hd):
                dsl = slice(d * 16, d * 16 + 16)
                h16 = lambda ap: ap.rearrange("p (h q) -> p h q", h=16)
                h4 = lambda ap: ap.rearrange("p (h q) -> p h q", h=4)
                dts, da, cum_sb, dtw, ecum, eL = hd["dts"], hd["da"], hd["cum_sb"], hd["dtw"], hd["ecum"], hd["eL"]
                xsb, xd, xdw, la, vsb, kdec = hd["xsb"], hd["xd"], hd["xdw"], hd["la"], hd["vsb"], hd["kdec"]
                lah, lal, dahl, ybuf, obuf = hd["lah"], hd["lal"], hd["dahl"], hd["ybuf"], hd["obuf"]
                ndahl = hd["ndahl"]
                pss, psc, sps = hd["pss"], hd["psc"], hd["sps"]
                MTt, cmc, eT, qdT, kdT, qeT = MTt_r.next(), cmc_r.next(), eT_r.next(), qdT_r.next(), kdT_r.next(), qeT_r.next()
                hd.update(MTt=MTt, cmc=cmc, eT=eT, qdT=qdT, kdT=kdT, qeT=qeT)
                if not is_ctx:
                    CP(cmc.t[:].rearrange("p (g q) -> p g q", g=4), xbcT.t[:, 12:16, cl:cl + 64], [xbcT], [cmc], eng='pool')
                for (oc, lt) in ((slice(32, 48), Lb(d)), (slice(48, 64), Ub(d))):
                    MM(pss.t[0:64, oc], lt, dahl.t[:, 0:16], True, False, [constb, dahl], [pss])
                    MM(pss.t[0:64, oc], lt, dahl.t[:, 16:32], False, True, [constb, dahl], [pss])
                MM(pss.t[:, 64:80], constb.t[0:64, 0:128], dahl.t[:, 0:16], True, False, [constb, dahl], [pss])
                MM(pss.t[:, 64:80], constb.t[0:64, 0:128], dahl.t[:, 16:32], False, True, [constb, dahl], [pss])
                psx = pp[3]
                psxb = psx.t[:].bitcast(BF16)
                for j in range(12):
                    TR(psxb[0:64, j * 128:(j + 1) * 128], xbcT.t[:, j, cl:cl + 64], identb.t[:], [xbcT, identb], [psx])
                CP(xsb.t[:], psxb[0:64, 0:1536], [psx], [xsb], eng='act')
                if not is_ctx:
                    cq = pp[0]
                    for hb in range(2):
                        MM(cq.t[0:64, hb * 512:(hb + 1) * 512], identb.t[0:64, 0:64], nmrep.t[:, d, hb * 512:(hb + 1) * 512],
                           True, False, [identb, nmrep], [cq])
                    for h in range(16):
                        for part in range(2):
                            MM(cq.t[0:64, h * 64:(h + 1) * 64], dahl.t[:, part * 16 + h:part * 16 + h + 1].to_broadcast([64, 64]),
                               Lb(d), False, False, [dahl, constb], [cq])
                        for part in range(2):
                            MM(cq.t[0:64, h * 64:(h + 1) * 64], nLb.t[:, d, :],
                               dahl.t[:, part * 16 + h:part * 16 + h + 1].to_broadcast([64, 64]),
                               False, (h % 8 == 7) and part == 1, [dahl, nLb], [cq])
                cps = pp[2]
                for h in range(4):
                    MM(cps.t[:, h * 64:(h + 1) * 64], lah.t[:, h * 128:(h + 1) * 128], Lb(d), True, False, [lah, constb], [cps])
                    MM(cps.t[:, h * 64:(h + 1) * 64], lal.t[:, h * 128:(h + 1) * 128], Lb(d), False, True, [lal, constb], [cps])
                for h in range(4):
                    co = slice(256 + h * 64, 256 + (h + 1) * 64)
                    MM(cps.t[:, co], lah.t[:, h * 128:(h + 1) * 128], Lb(d), True, False, [lah, constb], [cps])
                    MM(cps.t[:, co], lal.t[:, h * 128:(h + 1) * 128], Lb(d), False, False, [lal, constb], [cps])
                    MM(cps.t[:, co], lah.t[:, h * 128:(h + 1) * 128], nlm.t[:, d, :], False, False, [lah, nlm], [cps])
                    MM(cps.t[:, co], lal.t[:, h * 128:(h + 1) * 128], nlm.t[:, d, :], False, True, [lal, nlm], [cps])
                MM(cps.t[0:64, 512:1024], Ub(d), lah.t[:], True, False, [lah, constb], [cps])
                MM(cps.t[0:64, 512:1024], Ub(d), lal.t[:], False, True, [lal, constb], [cps])
                kps = pp[3]
                kpsb = kps.t[:].bitcast(BF16)
                for h in range(4):
                    TR(kpsb[0:64, h * 128:(h + 1) * 128], qkT.t[:, 4 + h, cl:cl + 64], identb.t[:], [qkT, identb], [kps])
                ACT(dtw.t[:], pss.t[0:64, 48:64], AF.Exp, [pss], [dtw])
                TT(dtw.t[:], dtw.t[:], dts.t[:, dsl], ALU.mult, [dtw, dts], [dtw])
                ACT(eL.t[:], pss.t[:, 64:80], AF.Exp, [pss], [eL])
                TT(h16(xd.t[:]), h16(xsb.t[:, 0:1024]), dts.t[:, dsl].unsqueeze(2).to_broadcast([64, 16, 64]), ALU.mult, [xsb, dts], [xd])
                TT(h16(xdw.t[:]), h16(xsb.t[:, 0:1024]), dtw.t[:].unsqueeze(2).to_broadcast([64, 16, 64]), ALU.mult, [xsb, dtw], [xdw], eng='pool')
                if not is_ctx:
                    ACT(seg.t[:], cq.t[0:64, :], AF.Exp, [cq], [seg])
                    ACT(ecum.t[:], pss.t[0:64, 32:48], AF.Exp, [pss], [ecum])
                    for g in range(4):
                        MM(psc.t[0:64, g * 64:(g + 1) * 64], xbcT.t[:, 8 + g, cl:cl + 64], xbcT.t[:, 12 + g, cl:cl + 64],
                           True, True, [xbcT], [psc])
                if not is_ctx:
                    TT(MTt.t[:].rearrange("p (g r q) -> p g r q", g=4, r=4),
                       seg.t[:].rearrange("p (g r q) -> p g r q", g=4, r=4),
                       h4(psc.t[0:64, 0:256]).unsqueeze(2).to_broadcast([64, 4, 4, 64]),
                       ALU.mult, [seg, psc], [MTt])
                cv = h4(cps.t[:, 0:256])
                mid = 31 if d == 0 else 32
                last = 63 if d == 0 else 0
                ACT(erc.t[:], cps.t[0:64, 512:1024], AF.Exp, [cps], [erc], scale=-1.0 / 16)
                ACT(ek.t[:], cps.t[:, 256:512], AF.Exp, [cps], [ek], scale=1.0 / 16)
                ACT(eT.t[:], cv[:, :, last], AF.Exp, [cps], [eT], scale=-1.0 / 16)
                TT(kdec.t[:], kpsb[0:64, 0:512], erc.t[:], ALU.mult, [kps, erc], [kdec])
                TT(h4(kdT.t[:]), qkT.t[:, 4:8, cl:cl + 64], h4(ek.t[:]), ALU.mult, [qkT, ek], [kdT], eng='pool')
                if not is_ctx:
                    ACT(eq.t[:], cps.t[:, 256:512], AF.Exp, [cps], [eq], scale=-1.0 / 16)
                    ACT(ec.t[:], cps.t[:, 0:256], AF.Exp, [cps], [ec], scale=-1.0 / 16)
                    TT(h4(qdT.t[:]), qkT.t[:, 0:4, cl:cl + 64], h4(eq.t[:]), ALU.mult, [qkT, eq], [qdT], eng='pool')
                    TT(h4(qeT.t[:]), qkT.t[:, 0:4, cl:cl + 64], h4(ec.t[:]), ALU.mult, [qkT, ec], [qeT], eng='pool')
                return hd

            def chunk_fin(b, d, t0, cl, is_ctx, hd):
                dsl = slice(d * 16, d * 16 + 16)
                h16 = lambda ap: ap.rearrange("p (h q) -> p h q", h=16)
                h4 = lambda ap: ap.rearrange("p (h q) -> p h q", h=4)
                ecum, eL, xsb, xd, xdw, vsb, kdec = hd["ecum"], hd["eL"], hd["xsb"], hd["xd"], hd["xdw"], hd["vsb"], hd["kdec"]
                ybuf, obuf, sps = hd["ybuf"], hd["obuf"], hd["sps"]
                MTt, cmc, eT, qdT, kdT, qeT = hd["MTt"], hd["cmc"], hd["eT"], hd["qdT"], hd["kdT"], hd["qeT"]
                TT(h16(Hs.t[:]), h16(Hs.t[:]), eL.t[:].unsqueeze(2).to_broadcast([128, 16, 64]), ALU.mult, [Hs, eL], [Hs], eng='pool')
                if not is_ctx:
                    for h in range(4):
                        hs = slice(h * 64, h * 64 + 64)
                        MM(sps.t[0:64, 256 + h * 64:256 + (h + 1) * 64], kdT.t[:, hs], qdT.t[:, hs], True, True, [kdT, qdT], [sps])
                    TT(h4(scTt.t[:]), h4(sps.t[0:64, 256:512]), Vm(d).unsqueeze(1).to_broadcast([64, 4, 64]), ALU.mult, [sps, consts], [scTt])
                    yi = pp[0]
                    for h in range(16):
                        hs = slice(h * 64, h * 64 + 64)
                        MM(yi.t[0:64, hs], MTt.t[:, hs], xd.t[:, hs], True, d == 1, [MTt, xd], [yi])
                        if d == 0:
                            MM(yi.t[0:64, hs], dkd.t[:, hs], xsb.t[:, hs], False, True, [dkd, xsb], [yi])
                    yh = pp[2]
                    for g in range(4):
                        gs = slice(g * 256, g * 256 + 256)
                        MM(yh.t[0:64, gs], cmc.t[:, g * 64:(g + 1) * 64], Hb.t[:, gs], True, True, [cmc, Hb], [yh])
                scp = pp[3]
                for h in range(16):
                    g = h // 4
                    hs = slice(h * 64, h * 64 + 64)
                    MM(scp.t[:, hs], xsb.t[:, 1024 + g * 128:1024 + (g + 1) * 128], xdw.t[:, hs], True, True, [xsb, xdw], [scp])
                if not is_ctx:
                    TT(h16(ybuf.t[:]), h16(yh.t[0:64, :]), ecum.t[:].unsqueeze(2).to_broadcast([64, 16, 64]), ALU.mult, [yh, ecum], [ybuf])
                    ops_ = pp[2]
                    for h in range(4):
                        hs = slice(h * 64, h * 64 + 64)
                        vs = slice(h * 256, h * 256 + 256)
                        MM(ops_.t[0:64, vs], scTt.t[:, hs], vsb.t[:, vs], True, False, [scTt, vsb], [ops_])
                        MM(ops_.t[0:64, vs], qeT.t[:, hs], Sb.t[:, vs], False, True, [qeT, Sb], [ops_])
                TT(Hs.t[:], Hs.t[:], scp.t[:], ALU.add, [Hs, scp], [Hs])
                CP(Hb.t[:], Hs.t[:], [Hs], [Hb], eng='act')
                sgp = pp[3]
                for h in range(4):
                    vs = slice(h * 256, h * 256 + 256)
                    MM(sgp.t[:, vs], kdec.t[:, h * 128:(h + 1) * 128], vsb.t[:, vs], True, True, [kdec, vsb], [sgp])
                if not is_ctx:
                    TT(ybuf.t[:], ybuf.t[:], yi.t[0:64, :], ALU.add, [ybuf, yi], [ybuf])
                    CP(obuf.t[:], ops_.t[0:64, :], [ops_], [obuf], eng='act')
                for h in range(4):
                    vs = slice(h * 256, h * 256 + 256)
                    STT(Ss.t[:, vs], Ss.t[:, vs], eT.t[:, h:h + 1], sgp.t[:, vs], ALU.mult, ALU.add, [Ss, eT, sgp], [Ss])
                CP(Sb.t[:], Ss.t[:], [Ss], [Sb], eng='act')
                if not is_ctx:
                    tx = t0 - CTXL
                    nm = "yo%d_%d" % (b, tx)
                    if d == 0:
                        p.dma('sp', yo_d[b, tx:tx + 64, 0:1024], ybuf.t[:], reads=[ybuf], dram_w=[nm + "y"])
                        p.dma('sp', yo_d[b, tx:tx + 64, 1024:2048], obuf.t[:], reads=[obuf], dram_w=[nm + "o"])
                    else:
                        p.dma('pool', yo_d[b, tx:tx + 64, 0:1024], ybuf.t[:], reads=[ybuf], dram_r=[nm + "y"], dram_w=[nm + "y"], accum=ALU.add)
                        p.dma('pool', yo_d[b, tx:tx + 64, 1024:2048], obuf.t[:], reads=[obuf], dram_r=[nm + "o"], dram_w=[nm + "o"], accum=ALU.add)

            nsc = dbg.get("nsc", 8)
            for b in range(nseq):
                for i in range(2 + 2 * nsc):
                    xt = scr8.t[:, 0:1024]
                    if i < 2:
                        p.dma('sp', xt, ctx_d[b, i * 128:(i + 1) * 128, :], writes=[scr8])
                        r = 2
                    else:
                        p.dma('sp', xt, x_d[b, (i - 2) * 128:(i - 1) * 128, :], writes=[scr8])
                        r = b
                    norm_to_T(xt, scr8, A1, B1fn, r, hT, i * 128, (sq.t[:], sq, ssq, scr8.t[:, 1024:2048], scr8))
                    if i >= 2:
                        p.dma('sp', hts_d[b, :, :, (i - 2) * 128:(i - 1) * 128], hT.t[:, :, i * 128:(i + 1) * 128],
                              reads=[hT], dram_w=["hts%d_%d" % (b, (i - 2) * 128)])
                if dbg.get("dump_hT") and b == 0:
                    dump("hT", hT, hT.t[:], [128, 8, NTOK])
                xhi = CTXL + 256 * nsc
                p.barrier()
                for d in range(2):
                    for st in (Hs, Ss):
                        MEMSET(st.t[:], 0.0, [st])
                    for st in (Hb, Sb):
                        MEMSET(st.t[:], 0.0, [st])
                    supers = [(0, 0, CTXL, True)] + [(CTXL + 256 * i, CTXL, xhi, False) for i in range(nsc)]
                    if d == 1:
                        supers = [supers[0]] + supers[1:][::-1]
                    chunks = []
                    for si, (tok0, lo, hi, is_ctx) in enumerate(supers):
                        cs = list(range(4) if d == 0 else range(3, -1, -1))
                        for ci, c in enumerate(cs):
                            chunks.append((tok0, lo, hi, is_ctx, c, ci == 0))

                    def prep(k):
                        tok0, lo, hi, is_ctx, c, first = chunks[k]
                        if first:
                            si = 0 if is_ctx else 1 + (tok0 - CTXL) // 256
                            nm = "sv%d_%d" % (b, si)
                            sxv = sx_d[b, si].rearrange("p (a t) -> p a t", a=16)
                            sqv = sqk_d[b, si].rearrange("p (a t) -> p a t", a=8)
                            if d == 0:
                                proj_super(tok0, lo, hi)
                                p.dma('sp', sxv, xbcT.t[:], reads=[xbcT], dram_w=[nm + "x"])
                                p.dma('sp', sqv, qkT.t[:], reads=[qkT], dram_w=[nm + "q"])
                                p.dma('sp', sg_d[b, si], glrT.t[:], reads=[glrT], dram_w=[nm + "g"])
                            else:
                                p.dma('sp', xbcT.t[:], sxv, writes=[xbcT], dram_r=[nm + "x"])
                                p.dma('sp', qkT.t[:], sqv, writes=[qkT], dram_r=[nm + "q"])
                                p.dma('sp', glrT.t[:], sg_d[b, si], writes=[glrT], dram_r=[nm + "g"])
                            if dbg.get("dump_xbc") and b == 0 and d == 0 and tok0 == dbg["dump_xbc"]:
                                dump("xbcT", xbcT, xbcT.t[:], [128, 16, 256])
                                dump("qkT", qkT, qkT.t[:], [128, 8, 256])
                        hd = chunk_head(b, d, tok0 + 64 * c, 64 * c, is_ctx)
                        return chunk_mid(b, d, tok0 + 64 * c, 64 * c, is_ctx, hd)

                    hd_cur = prep(0)
                    for k in range(len(chunks)):
                        hd_nxt = prep(k + 1) if k + 1 < len(chunks) else None
                        tok0, lo, hi, is_ctx, c, first = chunks[k]
                        chunk_fin(b, d, tok0 + 64 * c, 64 * c, is_ctx, hd_cur)
                        hd_cur = hd_nxt
                    if dbg.get("dump_state") and b == 0:
                        dump("H%d" % d, Hs, Hs.t[:], [128, 1024])
                        dump("S%d" % d, Ss, Ss.t[:], [128, 1024])
                p.barrier()
            p.es = es
          p.barrier()

        build_rest(nc, p, locals())
        p.finish()
    return nc, dbg_outs


def build_rest(nc, p, L):
    import types
    N = types.SimpleNamespace(**L)
    es, dbg, phases, nseq = N.es, N.dbg, N.phases, N.nseq
    MM, TR, ACT, TT, TS, STT, CP, MEMSET, PS, dump = N.MM, N.TR, N.ACT, N.TT, N.TS, N.STT, N.CP, N.MEMSET, N.PS, N.dump
    consts, vecs, modT, scT, A2, identf = N.consts, N.vecs, N.modT, N.scT, N.A2, N.identf
    x_d, yo_d, x1_d, out_d = N.x_d, N.yo_d, N.x1_d, N.out_d
    A1 = N.A1

    def load_w(dst, src_ap, nk):
        for dk in range(nk):
            p.dma('pool', dst.t[:, dk, :], src_ap[dk * 128:(dk + 1) * 128, :], writes=[dst])

    def compute_G(G, which, rb):
        with ExitStack() as esg:
            p.es = esg
            scR = p.sbuf("scR", [128, 8, 128], BF16)
            wb = p.sbuf("wgb", [128, 8, 1024], BF16)
            rb_t = p.sbuf("rbt", [128, 1024], F32)
            p.dma('sp', rb_t.t[:], N.rowsbig_d[0:1, rb:rb + 1024].partition_broadcast(128), writes=[rb_t])
            load_w(wb, N.w_ada_d[:, which * 1024:(which + 1) * 1024], 8)
            for b in range(NBL):
                CP(scR.t[:], scT.t[:, :, b:b + 1].to_broadcast([128, 8, 128]), [scT], [scR])
                gps = PS()
                for nb in range(2):
                    for dk in range(8):
                        MM(gps.t[:, nb * 512:(nb + 1) * 512], scR.t[:, dk, :], wb.t[:, dk, nb * 512:(nb + 1) * 512],
                           dk == 0, dk == 7, [wb, scR], [gps])
                TT(G.t[:, b, :], gps.t[:], rb_t.t[:], ALU.add, [gps, rb_t], [G])
            p.es = es
        p.barrier()

    def rstd_of(ssq_in, out, n, reads, writes):
        ACT(out, ssq_in, AF.Sqrt, reads, writes, bias=N.epsb.t[:, 0:1], scale=1.0 / n)
        p.op('dve', lambda e: e.reciprocal(out, out), writes, writes)

    if "P" in phases:
      with ExitStack() as esP:
        p.es = esP
        G1 = p.sbuf("G1", [128, NBL, 1024], F32)
        compute_G(G1, 2, RB_BA2)
        p.es = esP
        wzr = p.sbuf("wzr", [128, 8, 2048], BF16)
        wmg = p.sbuf("wmg", [128, 8, 2048], BF16)
        wbs = p.sbuf("wbs", [128, 8, 1024], BF16)
        wbg = p.sbuf("wbg", [128, 8, 1024], BF16)
        wo = p.sbuf("wo", [128, 8, 1024], BF16)
        for dk in range(8):
            p.dma('pool', wzr.t[:, dk, 0:1024], N.w_in_d[dk * 128:(dk + 1) * 128, 0:1024], writes=[wzr])
            p.dma('pool', wzr.t[:, dk, 1024:2048], N.w_in_d[dk * 128:(dk + 1) * 128, 5168:6192], writes=[wzr])
        load_w(wmg, N.w_merge_d, 8)
        load_w(wbs, N.w_brs_d, 8)
        load_w(wbg, N.w_brg_d, 8)
        load_w(wo, N.w_o_d, 8)
        for b in range(nseq):
            for q4 in range(4):
                p.dma('sp', x1_d[b, q4 * 512:(q4 + 1) * 512, :], x_d[b, q4 * 512:(q4 + 1) * 512, :],
                      dram_w=["x1_%d_%d" % (b, q4 * 512 + k * 128) for k in range(4)])
        TW = 256
        NS = TW // 128
        hT4s = [p.sbuf("hT4_%d" % i, [128, 8, TW], BF16) for i in range(2)]
        yT4s = [p.sbuf("yT4_%d" % i, [128, 8, TW], BF16) for i in range(2)]
        oT4s = [p.sbuf("oT4_%d" % i, [128, 8, TW], BF16) for i in range(2)]
        mT4s = [p.sbuf("mT4_%d" % i, [128, 8, TW], BF16) for i in range(2)]
        gT_r = Rot(p, "gT", [128, 2, TW], BF16, 2)
        m12_r = Rot(p, "m12", [128, 2 * TW], F32, 1)
        zrs = [p.sbuf("zr%d" % i, [128, 2048], BF16) for i in range(2)]
        yots = [p.sbuf("yot%d" % i, [128, 2048], F32) for i in range(2)]
        scr8 = p.sbuf("scr8p", [128, 2048], F32)
        tbuf = p.sbuf("tbuf", [128, 1024], F32)
        sq = p.sbuf("sqp", [128, 1024], BF16)
        sq2 = sq
        ssq = p.sbuf("ssqp", [128, 4], F32)
        so = p.sbuf("so", [128, 8], F32)

        def B1fn(j, r):
            return modT.t[:, j, r:r + 1]

        ntile = dbg.get("nt4", SEQ // TW)
        tiles = [(b, t) for b in range(nseq) for t in range(ntile)]

        mhalf = p.sbuf("mhalf", [128, 4], F32)
        MEMSET(mhalf.t[:], -0.5, [mhalf])
        sig_r = Rot(p, "sig", [128, 512], BF16, 2)

        def rstd_pow(ssq_ap, out_ap, n, res):
            ncol = ssq_ap.shape[-1]
            TS(out_ap, ssq_ap, 1.0 / n, EPS, ALU.mult, ALU.add, [res], [res])
            TT(out_ap, out_ap, mhalf.t[:, 0:ncol], ALU.pow, [res, mhalf], [res], eng='pool')

        def PA1a(ti, s):
            b, t = tiles[ti]
            yot = yots[s]
            tok = t * TW + s * 128
            p.dma('sp', yot.t[:], yo_d[b, tok:tok + 128, :], writes=[yot],
                  dram_r=["yo%d_%d%s" % (b, tok + o_, s_) for o_ in (0, 64) for s_ in ("y", "o")])

        def PA1b1(ti, s):
            b, t = tiles[ti]
            hT4 = hT4s[ti % 2]
            tok = t * TW + s * 128
            p.dma('sp', hT4.t[:, :, s * 128:(s + 1) * 128], N.hts_d[b, :, :, tok:tok + 128], writes=[hT4],
                  dram_r=["hts%d_%d" % (b, tok)])

        def PA1b2(ti, s):
            hT4 = hT4s[ti % 2]
            zr = zrs[s]
            for nb in range(4):
                ps = PS()
                for dk in range(8):
                    MM(ps.t[:, 0:512], hT4.t[:, dk, s * 128:(s + 1) * 128], wzr.t[:, dk, nb * 512:(nb + 1) * 512],
                       dk == 0, dk == 7, [hT4, wzr], [ps])
                sig = sig_r.next()
                ACT(sig.t[:], ps.t[:, 0:512], AF.Sigmoid, [ps], [sig])
                TT(zr.t[:, nb * 512:(nb + 1) * 512], ps.t[:, 0:512], sig.t[:], ALU.mult, [ps, sig], [zr])

        def PA2a(ti, s):
            zr, yot = zrs[s], yots[s]
            TT(yot.t[:, 0:1024], yot.t[:, 0:1024], zr.t[:, 0:1024], ALU.mult, [yot, zr], [yot])
            TT(sq2.t[:], yot.t[:, 0:1024], yot.t[:, 0:1024], ALU.mult, [yot], [sq2])
            p.op('dve', lambda e: e.reduce_sum(so.t[:, 0:1], sq2.t[:], AX.X), [sq2], [so])
            rstd_pow(so.t[:, 0:1], so.t[:, 0:1], 1024.0, so)
            TS(yot.t[:, 0:1024], yot.t[:, 0:1024], so.t[:, 0:1], None, ALU.mult, None, [yot, so], [yot])
            TT(sq2.t[:], yot.t[:, 1024:2048], yot.t[:, 1024:2048], ALU.mult, [yot], [sq2], eng='pool')
            p.op('dve', lambda e: e.reduce_sum(so.t[:, 4:8], sq2.t[:].rearrange("p (h v) -> p h v", h=4), AX.X), [sq2], [so])
            rstd_pow(so.t[:, 4:8], so.t[:, 4:8], 256.0, so)
            TT(yot.t[:, 1024:2048].rearrange("p (h v) -> p h v", h=4), yot.t[:, 1024:2048].rearrange("p (h v) -> p h v", h=4),
               so.t[:, 4:8].unsqueeze(2).to_broadcast([128, 4, 256]), ALU.mult, [yot, so], [yot])
            TT(yot.t[:, 1024:2048], yot.t[:, 1024:2048], zr.t[:, 1024:2048], ALU.mult, [yot, zr], [yot], eng='pool')

        def PA2b(ti, s):
            yT4, oT4 = yT4s[ti % 2], oT4s[ti % 2]
            yot = yots[s]
            for (half, dstT, vg) in ((0, yT4, V_SNG), (1, oT4, V_GNG)):
                ps = PS()
                for j in range(8):
                    TR(ps.t[:, j * 128:(j + 1) * 128], yot.t[:, half * 1024 + j * 128:half * 1024 + (j + 1) * 128],
                       identf, [yot, consts], [ps])
                if half == 0:
                    for j in range(8):
                        ACT(dstT.t[:, j, s * 128:(s + 1) * 128], ps.t[:, j * 128:(j + 1) * 128], AF.Identity,
                            [ps, vecs], [dstT], scale=vecs.t[:, vg + j:vg + j + 1])
                else:
                    TT(dstT.t[:, :, s * 128:(s + 1) * 128], ps.t[:].rearrange("p (j t) -> p j t", j=8),
                       vecs.t[:, vg:vg + 8].unsqueeze(2).to_broadcast([128, 8, 128]), ALU.mult, [ps, vecs], [dstT])

        def Bstep(ti, jc):
            hT4, yT4, oT4, mT4 = hT4s[ti % 2], yT4s[ti % 2], oT4s[ti % 2], mT4s[ti % 2]
            gT, m12 = gT_r.next(), m12_r.next()
            ps = PS()
            for gi in range(2):
                gc = gi * 8 + jc
                for dk in range(8):
                    MM(ps.t[:, gi * 512:gi * 512 + TW], wmg.t[:, dk, gc * 128:(gc + 1) * 128], hT4.t[:, dk, :], dk == 0, dk == 7, [wmg, hT4], [ps])
                ACT(gT.t[:, gi, :], ps.t[:, gi * 512:gi * 512 + TW], AF.Sigmoid, [ps, vecs], [gT], bias=vecs.t[:, V_BM + gc:V_BM + gc + 1])
            ps = PS()
            for dk in range(8):
                MM(ps.t[:, 0:TW], wbs.t[:, dk, jc * 128:(jc + 1) * 128], yT4.t[:, dk, :], dk == 0, dk == 7, [wbs, yT4], [ps])
            for dk in range(8):
                MM(ps.t[:, 512:512 + TW], wbg.t[:, dk, jc * 128:(jc + 1) * 128], oT4.t[:, dk, :], dk == 0, dk == 7, [wbg, oT4], [ps])
            TT(m12.t[:].rearrange("p (g t) -> p g t", g=2), ps.t[:].rearrange("p (g t) -> p g t", g=2)[:, :, 0:TW], gT.t[:], ALU.mult, [ps, gT], [m12])
            TT(mT4.t[:, jc, :], m12.t[:, 0:TW], m12.t[:, TW:2 * TW], ALU.add, [m12], [mT4], eng='pool')

        def Cstep(ti, s):
            b, t = tiles[ti]
            mT4 = mT4s[ti % 2]
            tok = t * TW + s * 128
            ps = PS()
            for nb in range(2):
                for dk in range(8):
                    MM(ps.t[:, nb * 512:(nb + 1) * 512], mT4.t[:, dk, s * 128:(s + 1) * 128], wo.t[:, dk, nb * 512:(nb + 1) * 512],
                       dk == 0, dk == 7, [mT4, wo], [ps])
            TT(tbuf.t[:], ps.t[:], G1.t[:, b, :], ALU.mult, [ps, G1], [tbuf])
            nm = "x1_%d_%d" % (b, tok)
            p.dma('pool', x1_d[b, tok:tok + 128, :], tbuf.t[:], reads=[tbuf], dram_r=[nm], dram_w=[nm], accum=ALU.add)

        A1m = N.A1
        for s in range(NS):
            PA1a(0, s)
            PA1b1(0, s)
            PA1b2(0, s)
        for s in range(NS):
            PA2a(0, s)
            PA2b(0, s)
        sched = {0: [(PA1a, 0)], 1: [(PA1b1, 0), (PA1a, 1)], 2: [(PA1b2, 0), (PA1b1, 1)], 3: [(PA2a, 0)],
                 4: [(PA1b2, 1)], 5: [(PA2b, 0), (PA2a, 1)], 7: [(PA2b, 1)]}
        for ti in range(len(tiles)):
            nxt = ti + 1 < len(tiles)
            for jc in range(8):
                Bstep(ti, jc)
                if nxt:
                    for (fn, s) in sched.get(jc, []):
                        fn(ti + 1, s)
            for s in range(NS):
                Cstep(ti, s)
        p.es = es
      p.barrier()

    if "F" in phases:
      with ExitStack() as esF:
        p.es = esF
        G2 = p.sbuf("G2", [128, NBL, 1024], F32)
        compute_G(G2, 5, RB_BA5)
        p.es = esF
        fng = p.sbuf("fng", [128, 1024], F32)
        p.dma('sp', fng.t[:], N.rowsbig_d[0:1, RB_FNG:RB_FNG + 1024].partition_broadcast(128), writes=[fng])
        wdn = p.sbuf("wdn", [128, 22, 1024], BF16)
        load_w(wdn, N.w_down_d, 22)
        wupr = Rot(p, "wup", [128, 8, 256], BF16, 3)
        h2T = p.sbuf("h2T", [128, 8, 1152], BF16)
        aT = p.sbuf("aT", [128, 22, 1024], BF16)
        scr8 = p.sbuf("scr8f", [128, 2048], F32)
        sq = p.sbuf("sqf", [128, 1024], BF16)
        ssq = p.sbuf("ssqf", [128, 4], F32)
        usb_r = Rot(p, "usb", [128, 17 * 66], BF16, 2)
        for _u in usb_r.b:
            MEMSET(_u.t[:], 0.0, [_u])
        dg_r = Rot(p, "dg", [128, 9, 128], BF16, 2)
        sg_r = Rot(p, "sg", [128, 1024], F32, 2)
        identb = N.identb

        def B2fn(j, r):
            return modT.t[:, 24 + j, r:r + 1]

        nj = dbg.get("nj", 22)
        wsrc = N.w_up_d.rearrange("(dk p) n -> p dk n", p=128)
        for b in range(nseq):
            for hf in range(dbg.get("nhalf", 2)):
                base = 0 if hf == 0 else 896
                for i in range(9):
                    tok = base + i * 128
                    xt = scr8.t[:, 0:1024]
                    p.dma('sp', xt, x1_d[b, tok:tok + 128, :], writes=[scr8], dram_r=["x1_%d_%d" % (b, tok)])
                    N.norm_to_T(xt, scr8, A2, B2fn, b, h2T, i * 128, (sq.t[:], sq, ssq, scr8.t[:, 1024:2048], scr8))
                if dbg.get("dump_h2T") and b == 0 and hf == 0:
                    dump("h2T", h2T, h2T.t[:], [128, 8, 1152])
                m0 = 0 if hf == 0 else 128
                h0 = 1024 if hf == 0 else 64
                off = 0 if hf == 0 else 1
                hrow = 16 if hf == 0 else 0
                for j in range(nj):
                    wu = wupr.next()
                    p.dma('pool', wu.t[:, :, 0:128], wsrc[:, :, j * 128:(j + 1) * 128], writes=[wu])
                    p.dma('pool', wu.t[:, :, 128:256], wsrc[:, :, DFF + j * 128:DFF + (j + 1) * 128], writes=[wu])
                    pcs = []
                    for part in range(2):
                        ch = part * 22 + j
                        pm = PS()
                        pc = PS()
                        for nb in range(2):
                            for dk in range(8):
                                MM(pm.t[:, nb * 512:(nb + 1) * 512], wu.t[:, dk, part * 128:(part + 1) * 128],
                                   h2T.t[:, dk, m0 + nb * 512:m0 + (nb + 1) * 512], dk == 0, dk == 7, [wu, h2T], [pm])
                        for dk in range(8):
                            MM(pc.t[:, 0:64], wu.t[:, dk, part * 128:(part + 1) * 128], h2T.t[:, dk, h0:h0 + 64],
                               dk == 0, dk == 7, [wu, h2T], [pc])
                        usb = usb_r.next()
                        u3 = usb.t[:].rearrange("p (r c) -> p r c", c=66)
                        CP(u3[:, off:off + 16, 1:65], pm.t[:].rearrange("p (r c) -> p r c", c=64), [pm], [usb], eng='act')
                        CP(u3[:, hrow, 1:65], pc.t[:, 0:64], [pc], [usb], eng='dve')
                        dg = dg_r.next()
                        wb_ = V_FCW + ch * 9
                        TT(dg.t[:], identb.t[:].unsqueeze(1).to_broadcast([128, 9, 128]),
                           vecs.t[:, wb_:wb_ + 9].unsqueeze(2).to_broadcast([128, 9, 128]), ALU.mult, [identb, vecs], [dg], eng='pool')
                        taps = [(0, 0)] + [(dr, dc) for dr in (-1, 0, 1) for dc in (-1, 0, 1) if (dr, dc) != (0, 0)]
                        for bank in range(2):
                            todo = []
                            for (dr, dc) in taps:
                                mlo = 1 if (hf == 0 and dr == -1) else 0
                                mhi = 15 if (hf == 1 and dr == 1) else 16
                                lo = max(mlo, bank * 8)
                                hi = min(mhi, bank * 8 + 8)
                                if hi <= lo:
                                    continue
                                todo.append((dr, dc, lo, hi))
                            for ti, (dr, dc, lo, hi) in enumerate(todo):
                                k = (dr + 1) * 3 + (dc + 1)
                                MM(pc.t[:, lo * 64:hi * 64], dg.t[:, k, :], u3[:, lo + off + dr:hi + off + dr, 1 + dc:65 + dc],
                                   ti == 0, ti == len(todo) - 1, [dg, usb], [pc])
                        pcs.append((pc, ch))
                    sg = sg_r.next()
                    (pcg, chg), (pcv, chv) = pcs
                    ACT(sg.t[:], pcg.t[:], AF.Silu, [pcg, vecs], [sg], bias=vecs.t[:, V_FCB + chg:V_FCB + chg + 1])
                    STT(aT.t[:, j, :], pcv.t[:], vecs.t[:, V_FCB + chv:V_FCB + chv + 1], sg.t[:], ALU.add, ALU.mult, [pcv, vecs, sg], [aT])
                for s in range(8):
                    tok = hf * 1024 + s * 128
                    ps = PS()
                    for nb in range(2):
                        for j in range(nj):
                            MM(ps.t[:, nb * 512:(nb + 1) * 512], aT.t[:, j, s * 128:(s + 1) * 128], wdn.t[:, j, nb * 512:(nb + 1) * 512],
                               j == 0, j == nj - 1, [aT, wdn], [ps])
                    xt = scr8.t[:, 0:1024]
                    x2 = scr8.t[:, 1024:2048]
                    p.dma('sp', xt, x1_d[b, tok:tok + 128, :], writes=[scr8], dram_r=["x1_%d_%d" % (b, tok)])
                    TT(x2, ps.t[:], G2.t[:, b, :], ALU.mult, [ps, G2], [scr8])
                    TT(x2, x2, xt, ALU.add, [scr8], [scr8])
                    ACT(sq.t[:], x2, AF.Square, [scr8], [sq])
                    p.op('dve', lambda e: e.reduce_sum(ssq.t[:, 0:1], sq.t[:], AX.X), [sq], [ssq])
                    rstd_of(ssq.t[:, 0:1], ssq.t[:, 1:2], 1024.0, [ssq], [ssq])
                    TS(x2, x2, ssq.t[:, 1:2], None, ALU.mult, None, [scr8, ssq], [scr8])
                    TT(xt, x2, fng.t[:], ALU.mult, [scr8, fng], [scr8])
                    p.dma('sp', out_d[b, tok:tok + 128, :], xt, reads=[scr8], final=True)
        p.es = es
      p.barrier()


_CACHE = {}


def kernel(**inputs):
    inp = {k: np.asarray(v) for k, v in inputs.items()}
    if "nc" not in _CACHE:
        _CACHE["nc"] = build_program()[0]
    nc = _CACHE["nc"]
    sh = prep_shared(inp)
    shared = dict(w_ada=inp["w_ada"][0], w_in=inp["w_in"][0], w_merge=inp["w_merge"][0], w_br_ssd=inp["w_br_ssd"][0],
                  w_br_gla=inp["w_br_gla"][0], w_o=inp["w_o"][0], w_up=inp["w_up"][0], w_down=inp["w_down"][0])
    shared.update(sh)
    shared = {k: np.ascontiguousarray(np.asarray(v, np.float32)) for k, v in shared.items()}
    in_maps = []
    for core in range(8):
        b0 = core * NBL
        cT = np.stack([inp["c"][b0], inp["c"][b0 + 1], inp["c_ctx"]], axis=1).astype(np.float32)
        cT = np.ascontiguousarray(cT.reshape(8, 128, 3).transpose(1, 0, 2))
        m = dict(shared)
        m["x"] = np.ascontiguousarray(inp["x"][b0:b0 + NBL], dtype=np.float32)
        m["ctx"] = np.ascontiguousarray(inp["ctx"][b0:b0 + NBL], dtype=np.float32)
        m["cT"] = cT
        in_maps.append(m)
    res = run_bass_kernel_spmd(nc, in_maps, core_ids=list(range(8)))
    out = np.concatenate([np.asarray(r["out"]) for r in res.results], axis=0)
    return out.astype(np.float32)
```

```python
import numpy as np
import concourse.bass as bass
import concourse.mybir as mybir
from concourse.bass_utils import run_bass_kernel_spmd
from contextlib import ExitStack

F32 = mybir.dt.float32
BF16 = mybir.dt.bfloat16
AF = mybir.ActivationFunctionType
ALU = mybir.AluOpType
AX = mybir.AxisListType


class Res:
    def __init__(self, name, t=None):
        self.name = name
        self.t = t
        self.lw = {}
        self.rd = {}


class Prog:
    NDMA = 8

    def __init__(self, nc, es):
        self.nc = nc
        self.es = es
        self.names = ['pe', 'act', 'dve', 'pool', 'sp']
        self.E = {'pe': nc.tensor, 'act': nc.scalar, 'dve': nc.vector, 'pool': nc.gpsimd, 'sp': nc.sync}
        self.cnt = {k: 0 for k in self.names}
        self.h = {}
        for k in self.names:
            self.h[('e', k)] = es.enter_context(nc.semaphore("s_" + k))
        self.dq = ('sp', 'pool', 'act')
        self.dcnt = {q: 0 for q in self.dq}
        self.dval = {}
        for q in self.dq:
            for i in range(self.NDMA):
                self.h[('d', q, i)] = es.enter_context(nc.semaphore("d_%s%d" % (q, i)))
                self.dval[('d', q, i)] = 0
        self.seen = {k: {} for k in self.names}
        self.dram = {}
        self.final = []
        self.nwait = 0

    def sbuf(self, name, shape, dtype):
        self.uid = getattr(self, "uid", 0) + 1
        t = self.es.enter_context(self.nc.sbuf_tensor("sb%d_%s" % (self.uid, name), list(shape), dtype))
        return Res(name, t)

    def psum(self, name, shape, dtype):
        t = self.es.enter_context(self.nc.psum_tensor("ps_" + name, list(shape), dtype))
        return Res(name, t)

    def _dres(self, name):
        if name not in self.dram:
            self.dram[name] = Res(name)
        return self.dram[name]

    def _collect(self, eng, reads, writes):
        need = {}

        def add(tok):
            if tok is None:
                return
            k, v = tok
            if need.get(k, 0) < v:
                need[k] = v

        for r in reads:
            for k, v in r.lw.items():
                add((k, v))
        for w in writes:
            for k, v in w.lw.items():
                add((k, v))
            for k, v in w.rd.items():
                add((k, v))
        out = []
        for k, v in need.items():
            if eng == 'pe' and k == ('e', 'pe'):
                continue
            if self.seen[eng].get(k, 0) >= v:
                continue
            self.seen[eng][k] = v
            out.append((k, v))
        return out

    def _mark(self, tok, reads, writes):
        k, v = tok
        for r in reads:
            if r.rd.get(k, 0) < v:
                r.rd[k] = v
        for w in writes:
            if w.lw.get(k, 0) < v:
                w.lw[k] = v
            w.rd = {}

    def op(self, eng, fn, reads=(), writes=()):
        waits = self._collect(eng, reads, writes)
        self.cnt[eng] += 1
        sem = self.h[('e', eng)]
        hs = [(self.h[k], v) for k, v in waits]
        self.nwait += len(hs)

        e = self.E[eng]
        for hh, v in hs:
            e.wait_ge(hh, v)
        fn(e).then_inc(sem, 1)
        self._mark((('e', eng), self.cnt[eng]), reads, writes)

    def dma(self, q, out_ap, in_ap, reads=(), writes=(), dram_r=(), dram_w=(), final=False, accum=None):
        reads = list(reads) + [self._dres(n) for n in dram_r]
        writes = list(writes) + [self._dres(n) for n in dram_w]
        i = self.dcnt[q] % self.NDMA
        self.dcnt[q] += 1
        key = ('d', q, i)
        prev = self.dval[key]
        waits = self._collect(q, reads, writes)
        if prev > 0 and self.seen[q].get(key, 0) < prev:
            self.seen[q][key] = prev
            waits.append((key, prev))
        self.dval[key] = prev + 16
        sem = self.h[key]
        hs = [(self.h[k], v) for k, v in waits]
        self.nwait += len(hs)

        e = self.E[q]
        for hh, v in hs:
            e.wait_ge(hh, v)
        if accum is not None:
            e.dma_start(out=out_ap, in_=in_ap, accum_op=accum).then_inc(sem, 16)
        else:
            e.dma_start(out=out_ap, in_=in_ap).then_inc(sem, 16)
        tok = (key, prev + 16)
        self._mark(tok, reads, writes)
        if final:
            self.final.append(tok)

    def barrier(self):
        toks = [(('e', k), self.cnt[k]) for k in self.names if self.cnt[k] > 0]
        toks += [(k, v) for k, v in self.dval.items() if v > 0]
        for eng in self.names:
            e = self.E[eng]
            for k, v in toks:
                if eng == 'pe' and k == ('e', 'pe'):
                    continue
                if self.seen[eng].get(k, 0) >= v:
                    continue
                self.seen[eng][k] = v
                e.wait_ge(self.h[k], v)

    def finish(self):
        fin = {}
        for k, v in self.final:
            fin[k] = max(fin.get(k, 0), v)
        hs = [(self.h[k], v) for k, v in fin.items()]

        e = self.E['sp']
        for hh, v in hs:
            e.wait_ge(hh, v)


D = 1024
SEQ = 2048
CTXL = 256
NTOK = CTXL + SEQ
NBL = 2
DFF = 2816
EPS = 1e-6
WM0, WMN = 1024, 4144
O_XBC, O_DT, O_Q, O_K, O_V, O_G = 0, 2048, 2080, 2592, 3104, 4128
V_N1G, V_N2G, V_BADA, V_CW, V_CB, V_BM, V_FCW, V_FCB, V_SNG, V_GNG, NV = 0, 8, 16, 64, 112, 128, 144, 540, 584, 592, 600
R_DTB, R_ALOG, R_DSK, NR = 0, 32, 64, 80
RB_FNG, RB_BA2, RB_BA5, NRB = 0, 1024, 2048, 3072
K_ID, K_ONE, K_L, K_U, K_NM, K_V, NCN = 0, 128, 256, 384, 512, 640, 768


def host_consts():
    c = np.zeros((128, NCN), np.float32)
    c[:, K_ID:K_ID + 128] = np.eye(128, dtype=np.float32)
    c[:, K_ONE:K_ONE + 128] = 1.0
    t = np.arange(64)[:, None]
    i = np.arange(64)[None, :]
    c[0:64, K_L:K_L + 64] = (t <= i)
    c[0:64, K_L + 64:K_L + 128] = (t >= i)
    c[0:64, K_U:K_U + 64] = (t > i)
    c[0:64, K_U + 64:K_U + 128] = (t < i)
    c[0:64, K_NM:K_NM + 64] = np.where(t <= i, 0.0, -30000.0)
    c[0:64, K_NM + 64:K_NM + 128] = np.where(t >= i, 0.0, -30000.0)
    c[0:64, K_V:K_V + 64] = (t <= i)
    c[0:64, K_V + 64:K_V + 128] = (t >= i)
    return c


def fm(v):
    v = np.asarray(v, np.float32)
    return np.ascontiguousarray(v.reshape(-1, 128).T)


def prep_shared(inp):
    vecs = np.zeros((128, NV), np.float32)
    vecs[:, V_N1G:V_N1G + 8] = fm(inp["norm1_g"][0])
    vecs[:, V_N2G:V_N2G + 8] = fm(inp["norm2_g"][0])
    vecs[:, V_BADA:V_BADA + 48] = fm(inp["b_ada"][0])
    cw = inp["ssd_conv_w"][0]
    for k in range(3):
        vecs[:, V_CW + k:V_CW + 48:3] = fm(cw[k])
    vecs[:, V_CB:V_CB + 16] = fm(inp["ssd_conv_b"][0])
    vecs[:, V_BM:V_BM + 16] = fm(inp["b_merge"][0])
    fw = inp["ffn_conv_w"][0].reshape(9, -1)
    for k in range(9):
        vecs[:, V_FCW + k:V_FCW + 396:9] = fm(fw[k])
    vecs[:, V_FCB:V_FCB + 44] = fm(inp["ffn_conv_b"][0])
    vecs[:, V_SNG:V_SNG + 8] = fm(inp["ssd_norm_g"][0])
    vecs[:, V_GNG:V_GNG + 8] = np.tile(fm(inp["gla_norm_g"][0]), (1, 4))
    rows = np.zeros((1, NR), np.float32)
    rowsbig = np.zeros((1, NRB), np.float32)
    rowsbig[0, RB_FNG:RB_FNG + 1024] = inp["final_norm_g"]
    rowsbig[0, RB_BA2:RB_BA2 + 1024] = inp["b_ada"][0][2048:3072]
    rowsbig[0, RB_BA5:RB_BA5 + 1024] = inp["b_ada"][0][5120:6144]
    rows[0, R_DTB:R_DTB + 32] = inp["ssd_dt_bias"][0].reshape(-1)
    rows[0, R_ALOG:R_ALOG + 32] = inp["ssd_a_log"][0].reshape(-1)
    rows[0, R_DSK:R_DSK + 16] = inp["ssd_d"][0]
    gw = np.zeros((17, 1024), np.float32)
    gw[0:16] = np.transpose(inp["gla_gate_w"][0], (1, 0, 2)).reshape(16, 1024)
    gw[16] = inp["gla_gate_b"][0].reshape(-1)
    return dict(vecs=vecs, rows=rows, rowsbig=rowsbig, gw=gw, consts=host_consts())


class Rot:
    def __init__(self, p, name, shape, dtype, n=2):
        self.b = [p.sbuf("%s%d" % (name, i), shape, dtype) for i in range(n)]
        self.i = 0

    def next(self):
        r = self.b[self.i % len(self.b)]
        self.i += 1
        return r


def build_program(dbg=None, nseq=NBL, phases="AMPF"):
    dbg = dbg or {}
    nc = bass.Bass("TRN2", target_bir_lowering=False)
    IN = "ExternalInput"

    def din(name, shape):
        return nc.dram_tensor(name, list(shape), F32, kind=IN).ap()

    x_d = din("x", [NBL, SEQ, D])
    ctx_d = din("ctx", [NBL, CTXL, D])
    cT_d = din("cT", [128, 8, 3])
    w_ada_d = din("w_ada", [D, 6 * D])
    w_in_d = din("w_in", [D, 6192])
    w_merge_d = din("w_merge", [D, 2 * D])
    w_brs_d = din("w_br_ssd", [D, D])
    w_brg_d = din("w_br_gla", [D, D])
    w_o_d = din("w_o", [D, D])
    w_up_d = din("w_up", [D, 2 * DFF])
    w_down_d = din("w_down", [DFF, D])
    gw_d = din("gw", [17, 1024])
    vecs_d = din("vecs", [128, NV])
    rows_d = din("rows", [1, NR])
    rowsbig_d = din("rowsbig", [1, NRB])
    consts_d = din("consts", [128, NCN])
    out_d = nc.dram_tensor("out", [NBL, SEQ, D], F32, kind="ExternalOutput").ap()
    yo_kind = "ExternalOutput" if dbg.get("dump_yo") else "Internal"
    yo_d = nc.dram_tensor("yo", [NBL, SEQ, 2 * D], F32, kind=yo_kind).ap()
    sx_d = nc.dram_tensor("sx", [NBL, 9, 128, 16 * 256], BF16, kind="Internal").ap()
    sqk_d = nc.dram_tensor("sqk", [NBL, 9, 128, 8 * 256], BF16, kind="Internal").ap()
    sg_d = nc.dram_tensor("sg", [NBL, 9, 32, 256], BF16, kind="Internal").ap()
    hts_d = nc.dram_tensor("hts", [NBL, 128, 8, SEQ], BF16, kind="Internal").ap()
    x1_kind = "ExternalOutput" if dbg.get("dump_x1") else "Internal"
    x1_d = nc.dram_tensor("x1", [NBL, SEQ, D], F32, kind=x1_kind).ap()
    dbg_outs = {}

    es = ExitStack()
    with es:
        p = Prog(nc, es)

        def MM(out, lhsT, rhs, start, stop, reads, writes):
            p.op('pe', lambda e: e.matmul(out, lhsT, rhs, start=start, stop=stop), reads, writes)

        def TR(out, in_, ident, reads, writes):
            p.op('pe', lambda e: e.transpose(out, in_, ident), reads, writes)

        def ACT(out, in_, func, reads, writes, bias=None, scale=None):
            kw = {}
            if bias is not None:
                kw['bias'] = bias
            if scale is not None:
                kw['scale'] = scale
            p.op('act', lambda e: e.activation(out, in_, func, **kw), reads, writes)

        def TT(out, in0, in1, op, reads, writes, eng='dve'):
            p.op(eng, lambda e: e.tensor_tensor(out, in0, in1, op), reads, writes)

        def TS(out, in0, s1, s2, op0, op1, reads, writes, eng='dve'):
            if s2 is None:
                p.op(eng, lambda e: e.tensor_scalar(out, in0, s1, None, op0), reads, writes)
            else:
                p.op(eng, lambda e: e.tensor_scalar(out, in0, s1, s2, op0, op1), reads, writes)

        def STT(out, in0, sc, in1, op0, op1, reads, writes):
            p.op('dve', lambda e: e.scalar_tensor_tensor(out, in0, sc, in1, op0, op1), reads, writes)

        def CP(out, in_, reads, writes, eng='dve'):
            if eng == 'act':
                p.op('act', lambda e: e.activation(out, in_, AF.Identity), reads, writes)
            else:
                p.op(eng, lambda e: e.tensor_copy(out, in_), reads, writes)

        def MEMSET(ap, val, writes, eng='dve'):
            p.op(eng, lambda e: e.memset(ap, val), (), writes)

        def dump(name, res, ap, shape):
            t = nc.dram_tensor("dbg_" + name, list(shape), ap.dtype, kind="ExternalOutput").ap()
            p.dma('sp', t, ap, reads=[res], final=True)
            dbg_outs[name] = t

        pp = [p.psum("pp%d" % i, [128, 1024], F32) for i in range(4)]
        pidx = [0]

        def PS():
            r = pp[pidx[0] % 4]
            pidx[0] += 1
            return r

        consts = p.sbuf("consts", [128, NCN], F32)
        vecs = p.sbuf("vecs", [128, NV], F32)
        rowsb = p.sbuf("rowsb", [128, NR], F32)
        gw = p.sbuf("gw", [17, 1024], BF16)
        constb = p.sbuf("constb", [128, 384], BF16)
        p.dma('sp', consts.t[:], consts_d, writes=[consts])
        p.dma('sp', vecs.t[:], vecs_d, writes=[vecs])
        p.dma('sp', rowsb.t[:], rows_d.partition_broadcast(128), writes=[rowsb])
        p.dma('pool', gw.t[:], gw_d, writes=[gw])
        identf = consts.t[:, K_ID:K_ID + 128]
        onesf = consts.t[:, K_ONE:K_ONE + 128]

        def Lm(d):
            return consts.t[0:64, K_L + 64 * d:K_L + 64 * d + 64]

        def Um(d):
            return consts.t[0:64, K_U + 64 * d:K_U + 64 * d + 64]

        def NMm(d):
            return consts.t[0:64, K_NM + 64 * d:K_NM + 64 * d + 64]

        def Vm(d):
            return consts.t[0:64, K_V + 64 * d:K_V + 64 * d + 64]

        identb = p.sbuf("identb", [128, 128], BF16)
        CP(identb.t[:], identf, [consts], [identb])
        CP(constb.t[:], consts.t[:, K_ONE:K_ONE + 384], [consts], [constb])

        def Lb(d):
            return constb.t[0:64, 128 + 64 * d:128 + 64 * d + 64]

        def Ub(d):
            return constb.t[0:64, 256 + 64 * d:256 + 64 * d + 64]
        aneg = p.sbuf("aneg", [64, 32], F32)
        ACT(aneg.t[:], rowsb.t[0:64, R_ALOG:R_ALOG + 32], AF.Exp, [rowsb], [aneg])
        TS(aneg.t[:], aneg.t[:], -1.0, None, ALU.mult, None, [aneg], [aneg])
        dkd = p.sbuf("dkd", [64, 1024], BF16)
        TT(dkd.t[:].rearrange("p (h q) -> p h q", h=16),
           consts.t[0:64, K_ID:K_ID + 64].unsqueeze(1).to_broadcast([64, 16, 64]),
           rowsb.t[0:64, R_DSK:R_DSK + 16].unsqueeze(2).to_broadcast([64, 16, 64]),
           ALU.mult, [consts, rowsb], [dkd])

        modT = p.sbuf("modT", [128, 48, 3], F32)
        scT = p.sbuf("scT", [128, 8, 3], BF16)
        A1 = p.sbuf("A1", [128, 8, 3], F32)
        A2 = p.sbuf("A2", [128, 8, 3], F32)

        with ExitStack() as esA:
            p.es = esA
            cT = p.sbuf("cT", [128, 8, 3], F32)
            p.dma('sp', cT.t[:], cT_d, writes=[cT])
            ACT(scT.t[:], cT.t[:], AF.Silu, [cT], [scT])
            wrot = Rot(p, "wada", [128, 8, 512], BF16, 2)
            mps = PS()
            for nb in range(12):
                wb = wrot.next()
                for dk in range(8):
                    p.dma('pool', wb.t[:, dk, :], w_ada_d[dk * 128:(dk + 1) * 128, nb * 512:(nb + 1) * 512], writes=[wb])
                for cc in range(4):
                    j = nb * 4 + cc
                    for dk in range(8):
                        MM(mps.t[:, j * 4:j * 4 + 3], wb.t[:, dk, cc * 128:(cc + 1) * 128], scT.t[:, dk, :],
                           dk == 0, dk == 7, [wb, scT], [mps])
            TT(modT.t[:], mps.t[:, 0:192].rearrange("p (j r) -> p j r", r=4)[:, :, 0:3],
               vecs.t[:, V_BADA:V_BADA + 48].unsqueeze(2).to_broadcast([128, 48, 3]), ALU.add, [mps, vecs], [modT])
            for (A, vg, so) in ((A1, V_N1G, 8), (A2, V_N2G, 32)):
                TS(A.t[:], modT.t[:, so:so + 8, :], 1.0, None, ALU.add, None, [modT], [A])
                TT(A.t[:], A.t[:], vecs.t[:, vg:vg + 8].unsqueeze(2).to_broadcast([128, 8, 3]), ALU.mult, [A, vecs], [A])
            p.es = es
        p.barrier()
        if dbg.get("dump_mod"):
            dump("modT", modT, modT.t[:], [128, 48, 3])
            dump("A1", A1, A1.t[:], [128, 8, 3])

        def norm_to_T(xt, xtr, A, Bap_fn, r, dst, dst_tok0, tmp):
            sq, sqr, ssq, xn, xnr = tmp
            ACT(sq, xt, AF.Square, [xtr], [sqr])
            p.op('dve', lambda e: e.reduce_sum(ssq.t[:, 0:1], sq, AX.X), [sqr], [ssq])
            ACT(ssq.t[:, 1:2], ssq.t[:, 0:1], AF.Sqrt, [ssq], [ssq], bias=EPS_AP[0], scale=1.0 / D)
            p.op('dve', lambda e: e.reciprocal(ssq.t[:, 2:3], ssq.t[:, 1:2]), [ssq], [ssq])
            TS(xn, xt, ssq.t[:, 2:3], None, ALU.mult, None, [xtr, ssq], [xnr])
            ps = PS()
            for j in range(8):
                TR(ps.t[:, j * 128:(j + 1) * 128], xn[:, j * 128:(j + 1) * 128], identf, [xnr, consts], [ps])
            for j in range(8):
                if j % 2 == 0:
                    TS(dst.t[:, j, dst_tok0:dst_tok0 + 128], ps.t[:, j * 128:(j + 1) * 128],
                       A.t[:, j, r:r + 1], Bap_fn(j, r), ALU.mult, ALU.add, [ps, A, modT], [dst])
                else:
                    ACT(dst.t[:, j, dst_tok0:dst_tok0 + 128], ps.t[:, j * 128:(j + 1) * 128], AF.Identity,
                        [ps, A, modT], [dst], bias=Bap_fn(j, r), scale=A.t[:, j, r:r + 1])

        epsb = p.sbuf("epsb", [128, 1], F32)
        MEMSET(epsb.t[:], EPS, [epsb])
        EPS_AP = [epsb.t[:, 0:1]]

        if "M" in phases:
          with ExitStack() as esM:
            p.es = esM
            wmix = p.sbuf("wmix", [128, 8, WMN], BF16)
            nmrep = p.sbuf("nmrep", [64, 2, 1024], BF16)
            for d_ in range(2):
                CP(nmrep.t[:, d_, :].rearrange("p (h q) -> p h q", h=16),
                   consts.t[0:64, K_NM + 64 * d_:K_NM + 64 * d_ + 64].unsqueeze(1).to_broadcast([64, 16, 64]), [consts], [nmrep])

            for dk in range(8):
                p.dma('pool', wmix.t[:, dk, :], w_in_d[dk * 128:(dk + 1) * 128, WM0:WM0 + WMN], writes=[wmix])
            hT = p.sbuf("hT", [128, 8, NTOK], BF16)
            scr8 = p.sbuf("scr8", [128, 2048], F32)
            sq = p.sbuf("sq", [128, 1024], BF16)
            ssq = p.sbuf("ssq", [128, 4], F32)
            Hs = p.sbuf("Hs", [128, 1024], F32)
            Hb = p.sbuf("Hb", [128, 1024], BF16)
            Ss = p.sbuf("Ss", [128, 1024], F32)
            Sb = p.sbuf("Sb", [128, 1024], BF16)
            xbcT = p.sbuf("xbcT", [128, 16, 256], BF16)
            qkT = p.sbuf("qkT", [128, 8, 256], BF16)
            glrT = p.sbuf("glrT", [32, 256], BF16)
            MEMSET(glrT.t[:], 1.0, [glrT])
            accr = Rot(p, "acc", [128, 256], F32, 2)
            dts_r = Rot(p, "dts", [64, 32], F32, 2)
            da_r = Rot(p, "da", [64, 16], F32, 2)
            cum_r = Rot(p, "cum_sb", [64, 16], F32, 2)
            dtw_r = Rot(p, "dtw", [64, 16], F32, 2)
            ecum_r = Rot(p, "ecum", [64, 16], F32, 2)
            eL_r = Rot(p, "eL", [128, 16], F32, 2)
            dahl_r = Rot(p, "dahl", [64, 32], BF16, 2)
            dtbhl = p.sbuf("dtbhl", [1, 64], BF16)
            CP(dtbhl.t[0:1, 0:32], rowsb.t[0:1, R_DTB:R_DTB + 32], [rowsb], [dtbhl])
            TT(dtbhl.t[0:1, 32:64], rowsb.t[0:1, R_DTB:R_DTB + 32], dtbhl.t[0:1, 0:32], ALU.subtract, [rowsb, dtbhl], [dtbhl])
            nlm = p.sbuf("nlm", [64, 2, 64], BF16)
            for d_ in range(2):
                mid_ = 31 if d_ == 0 else 32
                TS(nlm.t[:, d_, :], consts.t[0:64, K_L + 64 * d_ + mid_:K_L + 64 * d_ + mid_ + 1].to_broadcast([64, 64]),
                   -1.0, None, ALU.mult, None, [consts], [nlm])
            ndahl_r = Rot(p, "ndahl", [64, 32], BF16, 2)
            seg = p.sbuf("seg", [64, 1024], F32)
            MTt_r = Rot(p, "MTt", [64, 1024], BF16, 2)
            cmc_r = Rot(p, "cmc", [128, 256], BF16, 2)
            xsb_r = Rot(p, "xsb", [64, 1536], BF16, 2)
            xd_r = Rot(p, "xd", [64, 1024], BF16, 2)
            xdw_r = Rot(p, "xdw", [64, 1024], BF16, 2)
            ybufs = [Res("yb0", scr8.t[0:64, 0:1024]), Res("yb1", scr8.t[0:64, 1024:2048])]
            ycnt = [0]
            obuf_r = Rot(p, "obuf", [64, 1024], F32, 2)
            la_r = Rot(p, "la", [64, 512], F32, 1)
            lah_r = Rot(p, "lah", [64, 512], BF16, 1)
            lal_r = Rot(p, "lal", [64, 512], BF16, 1)
            ref = p.sbuf("ref", [128, 4], F32)
            dl = p.sbuf("dl", [128, 256], F32)
            eq = p.sbuf("eq", [128, 256], F32)
            ek = p.sbuf("ek", [128, 256], F32)
            ec = p.sbuf("ec", [128, 256], F32)
            eT_r = Rot(p, "eT", [128, 4], F32, 2)
            erc = p.sbuf("erc", [64, 512], F32)
            vsb_r = Rot(p, "vsb", [64, 1024], BF16, 2)
            kdec_r = Rot(p, "kdec", [64, 512], BF16, 2)
            qdT_r = Rot(p, "qdT", [128, 256], BF16, 2)
            kdT_r = Rot(p, "kdT", [128, 256], BF16, 2)
            qeT_r = Rot(p, "qeT", [128, 256], BF16, 2)
            scTt = p.sbuf("scTt", [64, 256], BF16)

            def B1fn(j, r):
                return modT.t[:, j, r:r + 1]

            small_pssB = Res("pssB", pp[1].t[:, 0:512])
            small_pscB = Res("pscB", pp[1].t[:, 512:1024])
            chunk_par = [0]
            pj = [0]

            def PSJ():
                r = pp[(0, 2, 3)[pj[0] % 3]]
                pj[0] += 1
                return r

            def proj_super(tok0, lo, hi):
                T = 256
                a = max(tok0 - 1, lo)
                e_ = min(tok0 + T + 1, hi)
                n = e_ - a
                off = a - (tok0 - 1)
                for cc in range(16):
                    ps = PSJ()
                    for dk in range(8):
                        MM(ps.t[:, off:off + n], wmix.t[:, dk, O_XBC + cc * 128:O_XBC + (cc + 1) * 128],
                           hT.t[:, dk, a:e_], dk == 0, dk == 7, [wmix, hT], [ps])
                    acc = accr.next()
                    cwb = V_CW + cc * 3
                    ACT(acc.t[:], ps.t[:, 1:257], AF.Identity, [ps, vecs], [acc],
                        bias=vecs.t[:, V_CB + cc:V_CB + cc + 1], scale=vecs.t[:, cwb + 1:cwb + 2])
                    i0 = 1 if off == 1 else 0
                    STT(acc.t[:, i0:256], ps.t[:, i0:256], vecs.t[:, cwb:cwb + 1], acc.t[:, i0:256],
                        ALU.mult, ALU.add, [ps, vecs, acc], [acc])
                    i1 = 255 if e_ < tok0 + T + 1 else 256
                    STT(acc.t[:, 0:i1], ps.t[:, 2:2 + i1], vecs.t[:, cwb + 2:cwb + 3], acc.t[:, 0:i1],
                        ALU.mult, ALU.add, [ps, vecs, acc], [acc])
                    ACT(xbcT.t[:, cc, :], acc.t[:], AF.Silu, [acc], [xbcT])
                for j in range(8):
                    ps = PSJ()
                    for dk in range(8):
                        MM(ps.t[:, 0:256], wmix.t[:, dk, O_Q + j * 128:O_Q + (j + 1) * 128],
                           hT.t[:, dk, tok0:tok0 + 256], dk == 0, dk == 7, [wmix, hT], [ps])
                    ACT(qkT.t[:, j, :], ps.t[:, 0:256], AF.Identity, [ps], [qkT],
                        scale=(128.0 ** -0.5) if j < 4 else 1.0)
                ps = PSJ()
                for dk in range(8):
                    MM(ps.t[0:16, 0:256], wmix.t[:, dk, O_G:O_G + 16], hT.t[:, dk, tok0:tok0 + 256],
                       dk == 0, dk == 7, [wmix, hT], [ps])
                CP(glrT.t[0:16, :], ps.t[0:16, 0:256], [ps], [glrT])

            def chunk_head(b, d, t0, cl, is_ctx):
                dsl = slice(d * 16, d * 16 + 16)
                dts, da, cum_sb, dtw, ecum, eL = dts_r.next(), da_r.next(), cum_r.next(), dtw_r.next(), ecum_r.next(), eL_r.next()
                xsb, xd, xdw = xsb_r.next(), xd_r.next(), xdw_r.next()
                la, vsb, kdec = la_r.next(), vsb_r.next(), kdec_r.next()
                lah, lal, dahl = lah_r.next(), lal_r.next(), dahl_r.next()
                ybuf = obuf = None
                if not is_ctx:
                    ybuf = ybufs[ycnt[0] % 2]
                    ycnt[0] += 1
                    obuf = obuf_r.next()
                par = chunk_par[0] % 2
                chunk_par[0] += 1
                pss = small_pssB
                psc = small_pscB
                sps = small_pssB
                h16 = lambda ap: ap.rearrange("p (h q) -> p h q", h=16)
                h4 = lambda ap: ap.rearrange("p (h q) -> p h q", h=4)
                for dk in range(8):
                    MM(pss.t[0:64, 0:32], hT.t[:, dk, t0:t0 + 64], wmix.t[:, dk, O_DT:O_DT + 32],
                       dk == 0, False, [wmix, hT], [pss])
                MM(pss.t[0:64, 0:32], constb.t[0:1, 0:64], dtbhl.t[0:1, 0:32], False, False, [constb, dtbhl], [pss])
                MM(pss.t[0:64, 0:32], constb.t[0:1, 0:64], dtbhl.t[0:1, 32:64], False, True, [constb, dtbhl], [pss])
                ACT(dts.t[:], pss.t[0:64, 0:32], AF.Exp, [pss], [dts])
                ACT(dts.t[:], dts.t[:], AF.Ln, [dts], [dts], bias=1.0)
                TT(da.t[:], dts.t[:, dsl], aneg.t[:, dsl], ALU.mult, [dts, aneg], [da])
                CP(dahl.t[:, 0:16], da.t[:], [da], [dahl])
                TT(dahl.t[:, 16:32], da.t[:], dahl.t[:, 0:16], ALU.subtract, [da, dahl], [dahl])
                ndahl = ndahl_r.next()
                TS(ndahl.t[:], dahl.t[:], -1.0, None, ALU.mult, None, [dahl], [ndahl])
                lps = pp[0]
                MM(lps.t[0:64, 0:512], glrT.t[0:17, cl:cl + 64], gw.t[0:17, d * 512:(d + 1) * 512], True, True, [glrT, gw], [lps])
                ACT(la.t[:], lps.t[0:64, 0:512], AF.Exp, [lps], [la], scale=-1.0)
                ACT(la.t[:], la.t[:], AF.Ln, [la], [la], bias=1.0)
                CP(lah.t[:], la.t[:], [la], [lah], eng='act')
                TT(lal.t[:], la.t[:], lah.t[:], ALU.subtract, [la, lah], [lal])
                vps = pp[2]
                for nb in range(2):
                    for dk in range(8):
                        MM(vps.t[0:64, nb * 512:(nb + 1) * 512], hT.t[:, dk, t0:t0 + 64],
                           wmix.t[:, dk, O_V + nb * 512:O_V + (nb + 1) * 512], dk == 0, dk == 7, [wmix, hT], [vps])
                CP(vsb.t[:], vps.t[0:64, :], [vps], [vsb], eng='act')
                return dict(dts=dts, da=da, cum_sb=cum_sb, dtw=dtw, ecum=ecum, eL=eL, xsb=xsb, xd=xd, xdw=xdw, la=la, vsb=vsb,
                            kdec=kdec, lah=lah, lal=lal, dahl=dahl, ndahl=ndahl, ybuf=ybuf, obuf=obuf, pss=pss, psc=psc, sps=sps, vps=vps, lps=lps)

            def chunk_mid(b, d, t0, cl, is_ctx, hd):
                dsl = slice(d * 16, d * 16 + 16)
                h16 = lambda ap: ap.rearrange("p (h q) -> p h q", h=16)
                h4 = lambda ap: ap.rearrange("p (h q) -> p h q", h=4)
                dts, da, cum_sb, dtw, ecum, eL = hd["dts"], hd["da"], hd["cum_sb"], hd["dtw"], hd["ecum"], hd["eL"]
                xsb, xd, xdw, la, vsb, kdec = hd["xsb"], hd["xd"], hd["xdw"], hd["la"], hd["vsb"], hd["kdec"]
                lah, lal, dahl, ybuf, obuf = hd["lah"], hd["lal"], hd["dahl"], hd["ybuf"], hd["obuf"]
                ndahl = hd["ndahl"]
                pss, psc, sps = hd["pss"], hd["psc"], hd["sps"]
                MTt, cmc, eT, qdT, kdT, qeT = MTt_r.next(), cmc_r.next(), eT_r.next(), qdT_r.next(), kdT_r.next(), qeT_r.next()
                hd.update(MTt=MTt, cmc=cmc, eT=eT, qdT=qdT, kdT=kdT, qeT=qeT)
                if not is_ctx:
                    CP(cmc.t[:].rearrange("p (g q) -> p g q", g=4), xbcT.t[:, 12:16, cl:cl + 64], [xbcT], [cmc], eng='pool')
                for (oc, lt) in ((slice(32, 48), Lb(d)), (slice(48, 64), Ub(d))):
                    MM(pss.t[0:64, oc], lt, dahl.t[:, 0:16], True, False, [constb, dahl], [pss])
                    MM(pss.t[0:64, oc], lt, dahl.t[:, 16:32], False, True, [constb, dahl], [pss])
                MM(pss.t[:, 64:80], constb.t[0:64, 0:128], dahl.t[:, 0:16], True, False, [constb, dahl], [pss])
                MM(pss.t[:, 64:80], constb.t[0:64, 0:128], dahl.t[:, 16:32], False, True, [constb, dahl], [pss])
                psx = pp[3]
                psxb = psx.t[:].bitcast(BF16)
                for j in range(12):
                    TR(psxb[0:64, j * 128:(j + 1) * 128], xbcT.t[:, j, cl:cl + 64], identb.t[:], [xbcT, identb], [psx])
                CP(xsb.t[:], psxb[0:64, 0:1536], [psx], [xsb], eng='act')
                if not is_ctx:
                    cq = pp[0]
                    for hb in range(2):
                        MM(cq.t[0:64, hb * 512:(hb + 1) * 512], identb.t[0:64, 0:64], nmrep.t[:, d, hb * 512:(hb + 1) * 512],
                           True, False, [identb, nmrep], [cq])
                    for h in range(16):
                        for part in range(2):
                            MM(cq.t[0:64, h * 64:(h + 1) * 64], dahl.t[:, part * 16 + h:part * 16 + h + 1].to_broadcast([64, 64]),
                               Lb(d), False, False, [dahl, constb], [cq])
                        for part in range(2):
                            MM(cq.t[0:64, h * 64:(h + 1) * 64], Lb(d),
                               ndahl.t[:, part * 16 + h:part * 16 + h + 1].to_broadcast([64, 64]),
                               False, (h % 8 == 7) and part == 1, [ndahl, constb], [cq])
                cps = pp[2]
                for h in range(4):
                    MM(cps.t[:, h * 64:(h + 1) * 64], lah.t[:, h * 128:(h + 1) * 128], Lb(d), True, False, [lah, constb], [cps])
                    MM(cps.t[:, h * 64:(h + 1) * 64], lal.t[:, h * 128:(h + 1) * 128], Lb(d), False, True, [lal, constb], [cps])
                for h in range(4):
                    co = slice(256 + h * 64, 256 + (h + 1) * 64)
                    MM(cps.t[:, co], lah.t[:, h * 128:(h + 1) * 128], Lb(d), True, False, [lah, constb], [cps])
                    MM(cps.t[:, co], lal.t[:, h * 128:(h + 1) * 128], Lb(d), False, False, [lal, constb], [cps])
                    MM(cps.t[:, co], lah.t[:, h * 128:(h + 1) * 128], nlm.t[:, d, :], False, False, [lah, nlm], [cps])
                    MM(cps.t[:, co], lal.t[:, h * 128:(h + 1) * 128], nlm.t[:, d, :], False, True, [lal, nlm], [cps])
                MM(cps.t[0:64, 512:1024], Ub(d), lah.t[:], True, False, [lah, constb], [cps])
                MM(cps.t[0:64, 512:1024], Ub(d), lal.t[:], False, True, [lal, constb], [cps])
                kps = pp[3]
                kpsb = kps.t[:].bitcast(BF16)
                for h in range(4):
                    TR(kpsb[0:64, h * 128:(h + 1) * 128], qkT.t[:, 4 + h, cl:cl + 64], identb.t[:], [qkT, identb], [kps])
                ACT(dtw.t[:], pss.t[0:64, 48:64], AF.Exp, [pss], [dtw])
                TT(dtw.t[:], dtw.t[:], dts.t[:, dsl], ALU.mult, [dtw, dts], [dtw])
                ACT(eL.t[:], pss.t[:, 64:80], AF.Exp, [pss], [eL])
                TT(h16(xd.t[:]), h16(xsb.t[:, 0:1024]), dts.t[:, dsl].unsqueeze(2).to_broadcast([64, 16, 64]), ALU.mult, [xsb, dts], [xd])
                TT(h16(xdw.t[:]), h16(xsb.t[:, 0:1024]), dtw.t[:].unsqueeze(2).to_broadcast([64, 16, 64]), ALU.mult, [xsb, dtw], [xdw], eng='pool')
                if not is_ctx:
                    ACT(seg.t[:], cq.t[0:64, :], AF.Exp, [cq], [seg])
                    ACT(ecum.t[:], pss.t[0:64, 32:48], AF.Exp, [pss], [ecum])
                    for g in range(4):
                        MM(psc.t[0:64, g * 64:(g + 1) * 64], xbcT.t[:, 8 + g, cl:cl + 64], xbcT.t[:, 12 + g, cl:cl + 64],
                           True, True, [xbcT], [psc])
                if not is_ctx:
                    TT(MTt.t[:].rearrange("p (g r q) -> p g r q", g=4, r=4),
                       seg.t[:].rearrange("p (g r q) -> p g r q", g=4, r=4),
                       h4(psc.t[0:64, 0:256]).unsqueeze(2).to_broadcast([64, 4, 4, 64]),
                       ALU.mult, [seg, psc], [MTt])
                cv = h4(cps.t[:, 0:256])
                mid = 31 if d == 0 else 32
                last = 63 if d == 0 else 0
                ACT(erc.t[:], cps.t[0:64, 512:1024], AF.Exp, [cps], [erc], scale=-1.0 / 16)
                ACT(ek.t[:], cps.t[:, 256:512], AF.Exp, [cps], [ek], scale=1.0 / 16)
                ACT(eT.t[:], cv[:, :, last], AF.Exp, [cps], [eT], scale=-1.0 / 16)
                TT(kdec.t[:], kpsb[0:64, 0:512], erc.t[:], ALU.mult, [kps, erc], [kdec])
                TT(h4(kdT.t[:]), qkT.t[:, 4:8, cl:cl + 64], h4(ek.t[:]), ALU.mult, [qkT, ek], [kdT], eng='pool')
                if not is_ctx:
                    ACT(eq.t[:], cps.t[:, 256:512], AF.Exp, [cps], [eq], scale=-1.0 / 16)
                    ACT(ec.t[:], cps.t[:, 0:256], AF.Exp, [cps], [ec], scale=-1.0 / 16)
                    TT(h4(qdT.t[:]), qkT.t[:, 0:4, cl:cl + 64], h4(eq.t[:]), ALU.mult, [qkT, eq], [qdT], eng='pool')
                    TT(h4(qeT.t[:]), qkT.t[:, 0:4, cl:cl + 64], h4(ec.t[:]), ALU.mult, [qkT, ec], [qeT], eng='pool')
                return hd

            def chunk_fin(b, d, t0, cl, is_ctx, hd):
                dsl = slice(d * 16, d * 16 + 16)
                h16 = lambda ap: ap.rearrange("p (h q) -> p h q", h=16)
                h4 = lambda ap: ap.rearrange("p (h q) -> p h q", h=4)
                ecum, eL, xsb, xd, xdw, vsb, kdec = hd["ecum"], hd["eL"], hd["xsb"], hd["xd"], hd["xdw"], hd["vsb"], hd["kdec"]
                ybuf, obuf, sps = hd["ybuf"], hd["obuf"], hd["sps"]
                MTt, cmc, eT, qdT, kdT, qeT = hd["MTt"], hd["cmc"], hd["eT"], hd["qdT"], hd["kdT"], hd["qeT"]
                TT(h16(Hs.t[:]), h16(Hs.t[:]), eL.t[:].unsqueeze(2).to_broadcast([128, 16, 64]), ALU.mult, [Hs, eL], [Hs], eng='pool')
                if not is_ctx:
                    for h in range(4):
                        hs = slice(h * 64, h * 64 + 64)
                        MM(sps.t[0:64, 256 + h * 64:256 + (h + 1) * 64], kdT.t[:, hs], qdT.t[:, hs], True, True, [kdT, qdT], [sps])
                    TT(h4(scTt.t[:]), h4(sps.t[0:64, 256:512]), Vm(d).unsqueeze(1).to_broadcast([64, 4, 64]), ALU.mult, [sps, consts], [scTt])
                    yi = pp[0]
                    for h in range(16):
                        hs = slice(h * 64, h * 64 + 64)
                        MM(yi.t[0:64, hs], MTt.t[:, hs], xd.t[:, hs], True, d == 1, [MTt, xd], [yi])
                        if d == 0:
                            MM(yi.t[0:64, hs], dkd.t[:, hs], xsb.t[:, hs], False, True, [dkd, xsb], [yi])
                    yh = pp[2]
                    for g in range(4):
                        gs = slice(g * 256, g * 256 + 256)
                        MM(yh.t[0:64, gs], cmc.t[:, g * 64:(g + 1) * 64], Hb.t[:, gs], True, True, [cmc, Hb], [yh])
                scp = pp[3]
                for h in range(16):
                    g = h // 4
                    hs = slice(h * 64, h * 64 + 64)
                    MM(scp.t[:, hs], xsb.t[:, 1024 + g * 128:1024 + (g + 1) * 128], xdw.t[:, hs], True, True, [xsb, xdw], [scp])
                if not is_ctx:
                    TT(h16(ybuf.t[:]), h16(yh.t[0:64, :]), ecum.t[:].unsqueeze(2).to_broadcast([64, 16, 64]), ALU.mult, [yh, ecum], [ybuf])
                    ops_ = pp[2]
                    for h in range(4):
                        hs = slice(h * 64, h * 64 + 64)
                        vs = slice(h * 256, h * 256 + 256)
                        MM(ops_.t[0:64, vs], scTt.t[:, hs], vsb.t[:, vs], True, False, [scTt, vsb], [ops_])
                        MM(ops_.t[0:64, vs], qeT.t[:, hs], Sb.t[:, vs], False, True, [qeT, Sb], [ops_])
                TT(Hs.t[:], Hs.t[:], scp.t[:], ALU.add, [Hs, scp], [Hs])
                CP(Hb.t[:], Hs.t[:], [Hs], [Hb], eng='act')
                sgp = pp[3]
                for h in range(4):
                    vs = slice(h * 256, h * 256 + 256)
                    MM(sgp.t[:, vs], kdec.t[:, h * 128:(h + 1) * 128], vsb.t[:, vs], True, True, [kdec, vsb], [sgp])
                if not is_ctx:
                    TT(ybuf.t[:], ybuf.t[:], yi.t[0:64, :], ALU.add, [ybuf, yi], [ybuf])
                    CP(obuf.t[:], ops_.t[0:64, :], [ops_], [obuf], eng='act')
                for h in range(4):
                    vs = slice(h * 256, h * 256 + 256)
                    STT(Ss.t[:, vs], Ss.t[:, vs], eT.t[:, h:h + 1], sgp.t[:, vs], ALU.mult, ALU.add, [Ss, eT, sgp], [Ss])
                CP(Sb.t[:], Ss.t[:], [Ss], [Sb], eng='act')
                if not is_ctx:
                    tx = t0 - CTXL
                    nm = "yo%d_%d" % (b, tx)
                    if d == 0:
                        p.dma('sp', yo_d[b, tx:tx + 64, 0:1024], ybuf.t[:], reads=[ybuf], dram_w=[nm + "y"])
                        p.dma('sp', yo_d[b, tx:tx + 64, 1024:2048], obuf.t[:], reads=[obuf], dram_w=[nm + "o"])
                    else:
                        p.dma('pool', yo_d[b, tx:tx + 64, 0:1024], ybuf.t[:], reads=[ybuf], dram_r=[nm + "y"], dram_w=[nm + "y"], accum=ALU.add)
                        p.dma('pool', yo_d[b, tx:tx + 64, 1024:2048], obuf.t[:], reads=[obuf], dram_r=[nm + "o"], dram_w=[nm + "o"], accum=ALU.add)

            nsc = dbg.get("nsc", 8)
            for b in range(nseq):
                for i in range(2 + 2 * nsc):
                    xt = scr8.t[:, 0:1024]
                    if i < 2:
                        p.dma('sp', xt, ctx_d[b, i * 128:(i + 1) * 128, :], writes=[scr8])
                        r = 2
                    else:
                        p.dma('sp', xt, x_d[b, (i - 2) * 128:(i - 1) * 128, :], writes=[scr8])
                        r = b
                    norm_to_T(xt, scr8, A1, B1fn, r, hT, i * 128, (sq.t[:], sq, ssq, scr8.t[:, 1024:2048], scr8))
                    if i >= 2:
                        p.dma('sp', hts_d[b, :, :, (i - 2) * 128:(i - 1) * 128], hT.t[:, :, i * 128:(i + 1) * 128],
                              reads=[hT], dram_w=["hts%d_%d" % (b, (i - 2) * 128)])
                if dbg.get("dump_hT") and b == 0:
                    dump("hT", hT, hT.t[:], [128, 8, NTOK])
                xhi = CTXL + 256 * nsc
                p.barrier()
                for d in range(2):
                    for st in (Hs, Ss):
                        MEMSET(st.t[:], 0.0, [st])
                    for st in (Hb, Sb):
                        MEMSET(st.t[:], 0.0, [st])
                    supers = [(0, 0, CTXL, True)] + [(CTXL + 256 * i, CTXL, xhi, False) for i in range(nsc)]
                    if d == 1:
                        supers = [supers[0]] + supers[1:][::-1]
                    chunks = []
                    for si, (tok0, lo, hi, is_ctx) in enumerate(supers):
                        cs = list(range(4) if d == 0 else range(3, -1, -1))
                        for ci, c in enumerate(cs):
                            chunks.append((tok0, lo, hi, is_ctx, c, ci == 0))

                    def prep(k):
                        tok0, lo, hi, is_ctx, c, first = chunks[k]
                        if first:
                            si = 0 if is_ctx else 1 + (tok0 - CTXL) // 256
                            nm = "sv%d_%d" % (b, si)
                            sxv = sx_d[b, si].rearrange("p (a t) -> p a t", a=16)
                            sqv = sqk_d[b, si].rearrange("p (a t) -> p a t", a=8)
                            if d == 0:
                                proj_super(tok0, lo, hi)
                                p.dma('sp', sxv, xbcT.t[:], reads=[xbcT], dram_w=[nm + "x"])
                                p.dma('sp', sqv, qkT.t[:], reads=[qkT], dram_w=[nm + "q"])
                                p.dma('sp', sg_d[b, si], glrT.t[:], reads=[glrT], dram_w=[nm + "g"])
                            else:
                                p.dma('sp', xbcT.t[:], sxv, writes=[xbcT], dram_r=[nm + "x"])
                                p.dma('sp', qkT.t[:], sqv, writes=[qkT], dram_r=[nm + "q"])
                                p.dma('sp', glrT.t[:], sg_d[b, si], writes=[glrT], dram_r=[nm + "g"])
                            if dbg.get("dump_xbc") and b == 0 and d == 0 and tok0 == dbg["dump_xbc"]:
                                dump("xbcT", xbcT, xbcT.t[:], [128, 16, 256])
                                dump("qkT", qkT, qkT.t[:], [128, 8, 256])
                        hd = chunk_head(b, d, tok0 + 64 * c, 64 * c, is_ctx)
                        return chunk_mid(b, d, tok0 + 64 * c, 64 * c, is_ctx, hd)

                    hd_cur = prep(0)
                    for k in range(len(chunks)):
                        hd_nxt = prep(k + 1) if k + 1 < len(chunks) else None
                        tok0, lo, hi, is_ctx, c, first = chunks[k]
                        chunk_fin(b, d, tok0 + 64 * c, 64 * c, is_ctx, hd_cur)
                        hd_cur = hd_nxt
                    if dbg.get("dump_state") and b == 0:
                        dump("H%d" % d, Hs, Hs.t[:], [128, 1024])
                        dump("S%d" % d, Ss, Ss.t[:], [128, 1024])
                p.barrier()
            p.es = es
          p.barrier()

        build_rest(nc, p, locals())
        p.finish()
    return nc, dbg_outs


def build_rest(nc, p, L):
    import types
    N = types.SimpleNamespace(**L)
    es, dbg, phases, nseq = N.es, N.dbg, N.phases, N.nseq
    MM, TR, ACT, TT, TS, STT, CP, MEMSET, PS, dump = N.MM, N.TR, N.ACT, N.TT, N.TS, N.STT, N.CP, N.MEMSET, N.PS, N.dump
    consts, vecs, modT, scT, A2, identf = N.consts, N.vecs, N.modT, N.scT, N.A2, N.identf
    x_d, yo_d, x1_d, out_d = N.x_d, N.yo_d, N.x1_d, N.out_d
    A1 = N.A1

    def load_w(dst, src_ap, nk):
        for dk in range(nk):
            p.dma('pool', dst.t[:, dk, :], src_ap[dk * 128:(dk + 1) * 128, :], writes=[dst])

    def compute_G(G, which, rb):
        with ExitStack() as esg:
            p.es = esg
            scR = p.sbuf("scR", [128, 8, 128], BF16)
            wb = p.sbuf("wgb", [128, 8, 1024], BF16)
            rb_t = p.sbuf("rbt", [128, 1024], F32)
            p.dma('sp', rb_t.t[:], N.rowsbig_d[0:1, rb:rb + 1024].partition_broadcast(128), writes=[rb_t])
            load_w(wb, N.w_ada_d[:, which * 1024:(which + 1) * 1024], 8)
            for b in range(NBL):
                CP(scR.t[:], scT.t[:, :, b:b + 1].to_broadcast([128, 8, 128]), [scT], [scR])
                gps = PS()
                for nb in range(2):
                    for dk in range(8):
                        MM(gps.t[:, nb * 512:(nb + 1) * 512], scR.t[:, dk, :], wb.t[:, dk, nb * 512:(nb + 1) * 512],
                           dk == 0, dk == 7, [wb, scR], [gps])
                TT(G.t[:, b, :], gps.t[:], rb_t.t[:], ALU.add, [gps, rb_t], [G])
            p.es = es
        p.barrier()

    def rstd_of(ssq_in, out, n, reads, writes):
        ACT(out, ssq_in, AF.Sqrt, reads, writes, bias=N.epsb.t[:, 0:1], scale=1.0 / n)
        p.op('dve', lambda e: e.reciprocal(out, out), writes, writes)

    if "P" in phases:
      with ExitStack() as esP:
        p.es = esP
        G1 = p.sbuf("G1", [128, NBL, 1024], F32)
        compute_G(G1, 2, RB_BA2)
        p.es = esP
        wzr = p.sbuf("wzr", [128, 8, 2048], BF16)
        wmg = p.sbuf("wmg", [128, 8, 2048], BF16)
        wbs = p.sbuf("wbs", [128, 8, 1024], BF16)
        wbg = p.sbuf("wbg", [128, 8, 1024], BF16)
        wo = p.sbuf("wo", [128, 8, 1024], BF16)
        for dk in range(8):
            p.dma('pool', wzr.t[:, dk, 0:1024], N.w_in_d[dk * 128:(dk + 1) * 128, 0:1024], writes=[wzr])
            p.dma('pool', wzr.t[:, dk, 1024:2048], N.w_in_d[dk * 128:(dk + 1) * 128, 5168:6192], writes=[wzr])
        load_w(wmg, N.w_merge_d, 8)
        load_w(wbs, N.w_brs_d, 8)
        load_w(wbg, N.w_brg_d, 8)
        load_w(wo, N.w_o_d, 8)
        for b in range(nseq):
            for q4 in range(4):
                p.dma('sp', x1_d[b, q4 * 512:(q4 + 1) * 512, :], x_d[b, q4 * 512:(q4 + 1) * 512, :],
                      dram_w=["x1_%d_%d" % (b, q4 * 512 + k * 128) for k in range(4)])
        TW = 256
        NS = TW // 128
        hT4s = [p.sbuf("hT4_%d" % i, [128, 8, TW], BF16) for i in range(2)]
        yT4s = [p.sbuf("yT4_%d" % i, [128, 8, TW], BF16) for i in range(2)]
        oT4s = [p.sbuf("oT4_%d" % i, [128, 8, TW], BF16) for i in range(2)]
        mT4s = [p.sbuf("mT4_%d" % i, [128, 8, TW], BF16) for i in range(2)]
        gT_r = Rot(p, "gT", [128, 2, TW], BF16, 2)
        m12_r = Rot(p, "m12", [128, 2 * TW], F32, 1)
        zrs = [p.sbuf("zr%d" % i, [128, 2048], BF16) for i in range(2)]
        yots = [p.sbuf("yot%d" % i, [128, 2048], F32) for i in range(2)]
        scr8 = p.sbuf("scr8p", [128, 2048], F32)
        tbuf = p.sbuf("tbuf", [128, 1024], F32)
        sq = p.sbuf("sqp", [128, 1024], BF16)
        sq2 = sq
        ssq = p.sbuf("ssqp", [128, 4], F32)
        so = p.sbuf("so", [128, 8], F32)

        def B1fn(j, r):
            return modT.t[:, j, r:r + 1]

        ntile = dbg.get("nt4", SEQ // TW)
        tiles = [(b, t) for b in range(nseq) for t in range(ntile)]

        mhalf = p.sbuf("mhalf", [128, 4], F32)
        MEMSET(mhalf.t[:], -0.5, [mhalf])
        sig_r = Rot(p, "sig", [128, 512], BF16, 2)

        def rstd_pow(ssq_ap, out_ap, n, res):
            ncol = ssq_ap.shape[-1]
            TS(out_ap, ssq_ap, 1.0 / n, EPS, ALU.mult, ALU.add, [res], [res])
            TT(out_ap, out_ap, mhalf.t[:, 0:ncol], ALU.pow, [res, mhalf], [res], eng='pool')

        def PA1a(ti, s):
            b, t = tiles[ti]
            yot = yots[s]
            tok = t * TW + s * 128
            p.dma('sp', yot.t[:], yo_d[b, tok:tok + 128, :], writes=[yot],
                  dram_r=["yo%d_%d%s" % (b, tok + o_, s_) for o_ in (0, 64) for s_ in ("y", "o")])

        def PA1b1(ti, s):
            b, t = tiles[ti]
            hT4 = hT4s[ti % 2]
            tok = t * TW + s * 128
            p.dma('sp', hT4.t[:, :, s * 128:(s + 1) * 128], N.hts_d[b, :, :, tok:tok + 128], writes=[hT4],
                  dram_r=["hts%d_%d" % (b, tok)])

        def PA1b2(ti, s):
            hT4 = hT4s[ti % 2]
            zr = zrs[s]
            for nb in range(4):
                ps = PS()
                for dk in range(8):
                    MM(ps.t[:, 0:512], hT4.t[:, dk, s * 128:(s + 1) * 128], wzr.t[:, dk, nb * 512:(nb + 1) * 512],
                       dk == 0, dk == 7, [hT4, wzr], [ps])
                sig = sig_r.next()
                ACT(sig.t[:], ps.t[:, 0:512], AF.Sigmoid, [ps], [sig])
                TT(zr.t[:, nb * 512:(nb + 1) * 512], ps.t[:, 0:512], sig.t[:], ALU.mult, [ps, sig], [zr])

        def PA2a(ti, s):
            zr, yot = zrs[s], yots[s]
            TT(yot.t[:, 0:1024], yot.t[:, 0:1024], zr.t[:, 0:1024], ALU.mult, [yot, zr], [yot])
            TT(sq2.t[:], yot.t[:, 0:1024], yot.t[:, 0:1024], ALU.mult, [yot], [sq2])
            p.op('dve', lambda e: e.reduce_sum(so.t[:, 0:1], sq2.t[:], AX.X), [sq2], [so])
            rstd_pow(so.t[:, 0:1], so.t[:, 0:1], 1024.0, so)
            TS(yot.t[:, 0:1024], yot.t[:, 0:1024], so.t[:, 0:1], None, ALU.mult, None, [yot, so], [yot])
            TT(sq2.t[:], yot.t[:, 1024:2048], yot.t[:, 1024:2048], ALU.mult, [yot], [sq2], eng='pool')
            p.op('dve', lambda e: e.reduce_sum(so.t[:, 4:8], sq2.t[:].rearrange("p (h v) -> p h v", h=4), AX.X), [sq2], [so])
            rstd_pow(so.t[:, 4:8], so.t[:, 4:8], 256.0, so)
            TT(yot.t[:, 1024:2048].rearrange("p (h v) -> p h v", h=4), yot.t[:, 1024:2048].rearrange("p (h v) -> p h v", h=4),
               so.t[:, 4:8].unsqueeze(2).to_broadcast([128, 4, 256]), ALU.mult, [yot, so], [yot])
            TT(yot.t[:, 1024:2048], yot.t[:, 1024:2048], zr.t[:, 1024:2048], ALU.mult, [yot, zr], [yot], eng='pool')

        def PA2b(ti, s):
            yT4, oT4 = yT4s[ti % 2], oT4s[ti % 2]
            yot = yots[s]
            for (half, dstT, vg) in ((0, yT4, V_SNG), (1, oT4, V_GNG)):
                ps = PS()
                for j in range(8):
                    TR(ps.t[:, j * 128:(j + 1) * 128], yot.t[:, half * 1024 + j * 128:half * 1024 + (j + 1) * 128],
                       identf, [yot, consts], [ps])
                if half == 0:
                    for j in range(8):
                        ACT(dstT.t[:, j, s * 128:(s + 1) * 128], ps.t[:, j * 128:(j + 1) * 128], AF.Identity,
                            [ps, vecs], [dstT], scale=vecs.t[:, vg + j:vg + j + 1])
                else:
                    TT(dstT.t[:, :, s * 128:(s + 1) * 128], ps.t[:].rearrange("p (j t) -> p j t", j=8),
                       vecs.t[:, vg:vg + 8].unsqueeze(2).to_broadcast([128, 8, 128]), ALU.mult, [ps, vecs], [dstT])

        def Bstep(ti, jc):
            hT4, yT4, oT4, mT4 = hT4s[ti % 2], yT4s[ti % 2], oT4s[ti % 2], mT4s[ti % 2]
            gT, m12 = gT_r.next(), m12_r.next()
            ps = PS()
            for gi in range(2):
                gc = gi * 8 + jc
                for dk in range(8):
                    MM(ps.t[:, gi * 512:gi * 512 + TW], wmg.t[:, dk, gc * 128:(gc + 1) * 128], hT4.t[:, dk, :], dk == 0, dk == 7, [wmg, hT4], [ps])
                ACT(gT.t[:, gi, :], ps.t[:, gi * 512:gi * 512 + TW], AF.Sigmoid, [ps, vecs], [gT], bias=vecs.t[:, V_BM + gc:V_BM + gc + 1])
            ps = PS()
            for dk in range(8):
                MM(ps.t[:, 0:TW], wbs.t[:, dk, jc * 128:(jc + 1) * 128], yT4.t[:, dk, :], dk == 0, dk == 7, [wbs, yT4], [ps])
            for dk in range(8):
                MM(ps.t[:, 512:512 + TW], wbg.t[:, dk, jc * 128:(jc + 1) * 128], oT4.t[:, dk, :], dk == 0, dk == 7, [wbg, oT4], [ps])
            TT(m12.t[:].rearrange("p (g t) -> p g t", g=2), ps.t[:].rearrange("p (g t) -> p g t", g=2)[:, :, 0:TW], gT.t[:], ALU.mult, [ps, gT], [m12])
            TT(mT4.t[:, jc, :], m12.t[:, 0:TW], m12.t[:, TW:2 * TW], ALU.add, [m12], [mT4], eng='pool')

        def Cstep(ti, s):
            b, t = tiles[ti]
            mT4 = mT4s[ti % 2]
            tok = t * TW + s * 128
            ps = PS()
            for nb in range(2):
                for dk in range(8):
                    MM(ps.t[:, nb * 512:(nb + 1) * 512], mT4.t[:, dk, s * 128:(s + 1) * 128], wo.t[:, dk, nb * 512:(nb + 1) * 512],
                       dk == 0, dk == 7, [mT4, wo], [ps])
            TT(tbuf.t[:], ps.t[:], G1.t[:, b, :], ALU.mult, [ps, G1], [tbuf])
            nm = "x1_%d_%d" % (b, tok)
            p.dma('pool', x1_d[b, tok:tok + 128, :], tbuf.t[:], reads=[tbuf], dram_r=[nm], dram_w=[nm], accum=ALU.add)

        A1m = N.A1
        for s in range(NS):
            PA1a(0, s)
            PA1b1(0, s)
            PA1b2(0, s)
        for s in range(NS):
            PA2a(0, s)
            PA2b(0, s)
        sched = {0: [(PA1a, 0)], 1: [(PA1b1, 0), (PA1a, 1)], 2: [(PA1b2, 0), (PA1b1, 1)], 3: [(PA2a, 0)],
                 4: [(PA1b2, 1)], 5: [(PA2b, 0), (PA2a, 1)], 7: [(PA2b, 1)]}
        for ti in range(len(tiles)):
            nxt = ti + 1 < len(tiles)
            for jc in range(8):
                Bstep(ti, jc)
                if nxt:
                    for (fn, s) in sched.get(jc, []):
                        fn(ti + 1, s)
            for s in range(NS):
                Cstep(ti, s)
        p.es = es
      p.barrier()

    if "F" in phases:
      with ExitStack() as esF:
        p.es = esF
        G2 = p.sbuf("G2", [128, NBL, 1024], F32)
        compute_G(G2, 5, RB_BA5)
        p.es = esF
        fng = p.sbuf("fng", [128, 1024], F32)
        p.dma('sp', fng.t[:], N.rowsbig_d[0:1, RB_FNG:RB_FNG + 1024].partition_broadcast(128), writes=[fng])
        wdn = p.sbuf("wdn", [128, 22, 1024], BF16)
        load_w(wdn, N.w_down_d, 22)
        wupr = Rot(p, "wup", [128, 8, 256], BF16, 3)
        h2T = p.sbuf("h2T", [128, 8, 1152], BF16)
        aT = p.sbuf("aT", [128, 22, 1024], BF16)
        scr8 = p.sbuf("scr8f", [128, 2048], F32)
        sq = p.sbuf("sqf", [128, 1024], BF16)
        ssq = p.sbuf("ssqf", [128, 4], F32)
        usb_r = Rot(p, "usb", [128, 17 * 66], BF16, 2)
        for _u in usb_r.b:
            MEMSET(_u.t[:], 0.0, [_u])
        dg_r = Rot(p, "dg", [128, 9, 128], BF16, 2)
        sg_r = Rot(p, "sg", [128, 1024], F32, 2)
        identb = N.identb

        def B2fn(j, r):
            return modT.t[:, 24 + j, r:r + 1]

        nj = dbg.get("nj", 22)
        wsrc = N.w_up_d.rearrange("(dk p) n -> p dk n", p=128)
        for b in range(nseq):
            for hf in range(dbg.get("nhalf", 2)):
                base = 0 if hf == 0 else 896
                for i in range(9):
                    tok = base + i * 128
                    xt = scr8.t[:, 0:1024]
                    p.dma('sp', xt, x1_d[b, tok:tok + 128, :], writes=[scr8], dram_r=["x1_%d_%d" % (b, tok)])
                    N.norm_to_T(xt, scr8, A2, B2fn, b, h2T, i * 128, (sq.t[:], sq, ssq, scr8.t[:, 1024:2048], scr8))
                if dbg.get("dump_h2T") and b == 0 and hf == 0:
                    dump("h2T", h2T, h2T.t[:], [128, 8, 1152])
                m0 = 0 if hf == 0 else 128
                h0 = 1024 if hf == 0 else 64
                off = 0 if hf == 0 else 1
                hrow = 16 if hf == 0 else 0
                for j in range(nj):
                    wu = wupr.next()
                    p.dma('pool', wu.t[:, :, 0:128], wsrc[:, :, j * 128:(j + 1) * 128], writes=[wu])
                    p.dma('pool', wu.t[:, :, 128:256], wsrc[:, :, DFF + j * 128:DFF + (j + 1) * 128], writes=[wu])
                    pcs = []
                    for part in range(2):
                        ch = part * 22 + j
                        pm = PS()
                        pc = PS()
                        for nb in range(2):
                            for dk in range(8):
                                MM(pm.t[:, nb * 512:(nb + 1) * 512], wu.t[:, dk, part * 128:(part + 1) * 128],
                                   h2T.t[:, dk, m0 + nb * 512:m0 + (nb + 1) * 512], dk == 0, dk == 7, [wu, h2T], [pm])
                        for dk in range(8):
                            MM(pc.t[:, 0:64], wu.t[:, dk, part * 128:(part + 1) * 128], h2T.t[:, dk, h0:h0 + 64],
                               dk == 0, dk == 7, [wu, h2T], [pc])
                        usb = usb_r.next()
                        u3 = usb.t[:].rearrange("p (r c) -> p r c", c=66)
                        CP(u3[:, off:off + 16, 1:65], pm.t[:].rearrange("p (r c) -> p r c", c=64), [pm], [usb], eng='act')
                        CP(u3[:, hrow, 1:65], pc.t[:, 0:64], [pc], [usb], eng='dve')
                        dg = dg_r.next()
                        wb_ = V_FCW + ch * 9
                        TT(dg.t[:], identb.t[:].unsqueeze(1).to_broadcast([128, 9, 128]),
                           vecs.t[:, wb_:wb_ + 9].unsqueeze(2).to_broadcast([128, 9, 128]), ALU.mult, [identb, vecs], [dg], eng='pool')
                        taps = [(0, 0)] + [(dr, dc) for dr in (-1, 0, 1) for dc in (-1, 0, 1) if (dr, dc) != (0, 0)]
                        for bank in range(2):
                            todo = []
                            for (dr, dc) in taps:
                                mlo = 1 if (hf == 0 and dr == -1) else 0
                                mhi = 15 if (hf == 1 and dr == 1) else 16
                                lo = max(mlo, bank * 8)
                                hi = min(mhi, bank * 8 + 8)
                                if hi <= lo:
                                    continue
                                todo.append((dr, dc, lo, hi))
                            for ti, (dr, dc, lo, hi) in enumerate(todo):
                                k = (dr + 1) * 3 + (dc + 1)
                                MM(pc.t[:, lo * 64:hi * 64], dg.t[:, k, :], u3[:, lo + off + dr:hi + off + dr, 1 + dc:65 + dc],
                                   ti == 0, ti == len(todo) - 1, [dg, usb], [pc])
                        pcs.append((pc, ch))
                    sg = sg_r.next()
                    (pcg, chg), (pcv, chv) = pcs
                    ACT(sg.t[:], pcg.t[:], AF.Silu, [pcg, vecs], [sg], bias=vecs.t[:, V_FCB + chg:V_FCB + chg + 1])
                    STT(aT.t[:, j, :], pcv.t[:], vecs.t[:, V_FCB + chv:V_FCB + chv + 1], sg.t[:], ALU.add, ALU.mult, [pcv, vecs, sg], [aT])
                for s in range(8):
                    tok = hf * 1024 + s * 128
                    ps = PS()
                    for nb in range(2):
                        for j in range(nj):
                            MM(ps.t[:, nb * 512:(nb + 1) * 512], aT.t[:, j, s * 128:(s + 1) * 128], wdn.t[:, j, nb * 512:(nb + 1) * 512],
                               j == 0, j == nj - 1, [aT, wdn], [ps])
                    xt = scr8.t[:, 0:1024]
                    x2 = scr8.t[:, 1024:2048]
                    p.dma('sp', xt, x1_d[b, tok:tok + 128, :], writes=[scr8], dram_r=["x1_%d_%d" % (b, tok)])
                    TT(x2, ps.t[:], G2.t[:, b, :], ALU.mult, [ps, G2], [scr8])
                    TT(x2, x2, xt, ALU.add, [scr8], [scr8])
                    ACT(sq.t[:], x2, AF.Square, [scr8], [sq])
                    p.op('dve', lambda e: e.reduce_sum(ssq.t[:, 0:1], sq.t[:], AX.X), [sq], [ssq])
                    rstd_of(ssq.t[:, 0:1], ssq.t[:, 1:2], 1024.0, [ssq], [ssq])
                    TS(x2, x2, ssq.t[:, 1:2], None, ALU.mult, None, [scr8, ssq], [scr8])
                    TT(xt, x2, fng.t[:], ALU.mult, [scr8, fng], [scr8])
                    p.dma('sp', out_d[b, tok:tok + 128, :], xt, reads=[scr8], final=True)
        p.es = es
      p.barrier()


_CACHE = {}


def kernel(**inputs):
    inp = {k: np.asarray(v) for k, v in inputs.items()}
    if "nc" not in _CACHE:
        _CACHE["nc"] = build_program()[0]
    nc = _CACHE["nc"]
    sh = prep_shared(inp)
    shared = dict(w_ada=inp["w_ada"][0], w_in=inp["w_in"][0], w_merge=inp["w_merge"][0], w_br_ssd=inp["w_br_ssd"][0],
                  w_br_gla=inp["w_br_gla"][0], w_o=inp["w_o"][0], w_up=inp["w_up"][0], w_down=inp["w_down"][0])
    shared.update(sh)
    shared = {k: np.ascontiguousarray(np.asarray(v, np.float32)) for k, v in shared.items()}
    in_maps = []
    for core in range(8):
        b0 = core * NBL
        cT = np.stack([inp["c"][b0], inp["c"][b0 + 1], inp["c_ctx"]], axis=1).astype(np.float32)
        cT = np.ascontiguousarray(cT.reshape(8, 128, 3).transpose(1, 0, 2))
        m = dict(shared)
        m["x"] = np.ascontiguousarray(inp["x"][b0:b0 + NBL], dtype=np.float32)
        m["ctx"] = np.ascontiguousarray(inp["ctx"][b0:b0 + NBL], dtype=np.float32)
        m["cT"] = cT
        in_maps.append(m)
    res = run_bass_kernel_spmd(nc, in_maps, core_ids=list(range(8)))
    out = np.concatenate([np.asarray(r["out"]) for r in res.results], axis=0)
    return out.astype(np.float32)
```
